# Optimizing a Trainium2 kernel written in Bass

```python
import jax, jax.numpy as jnp
from jax import lax
import numpy as np

D_MODEL = 1024
BATCH = 8
SEQ = 2048
DEPTH = 2

SB_HEADS = 8
SB_HEAD_DIM = 64
SB_WIDTH = SB_HEADS * SB_HEAD_DIM
SB_BLOCK = 128
SSD_WIDTH = D_MODEL
SSD_HEAD_DIM = 64
SSD_HEADS = SSD_WIDTH // SSD_HEAD_DIM
SSD_GROUPS = 2
SSD_STATE = 64
SSD_CONV = 4
SSD_CHUNK = 128
SSD_CONV_CH = SSD_WIDTH + 2 * SSD_GROUPS * SSD_STATE
RW_HEAD_DIM = 64
RW_WIDTH = D_MODEL // 2
RW_HEADS = RW_WIDTH // RW_HEAD_DIM
RW_DECAY_RANK = 64
RW_ICLR_RANK = 64
N_BRANCH = 3
SB_COLS = 4 * SB_WIDTH
SSD_COLS = SSD_WIDTH + SSD_CONV_CH + SSD_HEADS
RW_COLS = 4 * RW_WIDTH + RW_DECAY_RANK + RW_ICLR_RANK
GATE_COLS = N_BRANCH * D_MODEL
N_IN = SB_COLS + SSD_COLS + RW_COLS + GATE_COLS
RMS_EPS = 1e-6
GN_EPS = 64e-5

kernel_name = "hybrid_sba_ssd_rwkv7_gated"


def _split(x, sizes):
    idx = np.cumsum(sizes)[:-1].tolist()
    return jnp.split(x, idx, axis=-1)


def rms_norm(x, g):
    xf = x.astype(jnp.float32)
    y = xf * lax.rsqrt(jnp.mean(xf * xf, axis=-1, keepdims=True) + RMS_EPS)
    return (y * g.astype(jnp.float32)).astype(x.dtype)


def stick_breaking_attention(q, k, v):
    b, s, h, dh = q.shape
    scale = dh ** -0.5
    outs = []
    for i in range(s // SB_BLOCK):
        q0 = i * SB_BLOCK
        end = q0 + SB_BLOCK
        qb, kb, vb = q[:, q0:end], k[:, :end], v[:, :end]
        z = jnp.einsum('bqhd,bkhd->bhqk', qb, kb).astype(jnp.float32) * scale
        t_idx = q0 + jnp.arange(SB_BLOCK)
        s_idx = jnp.arange(end)
        mask = s_idx[None, :] < t_idx[:, None]
        log_beta = jax.nn.log_sigmoid(z)
        log_keep = jnp.where(mask, jax.nn.log_sigmoid(-z), 0.0)
        after = lax.cumsum(log_keep, axis=3, reverse=True) - log_keep
        att = jnp.where(mask, jnp.exp(log_beta + after), 0.0)
        outs.append(jnp.einsum('bhqk,bkhd->bqhd', att.astype(v.dtype), vb))
    return jnp.concatenate(outs, axis=1).reshape(b, s, h * dh)


def causal_depthwise_conv(x, w, bias):
    kw, ch = w.shape
    y = lax.conv_general_dilated(x, w[:, None, :], window_strides=(1,), padding=[(kw - 1, 0)],
                                 dimension_numbers=('NWC', 'WIO', 'NWC'), feature_group_count=ch)
    return y + bias


def segsum(a):
    t = a.shape[-1]
    rep = jnp.broadcast_to(a[..., None], a.shape + (t,))
    strict = jnp.tril(jnp.ones((t, t), dtype=bool), -1)
    cs = jnp.cumsum(jnp.where(strict, rep, 0), axis=-2)
    return jnp.where(jnp.tril(jnp.ones((t, t), dtype=bool)), cs, -jnp.inf)


def ssd_mixer(xbc_raw, dt_raw, conv_w, conv_b, dt_bias, a_log, d_skip):
    b, s, _ = xbc_raw.shape
    c, l, g = s // SSD_CHUNK, SSD_CHUNK, SSD_GROUPS
    j, p, n = SSD_HEADS // SSD_GROUPS, SSD_HEAD_DIM, SSD_STATE
    xbc = jax.nn.silu(causal_depthwise_conv(xbc_raw, conv_w, conv_b))
    xs, bm, cm = _split(xbc, [SSD_WIDTH, g * n, g * n])
    xs = xs.reshape(b, c, l, g, j, p)
    bm = bm.reshape(b, c, l, g, n)
    cm = cm.reshape(b, c, l, g, n)
    dt = jax.nn.softplus(dt_raw + dt_bias).reshape(b, c, l, g, j)
    a_head = -jnp.exp(a_log).reshape(g, j)
    da = jnp.moveaxis(dt * a_head, 2, -1)
    x_dt = xs * dt[..., None]
    a_cs = jnp.cumsum(da, axis=-1)
    decay_in = jnp.exp(segsum(da))
    cb = jnp.einsum('bclgn,bcsgn->bcgls', cm, bm)
    y_diag = jnp.einsum('bcgls,bcgjls,bcsgjp->bclgjp', cb, decay_in, x_dt)
    decay_states = jnp.exp(a_cs[..., -1:] - a_cs)
    states = jnp.einsum('bclgn,bcgjl,bclgjp->bcgjpn', bm, decay_states, x_dt)
    last = jnp.pad(jnp.moveaxis(a_cs[..., -1], 1, -1), [(0, 0), (0, 0), (0, 0), (1, 0)])
    decay_chunk = jnp.exp(segsum(last))
    states_p = jnp.concatenate([jnp.zeros_like(states[:, :1]), states], axis=1)
    new_states = jnp.einsum('bgjzc,bcgjpn->bzgjpn', decay_chunk, states_p)
    prev_states = new_states[:, :-1]
    y_off = jnp.einsum('bclgn,bcgjpn,bcgjl->bclgjp', cm, prev_states, jnp.exp(a_cs))
    y = y_diag + y_off + xs * d_skip.reshape(g, j)[:, :, None]
    return y.reshape(b, s, SSD_WIDTH)


def _rwkv7_step(state, inp):
    r_t, w_t, k_t, v_t, kk_t, a_t = inp
    sa = jnp.einsum('bhij,bhj->bhi', state, -kk_t)
    state = (state * w_t[:, :, None, :] + sa[..., None] * (kk_t * a_t)[:, :, None, :]
             + v_t[..., None] * k_t[:, :, None, :])
    y_t = jnp.einsum('bhij,bhj->bhi', state, r_t)
    return state, y_t


def rwkv7_mixer(slab, mu, w0, w_up, a0, a_up, k_k, k_a, r_k, ln_g, ln_b):
    b, s, _ = slab.shape
    hh, nn = RW_HEADS, RW_HEAD_DIM
    prev = jnp.pad(slab[:, :-1], [(0, 0), (1, 0), (0, 0)])
    mixed = slab + (prev - slab) * mu
    r, k, v, gate, w_lo, a_lo = _split(mixed, [RW_WIDTH] * 4 + [RW_DECAY_RANK, RW_ICLR_RANK])
    w = -jax.nn.softplus(-(w0 + jnp.tanh(w_lo) @ w_up)) - 0.5
    decay = jnp.exp(-jnp.exp(w.astype(jnp.float32)))
    a = jax.nn.sigmoid(a0 + a_lo @ a_up)
    kk = (k * k_k).reshape(b, s, hh, nn).astype(jnp.float32)
    kk = kk / jnp.maximum(jnp.sqrt(jnp.sum(kk * kk, axis=-1, keepdims=True)), 1e-12)
    k = k * (1 + (a - 1) * k_a)
    heads = lambda t: t.reshape(b, s, hh, nn).astype(jnp.float32)
    r4, k4, v4, a4, w4 = heads(r), heads(k), heads(v), heads(a), heads(decay)
    seq_first = lambda t: jnp.moveaxis(t, 1, 0)
    state0 = jnp.zeros((b, hh, nn, nn), jnp.float32)
    _, ys = lax.scan(_rwkv7_step, state0,
                     (seq_first(r4), seq_first(w4), seq_first(k4), seq_first(v4), seq_first(kk), seq_first(a4)))
    y = jnp.moveaxis(ys, 0, 1)
    mean = jnp.mean(y, axis=-1, keepdims=True)
    var = jnp.mean(jnp.square(y - mean), axis=-1, keepdims=True)
    y = ((y - mean) * lax.rsqrt(var + GN_EPS)).reshape(b, s, RW_WIDTH) * ln_g + ln_b
    bonus = jnp.sum(r4 * k4 * r_k, axis=-1, keepdims=True) * v4
    y = y + bonus.reshape(b, s, RW_WIDTH)
    return y.astype(slab.dtype), gate


def hybrid_layer(x, norm_g, w_in, conv_w, conv_b, dt_bias, a_log, d_skip, ssd_norm_g,
                 rw_mu, rw_w0, rw_w_up, rw_a0, rw_a_up, rw_k_k, rw_k_a, rw_r_k, rw_ln_g, rw_ln_b,
                 w_out_sb, w_out_ssd, w_out_rw, w_o):
    b, s, _ = x.shape
    h = rms_norm(x, norm_g)
    proj = h @ w_in
    sb_cols, ssd_cols, rw_cols, gate_cols = _split(proj, [SB_COLS, SSD_COLS, RW_COLS, GATE_COLS])
    q, k, v, sb_gate = _split(sb_cols, [SB_WIDTH] * 4)
    shp = (b, s, SB_HEADS, SB_HEAD_DIM)
    y_sb = stick_breaking_attention(q.reshape(shp), k.reshape(shp), v.reshape(shp)) * jax.nn.silu(sb_gate)
    z, xbc, dt_raw = _split(ssd_cols, [SSD_WIDTH, SSD_CONV_CH, SSD_HEADS])
    y_ssd = ssd_mixer(xbc, dt_raw, conv_w, conv_b, dt_bias, a_log, d_skip)
    y_ssd = rms_norm(y_ssd * jax.nn.silu(z), ssd_norm_g)
    y_rw, rw_gate = rwkv7_mixer(rw_cols, rw_mu, rw_w0, rw_w_up, rw_a0, rw_a_up, rw_k_k, rw_k_a,
                                rw_r_k, rw_ln_g, rw_ln_b)
    y_rw = y_rw * jax.nn.silu(rw_gate)
    g_sb, g_ssd, g_rw = _split(jax.nn.sigmoid(gate_cols), [D_MODEL] * N_BRANCH)
    merged = g_sb * (y_sb @ w_out_sb) + g_ssd * (y_ssd @ w_out_ssd) + g_rw * (y_rw @ w_out_rw)
    return x + merged @ w_o


def setup_inputs(seed: int = 0) -> dict:
    key = jax.random.key(seed)
    ks = jax.random.split(key, 24)
    f32 = jnp.float32
    nrm = lambda k, shp, sc: jax.random.normal(k, shp, f32) * sc
    dt0 = jnp.exp(jax.random.uniform(ks[5], (DEPTH, SSD_HEADS), f32, np.log(1e-3), np.log(1e-1)))
    return {
        "x": nrm(ks[0], (BATCH, SEQ, D_MODEL), 1.0),
        "norm_g": 1.0 + nrm(ks[1], (DEPTH, D_MODEL), 0.02),
        "w_in": nrm(ks[2], (DEPTH, D_MODEL, N_IN), D_MODEL ** -0.5),
        "conv_w": nrm(ks[3], (DEPTH, SSD_CONV, SSD_CONV_CH), SSD_CONV ** -0.5),
        "conv_b": nrm(ks[4], (DEPTH, SSD_CONV_CH), 0.02),
        "dt_bias": dt0 + jnp.log(-jnp.expm1(-dt0)),
        "a_log": jnp.log(jax.random.uniform(ks[6], (DEPTH, SSD_HEADS), f32, 1.0, 16.0)),
        "d_skip": 1.0 + nrm(ks[7], (DEPTH, SSD_HEADS), 0.02),
        "ssd_norm_g": 1.0 + nrm(ks[8], (DEPTH, SSD_WIDTH), 0.02),
        "rw_mu": jax.random.uniform(ks[9], (DEPTH, RW_COLS), f32, 0.0, 1.0),
        "rw_w0": jax.random.uniform(ks[10], (DEPTH, RW_WIDTH), f32, -6.0, -1.0),
        "rw_w_up": nrm(ks[11], (DEPTH, RW_DECAY_RANK, RW_WIDTH), 0.1),
        "rw_a0": nrm(ks[12], (DEPTH, RW_WIDTH), 0.1),
        "rw_a_up": nrm(ks[13], (DEPTH, RW_ICLR_RANK, RW_WIDTH), 0.1),
        "rw_k_k": 0.85 + nrm(ks[14], (DEPTH, RW_WIDTH), 0.02),
        "rw_k_a": 1.0 + nrm(ks[15], (DEPTH, RW_WIDTH), 0.02),
        "rw_r_k": nrm(ks[16], (DEPTH, RW_HEADS, RW_HEAD_DIM), 0.1),
        "rw_ln_g": 1.0 + nrm(ks[17], (DEPTH, RW_WIDTH), 0.02),
        "rw_ln_b": nrm(ks[18], (DEPTH, RW_WIDTH), 0.02),
        "w_out_sb": nrm(ks[19], (DEPTH, SB_WIDTH, D_MODEL), SB_WIDTH ** -0.5),
        "w_out_ssd": nrm(ks[20], (DEPTH, SSD_WIDTH, D_MODEL), SSD_WIDTH ** -0.5),
        "w_out_rw": nrm(ks[21], (DEPTH, RW_WIDTH, D_MODEL), RW_WIDTH ** -0.5),
        "w_o": nrm(ks[22], (DEPTH, D_MODEL, D_MODEL), D_MODEL ** -0.5),
        "final_g": 1.0 + nrm(ks[23], (D_MODEL,), 0.02),
    }


def reference(x, norm_g, w_in, conv_w, conv_b, dt_bias, a_log, d_skip, ssd_norm_g,
              rw_mu, rw_w0, rw_w_up, rw_a0, rw_a_up, rw_k_k, rw_k_a, rw_r_k, rw_ln_g, rw_ln_b,
              w_out_sb, w_out_ssd, w_out_rw, w_o, final_g):
    for i in range(DEPTH):
        x = hybrid_layer(x, norm_g[i], w_in[i], conv_w[i], conv_b[i], dt_bias[i], a_log[i], d_skip[i],
                         ssd_norm_g[i], rw_mu[i], rw_w0[i], rw_w_up[i], rw_a0[i], rw_a_up[i], rw_k_k[i],
                         rw_k_a[i], rw_r_k[i], rw_ln_g[i], rw_ln_b[i], w_out_sb[i], w_out_ssd[i],
                         w_out_rw[i], w_o[i])
    return rms_norm(x, final_g)
```

```python
from contextlib import ExitStack
import numpy as np
import concourse.bass as bass
import concourse.mybir as mybir
from concourse.bass_utils import run_bass_kernel_spmd

F32 = mybir.dt.float32
BF16 = mybir.dt.bfloat16
AF = mybir.ActivationFunctionType
ALU = mybir.AluOpType
AX = mybir.AxisListType

S = 2048
D = 1024
L = 2
NT = 16
N_IN = 9616
C_SBQ, C_SBK, C_SBV, C_SBG = 0, 512, 1024, 1536
C_Z, C_XBC, C_DT = 2048, 3072, 4352
C_RW = 4368
C_GATE = 6544
RMS_EPS = 1e-6
GN_EPS = 64e-5
CP_CW, CP_CB, CP_MU, CP_W0, CP_A0, CP_KK, CP_KA, CP_RK, CP_LG, CP_LB = 0, 40, 50, 67, 71, 75, 79, 83, 87, 91
NCP = 95
RP_NG, RP_SG, RP_DTB, RP_ALOG, RP_DSK = 0, 1024, 2048, 2064, 2080
NRP = 2096

SAME_ENGINE_SYNC = True
SAME_ENGINE_WAW = True
NDS = 24


class Buf:
    __slots__ = ("name", "w", "r", "excl")

    def __init__(self, name="", excl=False):
        self.name = name
        self.excl = excl
        self.w = None
        self.r = {}


class Eng:
    def __init__(self, name, e, sem, key):
        self.name, self.e, self.sem, self.key = name, e, sem, key
        self.cnt = 0
        self.clock = {}


class KB:
    def __init__(self, nc, es):
        self.nc = nc
        self.engs = {}
        self.sems = []
        for name, e in (("pe", nc.tensor), ("act", nc.scalar), ("dve", nc.vector),
                        ("pool", nc.gpsimd), ("sp", nc.sync)):
            sem = es.enter_context(nc.semaphore("s_" + name))
            self.engs[name] = Eng(name, e, sem, len(self.sems))
            self.sems.append(sem)
        self.dbase = len(self.sems)
        for i in range(NDS):
            self.sems.append(es.enter_context(nc.semaphore("d%d" % i)))
        self.dcnt = [0] * NDS
        self.dnext = 0
        self.ninst = 0

    def _wait_deps(self, E, reads, writes):
        need = {}

        def add(tok):
            if tok is None:
                return
            k = tok[0]
            if k == E.key and (E.name == "pe" or not SAME_ENGINE_SYNC):
                return
            cur = need.get(k)
            if cur is None or cur[1] < tok[1]:
                need[k] = tok

        for b in reads:
            add(b.w)
        for b in writes:
            if b.w is not None and (b.w[0] != E.key or SAME_ENGINE_WAW):
                add(b.w)
            for t in b.r.values():
                if t[0] != E.key or SAME_ENGINE_WAW:
                    add(t)
        new = None
        for k, tok in need.items():
            if E.clock.get(k, 0) < tok[1]:
                E.e.wait_ge(self.sems[k], tok[1])
                if new is None:
                    new = dict(E.clock)
                for kk, vv in tok[2].items():
                    if new.get(kk, 0) < vv:
                        new[kk] = vv
                if new.get(k, 0) < tok[1]:
                    new[k] = tok[1]
        if new is not None:
            E.clock = new

    def _mark(self, tok, reads, writes):
        for b in reads:
            b.r[tok[0]] = tok
        for b in writes:
            b.w = tok
            b.r = {}

    def op(self, en, fn, reads=(), writes=()):
        E = self.engs[en]
        if any(b.excl for b in reads):
            writes = list(writes) + [b for b in reads if b.excl and b not in writes]
            reads = [b for b in reads if not b.excl]
        self._wait_deps(E, reads, writes)
        inst = fn(E.e)
        E.cnt += 1
        inst.then_inc(E.sem, 1)
        self.ninst += 1
        tok = (E.key, E.cnt, E.clock)
        self._mark(tok, reads, writes)
        return tok

    def dma(self, q, out, in_, reads=(), writes=(), **kw):
        E = self.engs[q]
        self._wait_deps(E, reads, writes)
        j = self.dnext
        self.dnext = (j + 1) % NDS
        k = self.dbase + j
        prev = 16 * self.dcnt[j]
        if prev and E.clock.get(k, 0) < prev:
            E.e.wait_ge(self.sems[k], prev)
            new = dict(E.clock)
            new[k] = prev
            E.clock = new
        inst = E.e.dma_start(out=out, in_=in_, **kw)
        self.dcnt[j] += 1
        inst.then_inc(self.sems[k], 16)
        self.ninst += 1
        tok = (k, 16 * self.dcnt[j], E.clock)
        self._mark(tok, reads, writes)
        return tok

    def barrier(self, engines=("pe", "act", "dve", "pool", "sp")):
        for en in engines:
            E = self.engs[en]
            new = dict(E.clock)
            for F in self.engs.values():
                if F is E or F.cnt == 0:
                    continue
                if new.get(F.key, 0) < F.cnt:
                    E.e.wait_ge(F.sem, F.cnt)
                    new[F.key] = F.cnt
            for j in range(NDS):
                v = 16 * self.dcnt[j]
                k = self.dbase + j
                if v and new.get(k, 0) < v:
                    E.e.wait_ge(self.sems[k], v)
                    new[k] = v
            E.clock = new


def build(nlayers=L, branches=("sb", "ssd", "rw"), debug=False, dbg=None):
    dbg = dbg or {}
    nc = bass.Bass("TRN2", target_bir_lowering=False)
    es = ExitStack()
    kb = KB(nc, es)
    op, dma = kb.op, kb.dma

    def dram_in(name, shape):
        return nc.dram_tensor(name, shape, F32, kind="ExternalInput").ap()

    x_d = dram_in("x", [S, D])
    w_in_d = dram_in("w_in", [L, D, N_IN])
    w_out_sb_d = dram_in("w_out_sb", [L, 512, D])
    w_out_ssd_d = dram_in("w_out_ssd", [L, 1024, D])
    w_out_rw_d = dram_in("w_out_rw", [L, 512, D])
    w_o_d = dram_in("w_o", [L, D, D])
    w_up_d = dram_in("rw_w_up", [L, 64, 512])
    a_up_d = dram_in("rw_a_up", [L, 64, 512])
    cp_d = dram_in("cp", [L, 128, NCP])
    rp_d = dram_in("rp", [L, 128, NRP])
    fg_d = dram_in("fg", [128, D])
    wdt_d = dram_in("w_dt", [L, 128, 8, 16])
    y_d = nc.dram_tensor("y", [S, D], F32, kind="ExternalOutput").ap()
    tkind = "ExternalOutput" if debug else "Internal"
    T_d = {b: nc.dram_tensor("T_" + b, [S, D], F32, kind=tkind).ap() for b in ("sb", "ssd", "rw")}
    T_buf = {b: [Buf() for _ in range(NT)] for b in ("sb", "ssd", "rw")}

    def sb(name, shape, dt=F32):
        return es.enter_context(nc.sbuf_tensor(name, shape, dt))

    x_sb = sb("x_sb", [128, NT, D])
    x_b = [Buf("x%d" % i) for i in range(NT)]
    ident = sb("ident", [128, 128], BF16)
    ident_b = Buf("ident")
    ones_bf = sb("ones_bf", [128, 128], BF16)
    ones_b = Buf("ones")
    mask_lt = sb("mask_lt", [128, 2, 128], F32)
    mask_lt_b = Buf()
    negtri = sb("negtri", [128, 128], BF16)
    negtri_b = Buf()
    cp_sb = sb("cp_sb", [128, NCP])
    cp_b = Buf()
    rp_sb = sb("rp_sb", [128, NRP])
    rp_b = Buf()
    ps = [es.enter_context(nc.psum_tensor("ps%d" % i, [128, 512], F32)) for i in range(8)]
    ps_b = [Buf("ps%d" % i, excl=True) for i in range(8)]

    def mk_mask(t_ap, buf, val_true, val_false, base, cm, pat_step, n, cmp):
        op("pool", lambda e: e.memset(t_ap, val_true), writes=[buf])
        op("pool", lambda e: e.affine_select(out=t_ap, in_=t_ap, pattern=[[pat_step, n]], compare_op=cmp,
                                             fill=val_false, base=base, channel_multiplier=cm),
           reads=[buf], writes=[buf])

    mk_mask(ident[:], ident_b, 1.0, 0.0, 0, 1, -1, 128, ALU.is_equal)
    op("pool", lambda e: e.memset(ones_bf[:], 1.0), writes=[ones_b])
    for h in range(2):
        mk_mask(mask_lt[:, h, :], mask_lt_b, 1.0, 0.0, 0, -1, 1, 128, ALU.is_gt)
    mk_mask(negtri[:], negtri_b, -1.0, 0.0, 0, 1, -1, 128, ALU.is_ge)

    triLE = sb("triLE", [128, 128], F32)
    triLE_b = Buf()
    mk_mask(triLE[:], triLE_b, 1.0, 0.0, 0, -1, 1, 128, ALU.is_ge)
    maskGT = sb("maskGT", [128, 128], F32)
    maskGT_b = Buf()
    mk_mask(maskGT[:], maskGT_b, 1.0, 0.0, 0, 1, -1, 128, ALU.is_gt)
    ones_f = sb("ones_f", [128, 128], F32)
    ones_f_b = Buf()
    op("pool", lambda e: e.memset(ones_f[:], 1.0), writes=[ones_f_b])
    xs_d = nc.dram_tensor("xs_scr", [S, D], BF16, kind="Internal").ap()
    sz_d = nc.dram_tensor("sz_scr", [S, D], BF16, kind="Internal").ap()
    yn_d = nc.dram_tensor("yn_scr", [S, D], BF16, kind="Internal").ap()
    yrw_d = nc.dram_tensor("yrw_scr", [512, S], BF16, kind="Internal").ap()
    yrw_db = [Buf() for _ in range(8)]
    xs_db = [Buf() for _ in range(NT)]
    sz_db = [Buf() for _ in range(NT)]
    yn_db = [Buf() for _ in range(NT)]

    for i in range(NT):
        dma("sp", x_sb[:, i, :], x_d[i * 128:(i + 1) * 128, :], writes=[x_b[i]])

    NSTG = 4
    stg = [sb("wstg%d" % q, [128, 512]) for q in range(NSTG)]
    stg_b = [Buf() for _ in range(NSTG)]
    stg_i = [0]

    def load_w(dst_ap, src_ap, buf, fast=False):
        nk, n = dst_ap.shape[1], dst_ap.shape[2]
        g = max(1, 512 // n)
        for k0 in range(0, nk, g):
            k1 = min(nk, k0 + g)
            q = stg_i[0] % NSTG
            stg_i[0] += 1
            sv = stg[q][:, 0:(k1 - k0) * n].rearrange("p (a b) -> p a b", a=k1 - k0)
            dma("sp", sv, src_ap[:, k0:k1, :], writes=[stg_b[q]])
            if not fast:
                op("pool", lambda e: e.tensor_copy(out=dst_ap[:, k0:k1, :], in_=sv), reads=[stg_b[q]], writes=[buf])
            elif q % 2 == 0:
                op("dve", lambda e: e.tensor_copy(out=dst_ap[:, k0:k1, :], in_=sv), reads=[stg_b[q]], writes=[buf])
            else:
                op("act", lambda e: e.activation(out=dst_ap[:, k0:k1, :], in_=sv, func=AF.Copy), reads=[stg_b[q]], writes=[buf])

    def w_in_cols(l, c0, n):
        return w_in_d[l].rearrange("(kc p) n -> p kc n", p=128)[:, :, c0:c0 + n]

    def rms_rstd(src_ap, src_buf, junk_ap, junk_buf, ss_ap, ss_buf, width, eps):
        op("act", lambda e: e.activation(out=junk_ap, in_=src_ap, func=AF.Square, accum_out=ss_ap[:, 0:1]),
           reads=[src_buf], writes=[junk_buf, ss_buf])
        op("act", lambda e: e.activation(out=ss_ap[:, 1:2], in_=ss_ap[:, 0:1], func=AF.Sqrt,
                                         scale=1.0 / width, bias=eps), reads=[ss_buf], writes=[ss_buf])
        op("dve", lambda e: e.reciprocal(out=ss_ap[:, 2:3], in_=ss_ap[:, 1:2]), reads=[ss_buf], writes=[ss_buf])

    for l in range(nlayers):
        dma("sp", cp_sb[:], cp_d[l], writes=[cp_b])
        dma("sp", rp_sb[:], rp_d[l], writes=[rp_b])
        with ExitStack() as les:
            def lsb(name, shape, dt=F32, _les=les):
                return _les.enter_context(nc.sbuf_tensor("%s_l%d" % (name, l), shape, dt))

            hT = lsb("hT", [128, 8, S], BF16)
            hT_b = Buf("hT")
            with ExitStack() as pes:
                junk = pes.enter_context(nc.sbuf_tensor("p0junk_l%d" % l, [128, D], BF16))
                junk_b = Buf()
                xn = [pes.enter_context(nc.sbuf_tensor("p0xn%d_l%d" % (q, l), [128, D], BF16)) for q in range(2)]
                xn_b = [Buf(), Buf()]
                ssq = [pes.enter_context(nc.sbuf_tensor("p0ss%d_l%d" % (q, l), [128, 4], F32)) for q in range(2)]
                ssq_b = [Buf(), Buf()]
                for i in range(NT):
                    q = i % 2
                    rms_rstd(x_sb[:, i, :], x_b[i], junk[:], junk_b, ssq[q], ssq_b[q], D, RMS_EPS)
                    op("dve", lambda e: e.scalar_tensor_tensor(out=xn[q][:], in0=x_sb[:, i, :], scalar=ssq[q][:, 2:3],
                                                               in1=rp_sb[:, RP_NG:RP_NG + D], op0=ALU.mult, op1=ALU.mult),
                       reads=[x_b[i], ssq_b[q], rp_b], writes=[xn_b[q]])
                    pb = ps[i % 2].bitcast(BF16)
                    for kc in range(8):
                        op("pe", lambda e: e.transpose(pb[:, kc * 128:(kc + 1) * 128], xn[q][:, kc * 128:(kc + 1) * 128], ident[:]),
                           reads=[xn_b[q], ident_b], writes=[ps_b[i % 2]])
                    eng = "act" if i % 2 == 0 else "dve"
                    if eng == "act":
                        op("act", lambda e: e.activation(out=hT[:, :, i * 128:(i + 1) * 128],
                                                         in_=pb.rearrange("p (k t) -> p k t", k=8), func=AF.Identity),
                           reads=[ps_b[i % 2]], writes=[hT_b])
                    else:
                        op("dve", lambda e: e.tensor_copy(out=hT[:, :, i * 128:(i + 1) * 128],
                                                          in_=pb.rearrange("p (k t) -> p k t", k=8)),
                           reads=[ps_b[i % 2]], writes=[hT_b])
            kb.barrier()

            def epilogue(bname, nk, w_out_ap, gate_c0, get_lhsT, scale_fn=None):
                with ExitStack() as ees:
                    def esb(name, shape, dt=F32):
                        return ees.enter_context(nc.sbuf_tensor("ep_%s_%s_l%d" % (bname, name, l), shape, dt))
                    wout = esb("wout", [128, nk, D], BF16)
                    wout_b = Buf()
                    wg = esb("wg", [128, 8, D], BF16)
                    wg_b = Buf()
                    for n in range(2):
                        load_w(wout[:, :, n * 512:(n + 1) * 512],
                               w_out_ap.rearrange("(kc p) n -> p kc n", p=128)[:, :, n * 512:(n + 1) * 512], wout_b, fast=True)
                        load_w(wg[:, :, n * 512:(n + 1) * 512], w_in_cols(l, gate_c0 + n * 512, 512), wg_b, fast=True)
                    sig = [esb("sig%d" % q, [128, D]) for q in range(2)]
                    sig_b = [Buf(), Buf()]
                    tt = [esb("tt%d" % q, [128, D]) for q in range(2)]
                    tt_b = [Buf(), Buf()]
                    for i in range(dbg.get("ep_tiles", NT)):
                        q = i % 2
                        lhs_ap, lhs_bufs = get_lhsT(i)
                        pO = [4 + 4 * 0 + 0, 5]
                        gb0 = (6, 2)[i % 2]
                        for n in range(2):
                            for kc in range(8):
                                op("pe", lambda e: e.matmul(ps[gb0 + n][:, :], hT[:, kc, i * 128:(i + 1) * 128],
                                                            wg[:, kc, n * 512:(n + 1) * 512],
                                                            start=(kc == 0), stop=(kc == 7)),
                                   reads=[hT_b, wg_b], writes=[ps_b[gb0 + n]])
                        for n in range(2):
                            for kc in range(nk):
                                op("pe", lambda e: e.matmul(ps[4 + n][:, :], lhs_ap[:, kc, :], wout[:, kc, n * 512:(n + 1) * 512],
                                                            start=(kc == 0), stop=(kc == nk - 1)),
                                   reads=lhs_bufs + [wout_b], writes=[ps_b[4 + n]])
                        for n in range(2):
                            op("act", lambda e: e.activation(out=sig[q][:, n * 512:(n + 1) * 512], in_=ps[gb0 + n][:, :],
                                                             func=AF.Sigmoid), reads=[ps_b[gb0 + n]], writes=[sig_b[q]])
                        for n in range(2):
                            if scale_fn is None:
                                op("dve", lambda e: e.tensor_tensor(out=tt[q][:, n * 512:(n + 1) * 512], in0=ps[4 + n][:, :],
                                                                    in1=sig[q][:, n * 512:(n + 1) * 512], op=ALU.mult),
                                   reads=[ps_b[4 + n], sig_b[q]], writes=[tt_b[q]])
                            else:
                                sc_ap, sc_bufs = scale_fn(i)
                                op("dve", lambda e: e.scalar_tensor_tensor(out=tt[q][:, n * 512:(n + 1) * 512], in0=ps[4 + n][:, :],
                                                                           scalar=sc_ap, in1=sig[q][:, n * 512:(n + 1) * 512],
                                                                           op0=ALU.mult, op1=ALU.mult),
                                   reads=[ps_b[4 + n], sig_b[q]] + sc_bufs, writes=[tt_b[q]])
                        dma("sp", T_d[bname][i * 128:(i + 1) * 128, :], tt[q][:], reads=[tt_b[q]], writes=[T_buf[bname][i]])
                kb.barrier()

            if "sb" in branches:
                with ExitStack() as bes:
                    def bsb(name, shape, dt=F32):
                        return bes.enter_context(nc.sbuf_tensor("sb_%s_l%d" % (name, l), shape, dt))
                    yg_all = bsb("yg_all", [128, NT, 512], BF16)
                    yg_b = [Buf() for _ in range(NT)]
                    if dbg:
                        for i in range(NT):
                            op("pool", lambda e: e.memset(yg_all[:, i, :], 0.0), writes=[yg_b[i]])
                    with ExitStack() as aes:
                        def asb(name, shape, dt=F32):
                            return aes.enter_context(nc.sbuf_tensor("sba_%s_l%d" % (name, l), shape, dt))
                        wp = [asb("wp%d" % q, [128, 8, 512], BF16) for q in range(1)] * 2
                        wp_b = [Buf()] * 2
                        qz2 = asb("qz2", [128, NT, 2, 128], BF16)
                        qT_b = Buf()
                        op("pool", lambda e: e.memset(qz2[64:128, :, 0, :], 0.0), writes=[qT_b])
                        op("pool", lambda e: e.memset(qz2[0:64, :, 1, :], 0.0), writes=[qT_b])
                        kT = asb("kT", [128, S], BF16)
                        kT_b = Buf()
                        v_tm = asb("v_tm", [128, NT, 128], BF16)
                        v_b = Buf()
                        STR = []
                        NS = 4
                        for s_ in range(NS):
                            d_ = dict(sp_all=asb("sp_all%d" % s_, [128, NT, 2, 128], BF16), sp_b=[Buf() for _ in range(NT // 2)],
                                      etmp=[asb("etmp%d_%d" % (s_, q), [128, 512]) for q in range(1)] * 2, etmp_b=[Buf()] * 2,
                                      att=[asb("att%d_%d" % (s_, q), [128, 2, 2, 128], BF16) for q in range(1)] * 2, att_b=[Buf()] * 2,
                                      csA=asb("csA%d" % s_, [128, 32, 2]), csB=asb("csB%d" % s_, [128, 32, 2]), csA_b=Buf(), csB_b=Buf(),
                                      er=asb("er%d" % s_, [128, 16, 2]), er_b=Buf(),
                                      yacc=asb("yacc%d" % s_, [128, 2, 64]), yacc_b=[Buf(), Buf()],
                                      sg=asb("sg%d" % s_, [128, 128]), sg_b=Buf(), base=2 * s_)
                            op("dve", lambda e: e.memset(d_["csA"][:], 0.0), writes=[d_["csA_b"]])
                            op("dve", lambda e: e.memset(d_["csB"][:], 0.0), writes=[d_["csB_b"]])
                            STR.append(d_)

                        def load_pair(hp):
                            q = hp % 2
                            for ci, c0 in enumerate((C_SBQ, C_SBK, C_SBV, C_SBG)):
                                load_w(wp[q][:, :, ci * 128:(ci + 1) * 128], w_in_cols(l, c0 + hp * 128, 128), wp_b[q])

                        def sb_row(i, st, hp, q):
                            sp_all, sp_b, etmp, etmp_b, att, att_b = st["sp_all"], st["sp_b"], st["etmp"], st["etmp_b"], st["att"], st["att_b"]
                            csA, csB, csA_b, csB_b, er, er_b = st["csA"], st["csB"], st["csA_b"], st["csB_b"], st["er"], st["er_b"]
                            yacc, yacc_b, sg, sg_b, base = st["yacc"], st["yacc_b"], st["sg"], st["sg_b"], st["base"]
                            bCS = base + 1
                            bP = base + 1
                            PO = 256
                            ng = i // 2 + 1
                            qi = qz2[:, i, :, :].rearrange("p h t -> p (h t)")
                            for g in range(ng):
                                bk = base
                                eb = g % 2
                                js = [j for j in (2 * g, 2 * g + 1) if j <= i]
                                nj = len(js)
                                for jj, j in enumerate(js):
                                    op("pe", lambda e: e.matmul(ps[bk][:, jj * 256:(jj + 1) * 256], kT[:, j * 128:(j + 1) * 128], qi,
                                                                start=True, stop=True),
                                       reads=[kT_b, qT_b], writes=[ps_b[bk]])
                                yield
                                w = nj * 256
                                op("act", lambda e: e.activation(out=etmp[eb][:, 0:w], in_=ps[bk][:, 0:w], func=AF.Exp),
                                   reads=[ps_b[bk]], writes=[etmp_b[eb]])
                                yield
                                op("act", lambda e: e.activation(
                                    out=sp_all[:, 2 * g:2 * g + nj, :, :].rearrange("p a b c -> p (a b c)"),
                                    in_=etmp[eb][:, 0:w], func=AF.Ln, bias=1.0),
                                   reads=[etmp_b[eb]], writes=[sp_b[g]])
                                yield
                                if i in js:
                                    op("dve", lambda e: e.tensor_tensor(out=sp_all[:, i, :, :], in0=sp_all[:, i, :, :],
                                                                        in1=mask_lt[:], op=ALU.mult),
                                       reads=[sp_b[g], mask_lt_b], writes=[sp_b[g]])
                                    yield
                                for jj, j in enumerate(js):
                                    for h in range(2):
                                        op("pe", lambda e: e.matmul(ps[bCS][:, j * 2 + h:j * 2 + h + 1], sp_all[:, j, h, :],
                                                                    ones_bf[:, 0:1], start=True, stop=True),
                                           reads=[sp_b[g], ones_b], writes=[ps_b[bCS]])
                                    yield
                            nn = (i + 1) * 2
                            op("dve", lambda e: e.tensor_copy(out=csA[:, 0:i + 1, :].rearrange("p a b -> p (a b)"),
                                                              in_=ps[bCS][:, 0:nn]), reads=[ps_b[bCS]], writes=[csA_b])
                            yield
                            src, dst, src_b, dst_b = csA, csB, csA_b, csB_b
                            for d_ in (1, 2, 4, 8):
                                op("dve", lambda e: e.tensor_tensor(out=dst[:, 0:16, :], in0=src[:, 0:16, :],
                                                                    in1=src[:, d_:16 + d_, :], op=ALU.add),
                                   reads=[src_b], writes=[dst_b])
                                src, dst, src_b, dst_b = dst, src, dst_b, src_b
                                yield
                            op("act", lambda e: e.activation(out=er[:, 0:16, :], in_=src[:, 1:17, :], func=AF.Exp, scale=-1.0),
                               reads=[src_b], writes=[er_b])
                            yield
                            first = [True, True]
                            for g in range(ng):
                                bk = base
                                a = g % 2
                                js = [j for j in (2 * g, 2 * g + 1) if j <= i]
                                nj = len(js)
                                for jj, j in enumerate(js):
                                    o = ps[bk][:, jj * 256:(jj + 1) * 256]
                                    op("pe", lambda e: e.matmul(o, kT[:, j * 128:(j + 1) * 128], qi, start=True, stop=False),
                                       reads=[kT_b, qT_b], writes=[ps_b[bk]])
                                    op("pe", lambda e: e.matmul(o, negtri[:], sp_all[:, j, :, :].rearrange("p a b -> p (a b)"),
                                                                start=False, stop=True),
                                       reads=[negtri_b, sp_b[g]], writes=[ps_b[bk]])
                                    yield
                                w = nj * 256
                                op("act", lambda e: e.activation(out=att[a][:, 0:nj, :, :].rearrange("p a b c -> p (a b c)"),
                                                                 in_=ps[bk][:, 0:w], func=AF.Exp),
                                   reads=[ps_b[bk]], writes=[att_b[a]])
                                yield
                                if i in js:
                                    jj_i = js.index(i)
                                    op("dve", lambda e: e.tensor_tensor(out=att[a][:, jj_i, :, :], in0=att[a][:, jj_i, :, :],
                                                                        in1=mask_lt[:], op=ALU.mult),
                                       reads=[att_b[a], mask_lt_b], writes=[att_b[a]])
                                    yield
                                for jj, j in enumerate(js):
                                    for h in range(2):
                                        op("pe", lambda e: e.matmul(ps[bP][:, PO + (jj * 2 + h) * 64:PO + (jj * 2 + h + 1) * 64],
                                                                    att[a][:, jj, h, :], v_tm[:, j, h * 64:(h + 1) * 64],
                                                                    start=True, stop=True),
                                           reads=[att_b[a], v_b], writes=[ps_b[bP]])
                                    yield
                                for jj, j in enumerate(js):
                                    for h in range(2):
                                        pin = ps[bP][:, PO + (jj * 2 + h) * 64:PO + (jj * 2 + h + 1) * 64]
                                        if first[h]:
                                            op("dve", lambda e: e.tensor_scalar(out=yacc[:, h, :], in0=pin, scalar1=er[:, j, h:h + 1],
                                                                                scalar2=None, op0=ALU.mult),
                                               reads=[ps_b[bP], er_b], writes=[yacc_b[h]])
                                            first[h] = False
                                        else:
                                            op("dve", lambda e: e.scalar_tensor_tensor(out=yacc[:, h, :], in0=pin, scalar=er[:, j, h:h + 1],
                                                                                       in1=yacc[:, h, :], op0=ALU.mult, op1=ALU.add),
                                               reads=[ps_b[bP], er_b, yacc_b[h]], writes=[yacc_b[h]])
                                        yield
                            for kc in range(8):
                                op("pe", lambda e: e.matmul(ps[bCS][:, 128:256], hT[:, kc, i * 128:(i + 1) * 128], wp[q][:, kc, 384:512],
                                                            start=(kc == 0), stop=(kc == 7)),
                                   reads=[hT_b, wp_b[q]], writes=[ps_b[bCS]])
                                if kc % 4 == 3:
                                    yield
                            op("act", lambda e: e.activation(out=sg[:], in_=ps[bCS][:, 128:256], func=AF.Silu),
                               reads=[ps_b[bCS]], writes=[sg_b])
                            yield
                            op("dve", lambda e: e.tensor_tensor(out=yg_all[:, i, hp * 128:(hp + 1) * 128],
                                                                in0=yacc[:].rearrange("p a b -> p (a b)"), in1=sg[:], op=ALU.mult),
                               reads=[yacc_b[0], yacc_b[1], sg_b], writes=[yg_b[i]])
                            yield

                        for hp in range(dbg.get("sb_pairs", 4)):
                            q = hp % 2
                            load_pair(hp)
                            for st in STR:
                                op("dve", lambda e: e.memset(st["csA"][:], 0.0), writes=[st["csA_b"]])
                            for ci in range(2):
                                for tg in range(4):
                                    bk = (ci * 4 + tg) % 4
                                    for kc in range(8):
                                        op("pe", lambda e: e.matmul(ps[bk][:, :], wp[q][:, kc, ci * 128:(ci + 1) * 128],
                                                                    hT[:, kc, tg * 512:(tg + 1) * 512],
                                                                    start=(kc == 0), stop=(kc == 7)),
                                           reads=[wp_b[q], hT_b], writes=[ps_b[bk]])
                                    if ci == 0:
                                        for h in range(2):
                                            op("act", lambda e: e.activation(out=qz2[h * 64:(h + 1) * 64, tg * 4:(tg + 1) * 4, h, :],
                                                                             in_=ps[bk][h * 64:(h + 1) * 64, :].rearrange("p (a t) -> p a t", a=4),
                                                                             func=AF.Copy, scale=0.125),
                                               reads=[ps_b[bk]], writes=[qT_b])
                                    else:
                                        op("act", lambda e: e.activation(out=kT[:, tg * 512:(tg + 1) * 512], in_=ps[bk][:, :],
                                                                         func=AF.Copy), reads=[ps_b[bk]], writes=[kT_b])
                            for i4 in range(4):
                                bk = 4 + i4 % 4
                                for ii in range(4):
                                    i = i4 * 4 + ii
                                    for kc in range(8):
                                        op("pe", lambda e: e.matmul(ps[bk][:, ii * 128:(ii + 1) * 128], hT[:, kc, i * 128:(i + 1) * 128],
                                                                    wp[q][:, kc, 256:384], start=(kc == 0), stop=(kc == 7)),
                                           reads=[wp_b[q], hT_b], writes=[ps_b[bk]])
                                op("dve", lambda e: e.tensor_copy(out=v_tm[:, i4 * 4:(i4 + 1) * 4, :],
                                                                  in_=ps[bk][:, :].rearrange("p (a b) -> p a b", a=4)),
                                   reads=[ps_b[bk]], writes=[v_b])
                            nrows = dbg.get("sb_rows", NT)
                            for i0 in range(0, nrows, NS):
                                gens = [sb_row(i0 + s_, STR[s_], hp, q) for s_ in range(NS) if i0 + s_ < nrows]
                                while gens:
                                    for g_ in list(gens):
                                        try:
                                            next(g_)
                                        except StopIteration:
                                            gens.remove(g_)
                    kb.barrier()
                    ygT = [bsb("ygT%d" % q, [128, 4, 128], BF16) for q in range(2)]
                    ygT_b = [Buf(), Buf()]

                    def sb_lhsT(i):
                        q = i % 2
                        pb = ps[q].bitcast(BF16)
                        for c in range(4):
                            op("pe", lambda e: e.transpose(pb[:, c * 128:(c + 1) * 128], yg_all[:, i, c * 128:(c + 1) * 128], ident[:]),
                               reads=[yg_b[i], ident_b], writes=[ps_b[q]])
                        op("dve", lambda e: e.tensor_copy(out=ygT[q][:].rearrange("p a b -> p (a b)"), in_=pb[:, 0:512]),
                           reads=[ps_b[q]], writes=[ygT_b[q]])
                        return ygT[q], [ygT_b[q]]

                    epilogue("sb", 4, w_out_sb_d[l], C_GATE, sb_lhsT)


            if "ssd" in branches:
                with ExitStack() as bes:
                    def bsb(name, shape, dt=F32):
                        return bes.enter_context(nc.sbuf_tensor("ssd_%s_l%d" % (name, l), shape, dt))
                    pes = ExitStack()
                    bsb_outer = bsb

                    def bsb(name, shape, dt=F32):
                        return pes.enter_context(nc.sbuf_tensor("ssd_%s_l%d" % (name, l), shape, dt))
                    B_fm = bsb("B_fm", [128, S], BF16)
                    B_fm_b = Buf()
                    xs_tm = bsb("xs_tm", [128, NT, D], BF16)
                    xs_tm_b = [Buf() for _ in range(NT)]
                    Cz = [bsb("Cz%d" % g, [128, S], BF16) for g in range(2)]
                    Cz_b = Buf()
                    B_tm = bsb("B_tm", [128, NT, 128], BF16)
                    B_tm_b = Buf()
                    dt_tm = bsb("dt_tm", [128, NT, 16])
                    da_tm = bsb("da_tm", [128, NT, 16])
                    dt_b = Buf()
                    ah = bsb("ah", [128, 16])
                    ah_b = Buf()
                    op("act", lambda e: e.activation(out=ah[:], in_=rp_sb[:, RP_ALOG:RP_ALOG + 16], func=AF.Exp), reads=[rp_b], writes=[ah_b])
                    op("dve", lambda e: e.tensor_scalar(out=ah[:], in0=ah[:], scalar1=-1.0, scalar2=None, op0=ALU.mult), reads=[ah_b], writes=[ah_b])
                    op("pool", lambda e: e.memset(Cz[0][64:128, :], 0.0), writes=[Cz_b])
                    op("pool", lambda e: e.memset(Cz[1][0:64, :], 0.0), writes=[Cz_b])
                    with ExitStack() as aes:
                        def asb(name, shape, dt=F32):
                            return aes.enter_context(nc.sbuf_tensor("ssda_%s_l%d" % (name, l), shape, dt))
                        wc = [asb("wc%d" % q, [128, 8, 128], BF16) for q in range(2)]
                        wc_b = [Buf(), Buf()]
                        raw2 = [asb("raw%d" % q, [128, 3 + S]) for q in range(2)]
                        raw2_b = [Buf(), Buf()]
                        acc2 = [asb("acc%d" % q, [128, S]) for q in range(2)]
                        acc2_b = [Buf(), Buf()]
                        fm2 = [asb("fm%d" % q, [128, S], BF16) for q in range(1)] * 2
                        fm2_b = [Buf()] * 2
                        for q in range(2):
                            op("dve", lambda e: e.memset(raw2[q][:, 0:3], 0.0), writes=[raw2_b[q]])
                        load_w(wc[0][:], w_in_cols(l, C_XBC, 128), wc_b[0])

                        def conv_proj(cc):
                            q = cc % 2
                            raw, raw_b = raw2[q], raw2_b[q]
                            if cc + 1 < 10:
                                load_w(wc[1 - q][:], w_in_cols(l, C_XBC + (cc + 1) * 128, 128), wc_b[1 - q])
                            for tg in range(4):
                                bk = tg
                                for kc in range(8):
                                    op("pe", lambda e: e.matmul(ps[bk][:, :], wc[q][:, kc, :], hT[:, kc, tg * 512:(tg + 1) * 512],
                                                                start=(kc == 0), stop=(kc == 7)),
                                       reads=[wc_b[q], hT_b], writes=[ps_b[bk]])
                                op("act", lambda e: e.activation(out=raw[:, 3 + tg * 512:3 + (tg + 1) * 512], in_=ps[bk][:, :], func=AF.Copy),
                                   reads=[ps_b[bk]], writes=[raw_b])

                        def conv_post(cc):
                            q = cc % 2
                            raw, raw_b, acc, acc_b, fm, fm_b = raw2[q], raw2_b[q], acc2[q], acc2_b[q], fm2[q], fm2_b[q]
                            cw = lambda k: cp_sb[:, CP_CW + cc * 4 + k:CP_CW + cc * 4 + k + 1]
                            op("act", lambda e: e.activation(out=acc[:], in_=raw[:, 3:3 + S], func=AF.Identity, scale=cw(3),
                                                             bias=cp_sb[:, CP_CB + cc:CP_CB + cc + 1]),
                               reads=[raw_b, cp_b], writes=[acc_b])
                            for k in (2, 1, 0):
                                op("dve", lambda e: e.scalar_tensor_tensor(out=acc[:], in0=raw[:, k:k + S], scalar=cw(k), in1=acc[:],
                                                                           op0=ALU.mult, op1=ALU.add),
                                   reads=[raw_b, cp_b, acc_b], writes=[acc_b])
                            if cc < 8:
                                op("act", lambda e: e.activation(out=fm[:], in_=acc[:], func=AF.Silu), reads=[acc_b], writes=[fm_b])
                                src_t, src_tb = fm, fm_b
                            elif cc == 8:
                                op("act", lambda e: e.activation(out=B_fm[:], in_=acc[:], func=AF.Silu), reads=[acc_b], writes=[B_fm_b])
                                src_t, src_tb = B_fm, B_fm_b
                            else:
                                for g in range(2):
                                    op("act", lambda e: e.activation(out=Cz[g][g * 64:(g + 1) * 64, :], in_=acc[g * 64:(g + 1) * 64, :],
                                                                     func=AF.Silu), reads=[acc_b], writes=[Cz_b])
                                return
                            for i4 in range(2):
                                bk = 4 + i4
                                pb = ps[bk].bitcast(BF16)
                                for ii in range(8):
                                    i = i4 * 8 + ii
                                    op("pe", lambda e: e.transpose(pb[:, ii * 128:(ii + 1) * 128], src_t[:, i * 128:(i + 1) * 128], ident[:]),
                                       reads=[src_tb, ident_b], writes=[ps_b[bk]])
                                if cc < 8:
                                    op("dve", lambda e: e.tensor_copy(out=xs_tm[:, i4 * 8:(i4 + 1) * 8, cc * 128:(cc + 1) * 128],
                                                                      in_=pb.rearrange("p (a b) -> p a b", a=8)),
                                       reads=[ps_b[bk]], writes=xs_tm_b[i4 * 8:(i4 + 1) * 8])
                                else:
                                    op("dve", lambda e: e.tensor_copy(out=B_tm[:, i4 * 8:(i4 + 1) * 8, :],
                                                                      in_=pb.rearrange("p (a b) -> p a b", a=8)),
                                       reads=[ps_b[bk]], writes=[B_tm_b])

                        for step in range(11):
                            if step < 10:
                                conv_proj(step)
                            if step >= 1:
                                conv_post(step - 1)
                    kb.barrier()
                    with ExitStack() as aes:
                        def asb(name, shape, dt=F32):
                            return aes.enter_context(nc.sbuf_tensor("ssdb_%s_l%d" % (name, l), shape, dt))
                        wdt = asb("wdt", [128, 8, 16], BF16)
                        wdt_b = Buf()
                        wdt_f = asb("wdt_f", [128, 8, 16])
                        wdt_fb = Buf()
                        dma("sp", wdt_f[:], wdt_d[l], writes=[wdt_fb])
                        op("pool", lambda e: e.tensor_copy(out=wdt[:], in_=wdt_f[:]), reads=[wdt_fb], writes=[wdt_b])
                        wz = [asb("wz%d" % q, [128, 8, 512], BF16) for q in range(2)]
                        wz_b = [Buf(), Buf()]
                        for n in range(2):
                            load_w(wz[n][:], w_in_cols(l, C_Z + n * 512, 512), wz_b[n], fast=True)
                        dtmp = asb("dtmp", [128, 16])
                        dtmp_b = Buf()
                        szt = [asb("szt%d" % q, [128, 512], BF16) for q in range(2)]
                        szt_b = [Buf(), Buf()]
                        for i in range(NT):
                            for kc in range(8):
                                op("pe", lambda e: e.matmul(ps[0][:, 0:16], hT[:, kc, i * 128:(i + 1) * 128], wdt[:, kc, :],
                                                            start=(kc == 0), stop=(kc == 7)), reads=[hT_b, wdt_b], writes=[ps_b[0]])
                            op("dve", lambda e: e.tensor_tensor(out=dtmp[:], in0=ps[0][:, 0:16], in1=rp_sb[:, RP_DTB:RP_DTB + 16], op=ALU.add),
                               reads=[ps_b[0], rp_b], writes=[dtmp_b])
                            op("act", lambda e: e.activation(out=dtmp[:], in_=dtmp[:], func=AF.Exp), reads=[dtmp_b], writes=[dtmp_b])
                            op("act", lambda e: e.activation(out=dt_tm[:, i, :], in_=dtmp[:], func=AF.Ln, bias=1.0), reads=[dtmp_b], writes=[dt_b])
                            op("dve", lambda e: e.tensor_tensor(out=da_tm[:, i, :], in0=dt_tm[:, i, :], in1=ah[:], op=ALU.mult),
                               reads=[dt_b, ah_b], writes=[dt_b])
                        for n in range(2):
                            for i in range(NT):
                                q = i % 2
                                bk = 1 + q
                                for kc in range(8):
                                    op("pe", lambda e: e.matmul(ps[bk][:, :], hT[:, kc, i * 128:(i + 1) * 128], wz[n][:, kc, :],
                                                                start=(kc == 0), stop=(kc == 7)), reads=[hT_b, wz_b[n]], writes=[ps_b[bk]])
                                op("act", lambda e: e.activation(out=szt[q][:], in_=ps[bk][:, :], func=AF.Silu), reads=[ps_b[bk]], writes=[szt_b[q]])
                                dma("sp", sz_d[i * 128:(i + 1) * 128, n * 512:(n + 1) * 512], szt[q][:], reads=[szt_b[q]], writes=[sz_db[i]])
                    kb.barrier()
                    with ExitStack() as aes:
                        def asb(name, shape, dt=F32):
                            return aes.enter_context(nc.sbuf_tensor("ssdc_%s_l%d" % (name, l), shape, dt))
                        sz_t = [asb("sz_t%d" % q, [128, D], BF16) for q in range(2)]
                        sz_tb = [Buf(), Buf()]
                        x_dt = asb("x_dt", [128, 16, 64], BF16)
                        x_dt_b = Buf()
                        x_dts = asb("x_dts", [128, 16, 64], BF16)
                        x_dts_b = Buf()
                        eas = asb("eas", [128, 3, 16])
                        eas_b = Buf()
                        rseg = asb("rseg", [128, 16, 128])
                        rseg_b = Buf()
                        eseg = asb("eseg", [128, 16, 128], BF16)
                        eseg_b = Buf()
                        cbm = asb("cbm", [128, 2, 128], BF16)
                        cbm_b = Buf()
                        MT = asb("MT", [128, 16, 128], BF16)
                        MT_b = Buf()
                        ST = asb("ST", [128, 8, 64])
                        ST_b = Buf()
                        ST_bf = asb("ST_bf", [128, 512], BF16)
                        ST_bf_b = Buf()
                        t1 = asb("t1", [128, 16, 64])
                        t1_b = Buf()
                        t2 = asb("t2", [128, D])
                        t2_b = Buf()
                        t3, t3_b = t1, t1_b
                        ssq = asb("ssq", [128, 4])
                        ssq_b = Buf()
                        yn = [asb("yn%d" % q, [128, D], BF16) for q in range(2)]
                        yn_b = [Buf(), Buf()]
                        op("dve", lambda e: e.memset(ST[:], 0.0), writes=[ST_b])
                        op("dve", lambda e: e.memset(ST_bf[:], 0.0), writes=[ST_bf_b])

                        def bc_p(ap2, n0, n1):
                            return ap2.unsqueeze(2).broadcast_to([128, n0, n1])

                        for c in range(dbg.get("ssd_chunks", NT)):
                            q = c % 2
                            tok = slice(c * 128, (c + 1) * 128)
                            xs_c = xs_tm[:, c, :].rearrange("p (a b) -> p a b", a=16)
                            dma("sp", sz_t[q][:], sz_d[tok, :], reads=[sz_db[c]], writes=[sz_tb[q]])
                            da_c = da_tm[:, c, :]
                            for k3, (lh, lh_b) in enumerate(((triLE, triLE_b), (maskGT, maskGT_b), (ones_f, ones_f_b))):
                                op("pe", lambda e: e.matmul(ps[0][:, k3 * 16:(k3 + 1) * 16], lh[:], da_c, start=True, stop=True),
                                   reads=[lh_b, dt_b], writes=[ps_b[0]])
                            op("act", lambda e: e.activation(out=eas[:].rearrange("p a b -> p (a b)"), in_=ps[0][:, 0:48], func=AF.Exp),
                               reads=[ps_b[0]], writes=[eas_b])
                            for g in range(2):
                                op("pe", lambda e: e.matmul(ps[7][:, g * 128:(g + 1) * 128], B_fm[:, tok], Cz[g][:, tok], start=True, stop=True),
                                   reads=[B_fm_b, Cz_b], writes=[ps_b[7]])
                            op("dve", lambda e: e.tensor_tensor(out=cbm[:], in0=ps[7][:, 0:256].rearrange("p (a b) -> p a b", a=2),
                                                                in1=triLE[:].unsqueeze(1).broadcast_to([128, 2, 128]), op=ALU.mult),
                               reads=[ps_b[7], triLE_b], writes=[cbm_b])
                            op("pool", lambda e: e.tensor_tensor(out=rseg[:], in0=triLE[:].unsqueeze(1).broadcast_to([128, 16, 128]),
                                                                 in1=bc_p(da_c, 16, 128), op=ALU.mult),
                               reads=[triLE_b, dt_b], writes=[rseg_b])
                            for q4 in range(4):
                                bk = 1 + q4
                                op("pe", lambda e: e.matmul(ps[bk][:, :], maskGT[:], rseg[:, q4 * 4:(q4 + 1) * 4, :].rearrange("p a b -> p (a b)"),
                                                            start=True, stop=True), reads=[maskGT_b, rseg_b], writes=[ps_b[bk]])
                                op("act", lambda e: e.activation(out=eseg[:, q4 * 4:(q4 + 1) * 4, :].rearrange("p a b -> p (a b)"),
                                                                 in_=ps[bk][:, :], func=AF.Exp), reads=[ps_b[bk]], writes=[eseg_b])
                            for g in range(2):
                                op("dve", lambda e: e.tensor_tensor(out=MT[:, g * 8:(g + 1) * 8, :], in0=eseg[:, g * 8:(g + 1) * 8, :],
                                                                    in1=cbm[:, g, :].unsqueeze(1).broadcast_to([128, 8, 128]), op=ALU.mult),
                                   reads=[eseg_b, cbm_b], writes=[MT_b])
                            op("dve", lambda e: e.tensor_tensor(out=x_dt[:], in0=xs_c, in1=bc_p(dt_tm[:, c, :], 16, 64), op=ALU.mult),
                               reads=[xs_tm_b[c], dt_b], writes=[x_dt_b])
                            op("pool", lambda e: e.tensor_tensor(out=x_dts[:], in0=x_dt[:], in1=bc_p(eas[:, 1, :], 16, 64), op=ALU.mult),
                               reads=[x_dt_b, eas_b], writes=[x_dts_b])
                            for h in range(16):
                                bk = 5 + h // 8
                                op("pe", lambda e: e.matmul(ps[bk][:, (h % 8) * 64:(h % 8 + 1) * 64], MT[:, h, :], x_dt[:, h, :],
                                                            start=True, stop=True), reads=[MT_b, x_dt_b], writes=[ps_b[bk]])
                            for g in range(2):
                                op("pe", lambda e: e.matmul(ps[1 + g][:, :], Cz[g][:, tok], ST_bf[:], start=True, stop=True),
                                   reads=[Cz_b, ST_bf_b], writes=[ps_b[1 + g]])
                            for g in range(2):
                                op("pe", lambda e: e.matmul(ps[3 + g][:, :], B_tm[:, c, :],
                                                            x_dts[:, g * 8:(g + 1) * 8, :].rearrange("p a b -> p (a b)"),
                                                            start=True, stop=True), reads=[B_tm_b, x_dts_b], writes=[ps_b[3 + g]])
                            for g in range(2):
                                op("dve", lambda e: e.tensor_tensor(out=t1[:, g * 8:(g + 1) * 8, :],
                                                                    in0=ps[1 + g][:, :].rearrange("p (a b) -> p a b", a=8),
                                                                    in1=bc_p(eas[:, 0, g * 8:(g + 1) * 8], 8, 64), op=ALU.mult),
                                   reads=[ps_b[1 + g], eas_b], writes=[t1_b])
                            for g in range(2):
                                op("dve", lambda e: e.tensor_tensor(out=t2[:, g * 512:(g + 1) * 512], in0=ps[5 + g][:, :],
                                                                    in1=t1[:, g * 8:(g + 1) * 8, :].rearrange("p a b -> p (a b)"), op=ALU.add),
                                   reads=[ps_b[5 + g], t1_b], writes=[t2_b])
                            op("pool", lambda e: e.tensor_tensor(out=t3[:], in0=xs_c, in1=bc_p(rp_sb[:, RP_DSK:RP_DSK + 16], 16, 64), op=ALU.mult),
                               reads=[xs_tm_b[c], rp_b], writes=[t3_b])
                            op("pool", lambda e: e.tensor_tensor(out=t2[:], in0=t2[:], in1=t3[:].rearrange("p a b -> p (a b)"), op=ALU.add),
                               reads=[t2_b, t3_b], writes=[t2_b])
                            op("pool", lambda e: e.tensor_tensor(out=t2[:], in0=t2[:], in1=sz_t[q][:], op=ALU.mult),
                               reads=[t2_b, sz_tb[q]], writes=[t2_b])
                            for g in range(2):
                                hp_ = slice(g * 64, (g + 1) * 64)
                                op("dve", lambda e: e.tensor_tensor(out=ST[hp_, :, :], in0=ST[hp_, :, :],
                                                                    in1=eas[hp_, 2, g * 8:(g + 1) * 8].unsqueeze(2).broadcast_to([64, 8, 64]),
                                                                    op=ALU.mult), reads=[ST_b, eas_b], writes=[ST_b])
                                op("dve", lambda e: e.tensor_tensor(out=ST[hp_, :, :], in0=ps[3 + g][hp_, :].rearrange("p (a b) -> p a b", a=8),
                                                                    in1=ST[hp_, :, :], op=ALU.add), reads=[ps_b[3 + g], ST_b], writes=[ST_b])
                            op("act", lambda e: e.activation(out=ST_bf[:], in_=ST[:].rearrange("p a b -> p (a b)"), func=AF.Copy),
                               reads=[ST_b], writes=[ST_bf_b])
                            rms_rstd(t2[:], t2_b, x_dts[:].rearrange("p a b -> p (a b)"), x_dts_b, ssq, ssq_b, D, RMS_EPS)
                            op("dve", lambda e: e.scalar_tensor_tensor(out=yn[q][:], in0=t2[:], scalar=ssq[:, 2:3],
                                                                       in1=rp_sb[:, RP_SG:RP_SG + D], op0=ALU.mult, op1=ALU.mult),
                               reads=[t2_b, ssq_b, rp_b], writes=[yn_b[q]])
                            dma("sp", yn_d[tok, :], yn[q][:], reads=[yn_b[q]], writes=[yn_db[c]])
                    kb.barrier()
                    pes.close()
                    bsb = bsb_outer
                    ynl = [bsb("ynl%d" % q, [128, D], BF16) for q in range(2)]
                    ynl_b = [Buf(), Buf()]
                    ynT = [bsb("ynT%d" % q, [128, 8, 128], BF16) for q in range(2)]
                    ynT_b = [Buf(), Buf()]

                    def ssd_lhsT(i):
                        q = i % 2
                        dma("sp", ynl[q][:], yn_d[i * 128:(i + 1) * 128, :], reads=[yn_db[i]], writes=[ynl_b[q]])
                        pb = ps[q].bitcast(BF16)
                        for c in range(8):
                            op("pe", lambda e: e.transpose(pb[:, c * 128:(c + 1) * 128], ynl[q][:, c * 128:(c + 1) * 128], ident[:]),
                               reads=[ynl_b[q], ident_b], writes=[ps_b[q]])
                        op("dve", lambda e: e.tensor_copy(out=ynT[q][:].rearrange("p a b -> p (a b)"), in_=pb[:, :]),
                           reads=[ps_b[q]], writes=[ynT_b[q]])
                        return ynT[q], [ynT_b[q]]

                    epilogue("ssd", 8, w_out_ssd_d[l], C_GATE + D, ssd_lhsT)

            if "rw" in branches:
                with ExitStack() as bes:
                    def bsb(name, shape, dt=F32):
                        return bes.enter_context(nc.sbuf_tensor("rw_%s_l%d" % (name, l), shape, dt))
                    C0 = 0.6065306597126334
                    QT = 256
                    NQ = S // QT
                    NC_ = QT // 64
                    lora_bf = bsb("lora_bf", [128, S], BF16)
                    lora_b = Buf()
                    carry = bsb("carry", [128, 8])
                    carry_b = [Buf() for _ in range(5)]
                    raw = bsb("raw", [128, QT + 1])
                    raw_b = Buf()
                    tmpm = bsb("tmpm", [128, QT])
                    tmpm_b = Buf()
                    pbank = [0]

                    def proj_mix(wt, wt_b, wcol, mu_col, stream, qt, out_ap, out_buf):
                        t0 = qt * QT
                        bk = pbank[0] % 2
                        pbank[0] += 1
                        for kc in range(8):
                            op("pe", lambda e: e.matmul(ps[bk][:, 0:QT], wt[:, kc, wcol:wcol + 128], hT[:, kc, t0:t0 + QT],
                                                        start=(kc == 0), stop=(kc == 7)), reads=[wt_b, hT_b], writes=[ps_b[bk]])
                            if kc % 2 == 1:
                                yield
                        if qt == 0:
                            op("dve", lambda e: e.memset(raw[:, 0:1], 0.0), writes=[raw_b])
                        else:
                            op("dve", lambda e: e.tensor_copy(out=raw[:, 0:1], in_=carry[:, stream:stream + 1]),
                               reads=[carry_b[stream]], writes=[raw_b])
                        yield
                        op("act", lambda e: e.activation(out=raw[:, 1:QT + 1], in_=ps[bk][:, 0:QT], func=AF.Copy), reads=[ps_b[bk]], writes=[raw_b])
                        yield
                        op("dve", lambda e: e.tensor_copy(out=carry[:, stream:stream + 1], in_=raw[:, QT:QT + 1]),
                           reads=[raw_b], writes=[carry_b[stream]])
                        op("dve", lambda e: e.tensor_tensor(out=tmpm[:], in0=raw[:, 0:QT], in1=raw[:, 1:QT + 1], op=ALU.subtract),
                           reads=[raw_b], writes=[tmpm_b])
                        yield
                        op("dve", lambda e: e.scalar_tensor_tensor(out=out_ap, in0=tmpm[:], scalar=cp_sb[:, mu_col:mu_col + 1],
                                                                   in1=raw[:, 1:QT + 1], op0=ALU.mult, op1=ALU.add),
                           reads=[tmpm_b, raw_b, cp_b], writes=[out_buf])
                        yield

                    def drain(g):
                        for _ in g:
                            pass

                    with ExitStack() as aes:
                        wl = aes.enter_context(nc.sbuf_tensor("rw_wl_l%d" % l, [128, 8, 128], BF16))
                        wl_b = Buf()
                        load_w(wl[:], w_in_cols(l, C_RW + 2048, 128), wl_b)
                        lmix = aes.enter_context(nc.sbuf_tensor("rw_lmix_l%d" % l, [128, QT], F32))
                        lmix_b = Buf()
                        for qt in range(NQ):
                            drain(proj_mix(wl, wl_b, 0, CP_MU + 16, 0, qt, lmix[:], lmix_b))
                            op("act", lambda e: e.activation(out=lora_bf[0:64, qt * QT:(qt + 1) * QT], in_=lmix[0:64, :], func=AF.Tanh),
                               reads=[lmix_b], writes=[lora_b])
                            op("act", lambda e: e.activation(out=lora_bf[64:128, qt * QT:(qt + 1) * QT], in_=lmix[64:128, :], func=AF.Copy),
                               reads=[lmix_b], writes=[lora_b])
                    kb.barrier()

                    pes = ExitStack()
                    bsb_outer = bsb

                    def bsb(name, shape, dt=F32):
                        return pes.enter_context(nc.sbuf_tensor("rw_%s_l%d" % (name, l), shape, dt))
                    rmask = bsb("rmask", [128, QT])
                    rmask_b = Buf()
                    op("pool", lambda e: e.memset(rmask[:], 1.0), writes=[rmask_b])
                    op("pool", lambda e: e.memset(rmask[:].rearrange("p (c t) -> p c t", t=64)[:, :, 0:1], 0.0), reads=[rmask_b], writes=[rmask_b])
                    m192 = bsb("m192", [128, 192])
                    m192_b = Buf()
                    mLT = bsb("mLT", [128, 128])
                    mLT_b = Buf()
                    bones = bsb("bones", [128, 128], BF16)
                    bones_b = Buf()
                    op("pool", lambda e: e.memset(m192[:], 0.0), writes=[m192_b])
                    op("pool", lambda e: e.memset(mLT[:], 0.0), writes=[mLT_b])
                    op("pool", lambda e: e.memset(bones[:], 0.0), writes=[bones_b])
                    for h in range(2):
                        hs_ = slice(h * 64, (h + 1) * 64)
                        mk_mask(m192[hs_, h * 64:(h + 1) * 64], m192_b, 1.0, 0.0, 0, -1, 1, 64, ALU.is_gt)
                        mk_mask(m192[hs_, 128:192], m192_b, 1.0, 0.0, 0, -1, 1, 64, ALU.is_ge)
                        mk_mask(mLT[hs_, h * 64:(h + 1) * 64], mLT_b, 1.0, 0.0, 0, 1, -1, 64, ALU.is_gt)
                        op("pool", lambda e: e.memset(bones[hs_, h * 64:(h + 1) * 64], 1.0), reads=[bones_b], writes=[bones_b])
                    S_st = bsb("S_st", [128, 128])
                    S_st_b = Buf()
                    S_bf = bsb("S_bf", [128, 128], BF16)
                    S_bf_b = Buf()
                    omka = bsb("omka", [128, 4])
                    omka_b = Buf()
                    op("dve", lambda e: e.tensor_scalar(out=omka[:], in0=cp_sb[:, CP_KA:CP_KA + 4], scalar1=-1.0, scalar2=1.0,
                                                        op0=ALU.mult, op1=ALU.add), reads=[cp_b], writes=[omka_b])
                    wr = bsb("wr", [128, 8, 512], BF16)
                    wr_b = Buf()
                    wst = bsb("wst", [128, 128])
                    wst_b = Buf()
                    WU = bsb("WU", [128, 128], BF16)
                    AU = bsb("AU", [128, 128], BF16)
                    WU_b = Buf()
                    op("pool", lambda e: e.memset(WU[64:128, :], 0.0), writes=[WU_b])
                    op("pool", lambda e: e.memset(AU[0:64, :], 0.0), writes=[WU_b])
                    NPB = 3
                    PQ = []
                    for k_ in range(NPB):
                        d_ = dict(AR=bsb("AR%d" % k_, [128, NC_, 192], BF16), bbd=bsb("bbd%d" % k_, [128, NC_, 128], BF16),
                                  kbd=bsb("kbd%d" % k_, [128, NC_, 128], BF16), vbd=bsb("vbd%d" % k_, [128, NC_, 128], BF16),
                                  v=bsb("v_bf%d" % k_, [128, QT], BF16), g=bsb("g_bf%d" % k_, [128, QT], BF16),
                                  rk=bsb("rk_bf%d" % k_, [128, QT], BF16), gC=bsb("gC%d" % k_, [128, NC_]), b=Buf())
                        for nm in ("AR", "bbd", "kbd", "vbd"):
                            op("pool", lambda e: e.memset(d_[nm][:], 0.0), writes=[d_["b"]])
                        PQ.append(d_)
                    Fn, Fb = {}, {}
                    for nm in ("r", "k", "sig", "a", "kk", "L", "t1", "t2"):
                        Fn[nm] = bsb("F_" + nm, [128, QT])
                        Fb[nm] = Buf()
                    sq_bf = bsb("sq_bf", [128, QT], BF16)
                    sq_b = Buf()
                    TQ = []
                    for k_ in range(2):
                        TQ.append(dict(AB=bsb("AB%d" % k_, [128, NC_, 192], BF16), AK=bsb("AK%d" % k_, [128, NC_, 192], BF16),
                                       TR=bsb("TR%d" % k_, [128, NC_, 3, 128], BF16), Tinv=bsb("Tinv%d" % k_, [128, NC_, 128], BF16),
                                       cb=[Buf() for _ in range(NC_)], tb=Buf()))
                    ABT = bsb("ABT", [128, NC_, 128], BF16)
                    ABT_b = Buf()
                    An = [bsb("An%d" % q_, [128, 4, 2, 128], BF16) for q_ in range(2)]
                    An_b = [Buf(), Buf()]
                    Pn = [bsb("Pn%d" % q_, [128, 4, 128], BF16) for q_ in range(2)]
                    Pn_b = [Buf(), Buf()]
                    WT = bsb("WT", [128, 128], BF16)
                    WT_b = Buf()
                    UT = bsb("UT", [128, 128], BF16)
                    UT_b = Buf()
                    stmp = bsb("stmp", [128, 128])
                    stmp_b = Buf()
                    y_fm = bsb("y_fm", [128, QT])
                    y_fm_b = Buf()
                    yc = bsb("yc", [128, QT])
                    yc_b = Buf()
                    yb = bsb("yb", [128, QT], BF16)
                    yb_b = Buf()
                    f1 = bsb("f1", [128, QT])
                    f1_b = Buf()
                    yo = [bsb("yo%d" % q_, [128, QT], BF16) for q_ in range(2)]
                    yo_b = [Buf(), Buf()]

                    def cview(ap2):
                        return ap2.rearrange("p (c t) -> p c t", t=64)

                    units = [(hp, qt) for hp in range(dbg.get("rw_pairs", 4)) for qt in range(dbg.get("rw_quarters", NQ))]

                    def stage_P(u):
                        hp, qt = units[u]
                        pq = PQ[u % NPB]
                        pb_ = pq["b"]
                        t0 = qt * QT
                        cpc = lambda base: cp_sb[:, base + hp:base + hp + 1]
                        if qt == 0:
                            for ci in range(4):
                                load_w(wr[:, :, ci * 128:(ci + 1) * 128], w_in_cols(l, C_RW + ci * 512 + hp * 128, 128), wr_b)
                            dma("sp", wst[0:64, :], w_up_d[l][:, hp * 128:(hp + 1) * 128], writes=[wst_b])
                            dma("sp", wst[64:128, :], a_up_d[l][:, hp * 128:(hp + 1) * 128], writes=[wst_b])
                            op("pool", lambda e: e.tensor_copy(out=WU[0:64, :], in_=wst[0:64, :]), reads=[wst_b], writes=[WU_b])
                            op("pool", lambda e: e.tensor_copy(out=AU[64:128, :], in_=wst[64:128, :]), reads=[wst_b], writes=[WU_b])
                            yield
                        yield from proj_mix(wr, wr_b, 0, CP_MU + 0 + hp, 1, qt, Fn["r"][:], Fb["r"])
                        yield from proj_mix(wr, wr_b, 128, CP_MU + 4 + hp, 2, qt, Fn["k"][:], Fb["k"])
                        yield from proj_mix(wr, wr_b, 256, CP_MU + 8 + hp, 3, qt, pq["v"][:], pb_)
                        yield from proj_mix(wr, wr_b, 384, CP_MU + 12 + hp, 4, qt, Fn["t1"][:], Fb["t1"])
                        op("act", lambda e: e.activation(out=pq["g"][:], in_=Fn["t1"][:], func=AF.Silu), reads=[Fb["t1"]], writes=[pb_])
                        yield
                        op("pe", lambda e: e.matmul(ps[2][:, 0:QT], WU[:], lora_bf[:, t0:t0 + QT], start=True, stop=True),
                           reads=[WU_b, lora_b], writes=[ps_b[2]])
                        op("pe", lambda e: e.matmul(ps[2][:, QT:2 * QT], AU[:], lora_bf[:, t0:t0 + QT], start=True, stop=True),
                           reads=[WU_b, lora_b], writes=[ps_b[2]])
                        yield
                        op("act", lambda e: e.activation(out=Fn["sig"][:], in_=ps[2][:, 0:QT], func=AF.Sigmoid, bias=cpc(CP_W0)),
                           reads=[ps_b[2], cp_b], writes=[Fb["sig"]])
                        op("act", lambda e: e.activation(out=Fn["a"][:], in_=ps[2][:, QT:2 * QT], func=AF.Sigmoid, bias=cpc(CP_A0)),
                           reads=[ps_b[2], cp_b], writes=[Fb["a"]])
                        yield
                        op("dve", lambda e: e.tensor_scalar(out=Fn["kk"][:], in0=Fn["k"][:], scalar1=cpc(CP_KK), scalar2=None, op0=ALU.mult),
                           reads=[Fb["k"], cp_b], writes=[Fb["kk"]])
                        yield
                        op("act", lambda e: e.activation(out=sq_bf[:], in_=Fn["kk"][:], func=AF.Square), reads=[Fb["kk"]], writes=[sq_b])
                        yield
                        op("pe", lambda e: e.matmul(ps[2][:, 0:QT], bones[:], sq_bf[:], start=True, stop=True),
                           reads=[bones_b, sq_b], writes=[ps_b[2]])
                        yield
                        op("act", lambda e: e.activation(out=Fn["t1"][:], in_=ps[2][:, 0:QT], func=AF.Sqrt), reads=[ps_b[2]], writes=[Fb["t1"]])
                        yield
                        op("dve", lambda e: e.tensor_scalar(out=Fn["t1"][:], in0=Fn["t1"][:], scalar1=1e-12, scalar2=None, op0=ALU.max),
                           reads=[Fb["t1"]], writes=[Fb["t1"]])
                        yield
                        op("dve", lambda e: e.reciprocal(out=Fn["t1"][:], in_=Fn["t1"][:]), reads=[Fb["t1"]], writes=[Fb["t1"]])
                        yield
                        op("dve", lambda e: e.tensor_tensor(out=Fn["kk"][:], in0=Fn["kk"][:], in1=Fn["t1"][:], op=ALU.mult),
                           reads=[Fb["kk"], Fb["t1"]], writes=[Fb["kk"]])
                        yield
                        op("pool", lambda e: e.tensor_scalar(out=Fn["t1"][:], in0=Fn["a"][:], scalar1=cpc(CP_KA), scalar2=omka[:, hp:hp + 1],
                                                             op0=ALU.mult, op1=ALU.add), reads=[Fb["a"], cp_b, omka_b], writes=[Fb["t1"]])
                        yield
                        op("pool", lambda e: e.tensor_tensor(out=Fn["k"][:], in0=Fn["k"][:], in1=Fn["t1"][:], op=ALU.mult),
                           reads=[Fb["k"], Fb["t1"]], writes=[Fb["k"]])
                        yield
                        op("dve", lambda e: e.scalar_tensor_tensor(out=pq["rk"][:], in0=Fn["r"][:], scalar=cpc(CP_RK), in1=Fn["k"][:],
                                                                   op0=ALU.mult, op1=ALU.mult),
                           reads=[Fb["r"], Fb["k"], cp_b], writes=[pb_])
                        yield
                        op("dve", lambda e: e.tensor_tensor_scan(out=Fn["L"][:], data0=rmask[:], data1=Fn["sig"][:], initial=0.0,
                                                                 op0=ALU.mult, op1=ALU.add),
                           reads=[rmask_b, Fb["sig"]], writes=[Fb["L"]])
                        yield
                        op("act", lambda e: e.activation(out=Fn["t1"][:], in_=Fn["L"][:], func=AF.Exp, scale=-C0), reads=[Fb["L"]], writes=[Fb["t1"]])
                        yield
                        op("pool", lambda e: e.tensor_copy(out=pq["gC"][:], in_=cview(Fn["t1"][:])[:, :, 63]), reads=[Fb["t1"]], writes=[pb_])
                        op("dve", lambda e: e.tensor_tensor(out=pq["AR"][:, :, 128:192], in0=cview(Fn["r"][:]), in1=cview(Fn["t1"][:]), op=ALU.mult),
                           reads=[Fb["r"], Fb["t1"]], writes=[pb_])
                        yield
                        op("pool", lambda e: e.tensor_tensor(out=Fn["t2"][:], in0=Fn["L"][:], in1=Fn["sig"][:], op=ALU.subtract),
                           reads=[Fb["L"], Fb["sig"]], writes=[Fb["t2"]])
                        yield
                        op("act", lambda e: e.activation(out=Fn["t2"][:], in_=Fn["t2"][:], func=AF.Exp, scale=-C0), reads=[Fb["t2"]], writes=[Fb["t2"]])
                        yield
                        for h in range(2):
                            hs_ = slice(h * 64, (h + 1) * 64)
                            op("dve", lambda e: e.scalar_tensor_tensor(out=pq["AR"][hs_, :, h * 64:(h + 1) * 64], in0=cview(Fn["kk"][:])[hs_],
                                                                       scalar=-1.0, in1=cview(Fn["t2"][:])[hs_], op0=ALU.mult, op1=ALU.mult),
                               reads=[Fb["kk"], Fb["t2"]], writes=[pb_])
                            yield
                        op("act", lambda e: e.activation(out=Fn["t1"][:], in_=Fn["L"][:], func=AF.Exp, scale=C0), reads=[Fb["L"]], writes=[Fb["t1"]])
                        yield
                        op("pool", lambda e: e.tensor_tensor(out=Fn["t2"][:], in0=Fn["kk"][:], in1=Fn["a"][:], op=ALU.mult),
                           reads=[Fb["kk"], Fb["a"]], writes=[Fb["t2"]])
                        yield
                        for h in range(2):
                            hs_ = slice(h * 64, (h + 1) * 64)
                            op("dve", lambda e: e.tensor_tensor(out=pq["bbd"][hs_, :, h * 64:(h + 1) * 64], in0=cview(Fn["t2"][:])[hs_],
                                                                in1=cview(Fn["t1"][:])[hs_], op=ALU.mult),
                               reads=[Fb["t2"], Fb["t1"]], writes=[pb_])
                            yield
                            op("dve", lambda e: e.tensor_tensor(out=pq["kbd"][hs_, :, h * 64:(h + 1) * 64], in0=cview(Fn["k"][:])[hs_],
                                                                in1=cview(Fn["t1"][:])[hs_], op=ALU.mult),
                               reads=[Fb["k"], Fb["t1"]], writes=[pb_])
                            yield
                            op("pool", lambda e: e.tensor_copy(out=pq["vbd"][hs_, :, h * 64:(h + 1) * 64], in_=cview(pq["v"][:])[hs_]),
                               reads=[pb_], writes=[pb_])
                            yield

                    def stage_T(u):
                        pq = PQ[u % NPB]
                        pb_ = pq["b"]
                        tq = TQ[u % 2]
                        AB, AK, TR, Tinv = tq["AB"], tq["AK"], tq["TR"], tq["Tinv"]
                        cd_b = tq["cb"]
                        AR, bbd, kbd, vbd = pq["AR"], pq["bbd"], pq["kbd"], pq["vbd"]
                        for c in range(NC_):
                            bk = 3 + c % 2
                            op("pe", lambda e: e.matmul(ps[bk][:, 0:192], bbd[:, c, :], AR[:, c, :], start=True, stop=True),
                               reads=[pb_], writes=[ps_b[bk]])
                            op("pe", lambda e: e.matmul(ps[bk][:, 192:384], kbd[:, c, :], AR[:, c, :], start=True, stop=True),
                               reads=[pb_], writes=[ps_b[bk]])
                            op("pe", lambda e: e.matmul(ps[bk][:, 384:512], AR[:, c, 0:128], bbd[:, c, :], start=True, stop=True),
                               reads=[pb_], writes=[ps_b[bk]])
                            yield
                            op("dve", lambda e: e.tensor_tensor(out=AB[:, c, :], in0=ps[bk][:, 0:192], in1=m192[:], op=ALU.mult),
                               reads=[ps_b[bk], m192_b], writes=[cd_b[c]])
                            yield
                            op("dve", lambda e: e.tensor_tensor(out=AK[:, c, :], in0=ps[bk][:, 192:384], in1=m192[:], op=ALU.mult),
                               reads=[ps_b[bk], m192_b], writes=[cd_b[c]])
                            yield
                            op("dve", lambda e: e.tensor_tensor(out=ABT[:, c, :], in0=ps[bk][:, 384:512], in1=mLT[:], op=ALU.mult),
                               reads=[ps_b[bk], mLT_b], writes=[ABT_b])
                            yield
                            pb = ps[5].bitcast(BF16)
                            for k3, src_t in enumerate((vbd, bbd, kbd)):
                                op("pe", lambda e: e.transpose(pb[:, k3 * 128:(k3 + 1) * 128], src_t[:, c, :], ident[:]),
                                   reads=[pb_, ident_b], writes=[ps_b[5]])
                            yield
                            op("act", lambda e: e.activation(out=TR[:, c, :, :].rearrange("p a b -> p (a b)"), in_=pb[:, 0:384], func=AF.Copy),
                               reads=[ps_b[5]], writes=[cd_b[c]])
                            yield
                        cs_ = list(range(NC_))
                        cur_p = 0
                        for i_, c in enumerate(cs_):
                            op("dve", lambda e: e.tensor_tensor(out=Pn[0][:, i_, :], in0=AB[:, c, 0:128], in1=ident[:], op=ALU.add),
                               reads=[cd_b[c], ident_b], writes=[Pn_b[0]])
                            yield
                        getA = lambda i_, c: AB[:, c, 0:128]
                        getAT = lambda i_, c: ABT[:, c, :]
                        a_bufs = list(cd_b) + [ABT_b]
                        for lvl in range(1, 6):
                            an = An[lvl % 2]
                            an_b = An_b[lvl % 2]
                            for i_, c in enumerate(cs_):
                                bk = 3 + (i_ // 2)
                                off = (i_ % 2) * 256
                                if lvl < 5:
                                    op("pe", lambda e: e.matmul(ps[bk][:, off:off + 128], getAT(i_, c), getA(i_, c), start=True, stop=True),
                                       reads=a_bufs, writes=[ps_b[bk]])
                                op("pe", lambda e: e.matmul(ps[bk][:, off + 128:off + 256], getA(i_, c), getAT(i_, c), start=True, stop=True),
                                   reads=a_bufs, writes=[ps_b[bk]])
                                yield
                            for half in range(2):
                                bk = 3 + half
                                if lvl < 5 and half == 1:
                                    op("dve", lambda e: e.tensor_copy(out=an[:, half * 2:half * 2 + 2, :, :].rearrange("p a b c -> p (a b c)"),
                                                                      in_=ps[bk][:, :]), reads=[ps_b[bk]], writes=[an_b])
                                elif lvl < 5:
                                    op("act", lambda e: e.activation(out=an[:, half * 2:half * 2 + 2, :, :].rearrange("p a b c -> p (a b c)"),
                                                                     in_=ps[bk][:, :], func=AF.Copy), reads=[ps_b[bk]], writes=[an_b])
                                else:
                                    op("act", lambda e: e.activation(out=an[:, half * 2:half * 2 + 2, 1, :],
                                                                     in_=ps[bk][:, :].rearrange("p (a b c) -> p a b c", a=2, b=2)[:, :, 1, :],
                                                                     func=AF.Copy), reads=[ps_b[bk]], writes=[an_b])
                                yield
                            getA = lambda i_, c, an=an: an[:, i_, 0, :]
                            getAT = lambda i_, c, an=an: an[:, i_, 1, :]
                            a_bufs = [an_b]
                            pcur, pnew = Pn[cur_p], Pn[1 - cur_p]
                            pcur_b, pnew_b = Pn_b[cur_p], Pn_b[1 - cur_p]
                            for i_, c in enumerate(cs_):
                                op("pe", lambda e: e.matmul(ps[5][:, i_ * 128:(i_ + 1) * 128], getAT(i_, c), pcur[:, i_, :], start=True, stop=True),
                                   reads=[an_b, pcur_b], writes=[ps_b[5]])
                                if i_ % 2 == 1:
                                    yield
                            if lvl < 5:
                                op("dve", lambda e: e.tensor_tensor(out=pnew[:].rearrange("p a b -> p (a b)"), in0=ps[5][:, :],
                                                                    in1=pcur[:].rearrange("p a b -> p (a b)"), op=ALU.add),
                                   reads=[ps_b[5], pcur_b], writes=[pnew_b])
                            else:
                                op("dve", lambda e: e.tensor_tensor(out=Tinv[:].rearrange("p a b -> p (a b)"),
                                                                    in0=ps[5][:, :], in1=pcur[:].rearrange("p a b -> p (a b)"), op=ALU.add),
                                   reads=[ps_b[5], pcur_b], writes=[tq["tb"]])
                            yield
                            cur_p = 1 - cur_p

                    def stage_Q(u):
                        hp, qt = units[u]
                        pq = PQ[u % NPB]
                        pb_ = pq["b"]
                        tq = TQ[u % 2]
                        AB, AK, TR, Tinv = tq["AB"], tq["AK"], tq["TR"], tq["Tinv"]
                        cd_b = tq["cb"]
                        AR, gC = pq["AR"], pq["gC"]
                        t0 = qt * QT
                        cpc = lambda base: cp_sb[:, base + hp:base + hp + 1]
                        if qt == 0:
                            op("dve", lambda e: e.memset(S_st[:], 0.0), writes=[S_st_b])
                            op("dve", lambda e: e.memset(S_bf[:], 0.0), writes=[S_bf_b])
                            yield
                        for c in range(NC_):
                            op("pe", lambda e: e.matmul(ps[6][:, 0:128], AR[:, c, 0:128], S_bf[:], start=True, stop=False),
                               reads=[pb_, S_bf_b], writes=[ps_b[6]])
                            op("pe", lambda e: e.matmul(ps[6][:, 0:128], AK[:, c, 0:128], TR[:, c, 0, :], start=False, stop=True),
                               reads=[cd_b[c]], writes=[ps_b[6]])
                            yield
                            op("act", lambda e: e.activation(out=WT[:], in_=ps[6][:, 0:128], func=AF.Copy), reads=[ps_b[6]], writes=[WT_b])
                            yield
                            op("pe", lambda e: e.matmul(ps[6][:, 128:256], Tinv[:, c, :], WT[:], start=True, stop=True),
                               reads=[tq["tb"], WT_b], writes=[ps_b[6]])
                            yield
                            op("act", lambda e: e.activation(out=UT[:], in_=ps[6][:, 128:256], func=AF.Copy), reads=[ps_b[6]], writes=[UT_b])
                            yield
                            op("pe", lambda e: e.matmul(ps[6][:, 256:384], TR[:, c, 1, :], UT[:], start=True, stop=False),
                               reads=[cd_b[c], UT_b], writes=[ps_b[6]])
                            op("pe", lambda e: e.matmul(ps[6][:, 256:384], TR[:, c, 2, :], TR[:, c, 0, :], start=False, stop=True),
                               reads=[cd_b[c]], writes=[ps_b[6]])
                            yield
                            yo_ = ps[7][:, c * 64:(c + 1) * 64]
                            op("pe", lambda e: e.matmul(yo_, S_bf[:], AR[:, c, 128:192], start=True, stop=False),
                               reads=[S_bf_b, pb_], writes=[ps_b[7]])
                            op("pe", lambda e: e.matmul(yo_, UT[:], AB[:, c, 128:192], start=False, stop=False),
                               reads=[UT_b, cd_b[c]], writes=[ps_b[7]])
                            op("pe", lambda e: e.matmul(yo_, TR[:, c, 0, :], AK[:, c, 128:192], start=False, stop=True),
                               reads=[cd_b[c]], writes=[ps_b[7]])
                            yield
                            op("dve", lambda e: e.tensor_tensor(out=stmp[:], in0=ps[6][:, 256:384], in1=S_st[:], op=ALU.add),
                               reads=[ps_b[6], S_st_b], writes=[stmp_b])
                            yield
                            op("act", lambda e: e.activation(out=S_bf[:], in_=stmp[:], func=AF.Copy, scale=gC[:, c:c + 1]),
                               reads=[stmp_b, pb_], writes=[S_bf_b])
                            op("dve", lambda e: e.tensor_scalar(out=S_st[:], in0=stmp[:], scalar1=gC[:, c:c + 1], scalar2=None, op0=ALU.mult),
                               reads=[stmp_b, pb_], writes=[S_st_b])
                            yield
                        Y = ps[7][:, 0:QT]
                        M = ps[7][:, QT:2 * QT]
                        op("act", lambda e: e.activation(out=y_fm[:], in_=Y, func=AF.Copy), reads=[ps_b[7]], writes=[y_fm_b])
                        yield
                        op("act", lambda e: e.activation(out=yb[:], in_=y_fm[:], func=AF.Copy), reads=[y_fm_b], writes=[yb_b])
                        yield
                        op("pe", lambda e: e.matmul(M, bones[:], yb[:], start=True, stop=True), reads=[bones_b, yb_b], writes=[ps_b[7]])
                        yield
                        op("dve", lambda e: e.scalar_tensor_tensor(out=yc[:], in0=M, scalar=-1.0 / 64, in1=y_fm[:],
                                                                   op0=ALU.mult, op1=ALU.add), reads=[ps_b[7], y_fm_b], writes=[yc_b])
                        yield
                        op("act", lambda e: e.activation(out=yb[:], in_=yc[:], func=AF.Square), reads=[yc_b], writes=[yb_b])
                        yield
                        op("pe", lambda e: e.matmul(M, bones[:], yb[:], start=True, stop=True), reads=[bones_b, yb_b], writes=[ps_b[7]])
                        yield
                        op("act", lambda e: e.activation(out=f1[:], in_=M, func=AF.Sqrt, scale=1.0 / 64, bias=GN_EPS),
                           reads=[ps_b[7]], writes=[f1_b])
                        yield
                        op("pe", lambda e: e.matmul(M, bones[:], pq["rk"][:], start=True, stop=True), reads=[bones_b, pb_], writes=[ps_b[7]])
                        op("dve", lambda e: e.reciprocal(out=f1[:], in_=f1[:]), reads=[f1_b], writes=[f1_b])
                        yield
                        op("dve", lambda e: e.tensor_tensor(out=yc[:], in0=yc[:], in1=f1[:], op=ALU.mult), reads=[yc_b, f1_b], writes=[yc_b])
                        yield
                        op("dve", lambda e: e.tensor_scalar(out=yc[:], in0=yc[:], scalar1=cpc(CP_LG), scalar2=cpc(CP_LB), op0=ALU.mult, op1=ALU.add),
                           reads=[yc_b, cp_b], writes=[yc_b])
                        yield
                        op("dve", lambda e: e.tensor_tensor(out=f1[:], in0=M, in1=pq["v"][:], op=ALU.mult), reads=[ps_b[7], pb_], writes=[f1_b])
                        yield
                        op("pool", lambda e: e.tensor_tensor(out=yc[:], in0=yc[:], in1=f1[:], op=ALU.add), reads=[yc_b, f1_b], writes=[yc_b])
                        yield
                        q_ = u % 2
                        op("pool", lambda e: e.tensor_tensor(out=yo[q_][:], in0=yc[:], in1=pq["g"][:], op=ALU.mult), reads=[yc_b, pb_], writes=[yo_b[q_]])
                        yield
                        dma("sp", yrw_d[hp * 128:(hp + 1) * 128, t0:t0 + QT], yo[q_][:], reads=[yo_b[q_]], writes=[yrw_db[qt]])
                        yield

                    nu = len(units)
                    for tick in range(nu + 2):
                        gens = []
                        if tick - 2 >= 0:
                            gens.append(stage_Q(tick - 2))
                        if 0 <= tick - 1 < nu:
                            gens.append(stage_T(tick - 1))
                        if tick < nu:
                            gens.append(stage_P(tick))
                        while gens:
                            for g in list(gens):
                                try:
                                    next(g)
                                except StopIteration:
                                    gens.remove(g)
                    kb.barrier()
                    pes.close()
                    bsb = bsb_outer
                    rwl = [bsb("rwl%d" % q_, [128, 4, 128], BF16) for q_ in range(2)]
                    rwl_b = [Buf(), Buf()]

                    def rw_lhsT(i):
                        q_ = i % 2
                        dma("sp", rwl[q_][:], yrw_d.rearrange("(hp p) s -> p hp s", p=128)[:, :, i * 128:(i + 1) * 128],
                            reads=[yrw_db[(i * 128) // QT]], writes=[rwl_b[q_]])
                        return rwl[q_], [rwl_b[q_]]

                    epilogue("rw", 4, w_out_rw_d[l], C_GATE + 2 * D, rw_lhsT)
        kb.barrier()
        with ExitStack() as fes:
            def fsb(name, shape, dt=F32):
                return fes.enter_context(nc.sbuf_tensor("fin_%s_l%d" % (name, l), shape, dt))
            wo = fsb("wo", [128, 8, D], BF16)
            wo_b = Buf()
            for n in range(2):
                load_w(wo[:, :, n * 512:(n + 1) * 512], w_o_d[l].rearrange("(kc p) n -> p kc n", p=128)[:, :, n * 512:(n + 1) * 512], wo_b, fast=True)
            bl = [b for b in ("sb", "ssd", "rw") if b in branches]
            tin = {(b, q): fsb("tin_%s%d" % (b, q), [128, D]) for b in bl for q in range(2)}
            tin_b = {(b, q): Buf() for b in bl for q in range(2)}
            msum = [fsb("msum%d" % q, [128, D]) for q in range(2)]
            msum_b = [Buf(), Buf()]
            mbf = [fsb("mbf%d" % q, [128, D], BF16) for q in range(2)]
            mbf_b = [Buf(), Buf()]
            mT = [fsb("mT%d" % q, [128, 8, 128], BF16) for q in range(2)]
            mT_b = [Buf(), Buf()]
            for i in range(dbg.get("ep_tiles", NT)):
                q = i % 2
                for b in bl:
                    dma("sp", tin[(b, q)][:], T_d[b][i * 128:(i + 1) * 128, :], reads=[T_buf[b][i]], writes=[tin_b[(b, q)]])
                if len(bl) == 1:
                    op("pool", lambda e: e.tensor_copy(out=mbf[q][:], in_=tin[(bl[0], q)][:]), reads=[tin_b[(bl[0], q)]], writes=[mbf_b[q]])
                else:
                    op("pool", lambda e: e.tensor_tensor(out=msum[q][:], in0=tin[(bl[0], q)][:], in1=tin[(bl[1], q)][:], op=ALU.add),
                       reads=[tin_b[(bl[0], q)], tin_b[(bl[1], q)]], writes=[msum_b[q]])
                    if len(bl) == 3:
                        op("pool", lambda e: e.tensor_tensor(out=mbf[q][:], in0=msum[q][:], in1=tin[(bl[2], q)][:], op=ALU.add),
                           reads=[msum_b[q], tin_b[(bl[2], q)]], writes=[mbf_b[q]])
                    else:
                        op("pool", lambda e: e.tensor_copy(out=mbf[q][:], in_=msum[q][:]), reads=[msum_b[q]], writes=[mbf_b[q]])
                pb = ps[q].bitcast(BF16)
                for kc in range(8):
                    op("pe", lambda e: e.transpose(pb[:, kc * 128:(kc + 1) * 128], mbf[q][:, kc * 128:(kc + 1) * 128], ident[:]),
                       reads=[mbf_b[q], ident_b], writes=[ps_b[q]])
                op("act", lambda e: e.activation(out=mT[q][:].rearrange("p a b -> p (a b)"), in_=pb[:, :], func=AF.Identity),
                   reads=[ps_b[q]], writes=[mT_b[q]])
                for n in range(2):
                    bk = 2 + 2 * q + n
                    for kc in range(8):
                        op("pe", lambda e: e.matmul(ps[bk][:, :], mT[q][:, kc, :], wo[:, kc, n * 512:(n + 1) * 512],
                                                    start=(kc == 0), stop=(kc == 7)),
                           reads=[mT_b[q], wo_b], writes=[ps_b[bk]])
                    op("dve", lambda e: e.tensor_tensor(out=x_sb[:, i, n * 512:(n + 1) * 512], in0=ps[bk][:, :],
                                                        in1=x_sb[:, i, n * 512:(n + 1) * 512], op=ALU.add),
                       reads=[ps_b[bk], x_b[i]], writes=[x_b[i]])
        kb.barrier()

    with ExitStack() as oes:
        fg = oes.enter_context(nc.sbuf_tensor("fg_sb", [128, D], F32))
        fg_b = Buf()
        dma("sp", fg[:], fg_d[:, :], writes=[fg_b])
        junk = oes.enter_context(nc.sbuf_tensor("ojunk", [128, D], BF16))
        junk_b = Buf()
        yo = [oes.enter_context(nc.sbuf_tensor("yo%d" % q, [128, D], F32)) for q in range(2)]
        yo_b = [Buf(), Buf()]
        ssq = [oes.enter_context(nc.sbuf_tensor("oss%d" % q, [128, 4], F32)) for q in range(2)]
        ssq_b = [Buf(), Buf()]
        out_toks = []
        for i in range(NT):
            q = i % 2
            rms_rstd(x_sb[:, i, :], x_b[i], junk[:], junk_b, ssq[q], ssq_b[q], D, RMS_EPS)
            op("dve", lambda e: e.scalar_tensor_tensor(out=yo[q][:], in0=x_sb[:, i, :], scalar=ssq[q][:, 2:3], in1=fg[:],
                                                       op0=ALU.mult, op1=ALU.mult),
               reads=[x_b[i], ssq_b[q], fg_b], writes=[yo_b[q]])
            ob = Buf()
            dma("sp", y_d[i * 128:(i + 1) * 128, :], yo[q][:], reads=[yo_b[q]], writes=[ob])
        kb.barrier(engines=("sp",))
    kb.barrier()
    es.close()
    return nc, kb


def _prep_inputs(inp):
    f = lambda a: np.ascontiguousarray(np.asarray(a, dtype=np.float32))
    cp = np.zeros((L, 128, NCP), np.float32)
    rp = np.zeros((L, 128, NRP), np.float32)
    for l in range(L):
        cw = f(inp["conv_w"])[l]
        cp[l, :, CP_CW:CP_CW + 40] = cw.reshape(4, 10, 128).transpose(2, 1, 0).reshape(128, 40)
        cp[l, :, CP_CB:CP_CB + 10] = f(inp["conv_b"])[l].reshape(10, 128).T
        cp[l, :, CP_MU:CP_MU + 17] = f(inp["rw_mu"])[l].reshape(17, 128).T
        for off, nm in ((CP_W0, "rw_w0"), (CP_A0, "rw_a0"), (CP_KK, "rw_k_k"), (CP_KA, "rw_k_a"),
                        (CP_RK, "rw_r_k"), (CP_LG, "rw_ln_g"), (CP_LB, "rw_ln_b")):
            cp[l, :, off:off + 4] = f(inp[nm])[l].reshape(4, 128).T
        rp[l, :, RP_NG:RP_NG + D] = f(inp["norm_g"])[l][None, :]
        rp[l, :, RP_SG:RP_SG + D] = f(inp["ssd_norm_g"])[l][None, :]
        rp[l, :, RP_DTB:RP_DTB + 16] = f(inp["dt_bias"])[l][None, :]
        rp[l, :, RP_ALOG:RP_ALOG + 16] = f(inp["a_log"])[l][None, :]
        rp[l, :, RP_DSK:RP_DSK + 16] = f(inp["d_skip"])[l][None, :]
    fg = np.ascontiguousarray(np.broadcast_to(f(inp["final_g"])[None, :], (128, D)))
    shared = {
        "w_in": f(inp["w_in"]), "w_out_sb": f(inp["w_out_sb"]), "w_out_ssd": f(inp["w_out_ssd"]),
        "w_out_rw": f(inp["w_out_rw"]), "w_o": f(inp["w_o"]), "rw_w_up": f(inp["rw_w_up"]),
        "rw_a_up": f(inp["rw_a_up"]), "cp": cp, "rp": rp, "fg": fg,
        "w_dt": np.ascontiguousarray(f(inp["w_in"])[:, :, C_DT:C_DT + 16].reshape(L, 8, 128, 16).transpose(0, 2, 1, 3)),
    }
    x = f(inp["x"])
    return [dict(shared, x=x[b]) for b in range(x.shape[0])]


def kernel(**inputs):
    in_maps = _prep_inputs(inputs)
    nc, _ = build()
    res = run_bass_kernel_spmd(nc, in_maps, core_ids=list(range(8)))
    return np.stack([r["y"] for r in res.results], axis=0).astype(np.float32)
```

```python
from contextlib import ExitStack
import numpy as np
import concourse.bass as bass
import concourse.mybir as mybir
from concourse.bass_utils import run_bass_kernel_spmd

F32 = mybir.dt.float32
BF16 = mybir.dt.bfloat16
AF = mybir.ActivationFunctionType
ALU = mybir.AluOpType
AX = mybir.AxisListType

S = 2048
D = 1024
L = 2
NT = 16
N_IN = 9616
C_SBQ, C_SBK, C_SBV, C_SBG = 0, 512, 1024, 1536
C_Z, C_XBC, C_DT = 2048, 3072, 4352
C_RW = 4368
C_GATE = 6544
RMS_EPS = 1e-6
GN_EPS = 64e-5
CP_CW, CP_CB, CP_MU, CP_W0, CP_A0, CP_KK, CP_KA, CP_RK, CP_LG, CP_LB = 0, 40, 50, 67, 71, 75, 79, 83, 87, 91
NCP = 95
RP_NG, RP_SG, RP_DTB, RP_ALOG, RP_DSK = 0, 1024, 2048, 2064, 2080
NRP = 2096

SAME_ENGINE_SYNC = True
SAME_ENGINE_WAW = True
NDS = 24


class Buf:
    __slots__ = ("name", "w", "r", "excl")

    def __init__(self, name="", excl=False):
        self.name = name
        self.excl = excl
        self.w = None
        self.r = {}


class Eng:
    def __init__(self, name, e, sem, key):
        self.name, self.e, self.sem, self.key = name, e, sem, key
        self.cnt = 0
        self.clock = {}


class KB:
    def __init__(self, nc, es):
        self.nc = nc
        self.engs = {}
        self.sems = []
        for name, e in (("pe", nc.tensor), ("act", nc.scalar), ("dve", nc.vector),
                        ("pool", nc.gpsimd), ("sp", nc.sync)):
            sem = es.enter_context(nc.semaphore("s_" + name))
            self.engs[name] = Eng(name, e, sem, len(self.sems))
            self.sems.append(sem)
        self.dbase = len(self.sems)
        for i in range(NDS):
            self.sems.append(es.enter_context(nc.semaphore("d%d" % i)))
        self.dcnt = [0] * NDS
        self.dnext = 0
        self.ninst = 0

    def _wait_deps(self, E, reads, writes):
        need = {}

        def add(tok):
            if tok is None:
                return
            k = tok[0]
            if k == E.key and (E.name == "pe" or not SAME_ENGINE_SYNC):
                return
            cur = need.get(k)
            if cur is None or cur[1] < tok[1]:
                need[k] = tok

        for b in reads:
            add(b.w)
        for b in writes:
            if b.w is not None and (b.w[0] != E.key or SAME_ENGINE_WAW):
                add(b.w)
            for t in b.r.values():
                if t[0] != E.key or SAME_ENGINE_WAW:
                    add(t)
        new = None
        for k, tok in need.items():
            if E.clock.get(k, 0) < tok[1]:
                E.e.wait_ge(self.sems[k], tok[1])
                if new is None:
                    new = dict(E.clock)
                for kk, vv in tok[2].items():
                    if new.get(kk, 0) < vv:
                        new[kk] = vv
                if new.get(k, 0) < tok[1]:
                    new[k] = tok[1]
        if new is not None:
            E.clock = new

    def _mark(self, tok, reads, writes):
        for b in reads:
            b.r[tok[0]] = tok
        for b in writes:
            b.w = tok
            b.r = {}

    def op(self, en, fn, reads=(), writes=()):
        E = self.engs[en]
        if any(b.excl for b in reads):
            writes = list(writes) + [b for b in reads if b.excl and b not in writes]
            reads = [b for b in reads if not b.excl]
        self._wait_deps(E, reads, writes)
        inst = fn(E.e)
        E.cnt += 1
        inst.then_inc(E.sem, 1)
        self.ninst += 1
        tok = (E.key, E.cnt, E.clock)
        self._mark(tok, reads, writes)
        return tok

    def dma(self, q, out, in_, reads=(), writes=(), **kw):
        E = self.engs[q]
        self._wait_deps(E, reads, writes)
        j = self.dnext
        self.dnext = (j + 1) % NDS
        k = self.dbase + j
        prev = 16 * self.dcnt[j]
        if prev and E.clock.get(k, 0) < prev:
            E.e.wait_ge(self.sems[k], prev)
            new = dict(E.clock)
            new[k] = prev
            E.clock = new
        inst = E.e.dma_start(out=out, in_=in_, **kw)
        self.dcnt[j] += 1
        inst.then_inc(self.sems[k], 16)
        self.ninst += 1
        tok = (k, 16 * self.dcnt[j], E.clock)
        self._mark(tok, reads, writes)
        return tok

    def barrier(self, engines=("pe", "act", "dve", "pool", "sp")):
        for en in engines:
            E = self.engs[en]
            new = dict(E.clock)
            for F in self.engs.values():
                if F is E or F.cnt == 0:
                    continue
                if new.get(F.key, 0) < F.cnt:
                    E.e.wait_ge(F.sem, F.cnt)
                    new[F.key] = F.cnt
            for j in range(NDS):
                v = 16 * self.dcnt[j]
                k = self.dbase + j
                if v and new.get(k, 0) < v:
                    E.e.wait_ge(self.sems[k], v)
                    new[k] = v
            E.clock = new


def build(nlayers=L, branches=("sb", "ssd", "rw"), debug=False, dbg=None):
    dbg = dbg or {}
    nc = bass.Bass("TRN2", target_bir_lowering=False)
    es = ExitStack()
    kb = KB(nc, es)
    op, dma = kb.op, kb.dma

    def dram_in(name, shape):
        return nc.dram_tensor(name, shape, F32, kind="ExternalInput").ap()

    x_d = dram_in("x", [S, D])
    w_in_d = dram_in("w_in", [L, D, N_IN])
    w_out_sb_d = dram_in("w_out_sb", [L, 512, D])
    w_out_ssd_d = dram_in("w_out_ssd", [L, 1024, D])
    w_out_rw_d = dram_in("w_out_rw", [L, 512, D])
    w_o_d = dram_in("w_o", [L, D, D])
    w_up_d = dram_in("rw_w_up", [L, 64, 512])
    a_up_d = dram_in("rw_a_up", [L, 64, 512])
    cp_d = dram_in("cp", [L, 128, NCP])
    rp_d = dram_in("rp", [L, 128, NRP])
    fg_d = dram_in("fg", [128, D])
    wdt_d = dram_in("w_dt", [L, 128, 8, 16])
    y_d = nc.dram_tensor("y", [S, D], F32, kind="ExternalOutput").ap()
    tkind = "ExternalOutput" if debug else "Internal"
    T_d = {b: nc.dram_tensor("T_" + b, [S, D], F32, kind=tkind).ap() for b in ("sb", "ssd", "rw")}
    T_buf = {b: [Buf() for _ in range(NT)] for b in ("sb", "ssd", "rw")}

    def sb(name, shape, dt=F32):
        return es.enter_context(nc.sbuf_tensor(name, shape, dt))

    x_sb = sb("x_sb", [128, NT, D])
    x_b = [Buf("x%d" % i) for i in range(NT)]
    ident = sb("ident", [128, 128], BF16)
    ident_b = Buf("ident")
    ones_bf = sb("ones_bf", [128, 128], BF16)
    ones_b = Buf("ones")
    mask_lt = sb("mask_lt", [128, 2, 128], F32)
    mask_lt_b = Buf()
    negtri = sb("negtri", [128, 128], BF16)
    negtri_b = Buf()
    cp_sb = sb("cp_sb", [128, NCP])
    cp_b = Buf()
    rp_sb = sb("rp_sb", [128, NRP])
    rp_b = Buf()
    ps = [es.enter_context(nc.psum_tensor("ps%d" % i, [128, 512], F32)) for i in range(8)]
    ps_b = [Buf("ps%d" % i, excl=True) for i in range(8)]

    def mk_mask(t_ap, buf, val_true, val_false, base, cm, pat_step, n, cmp):
        op("pool", lambda e: e.memset(t_ap, val_true), writes=[buf])
        op("pool", lambda e: e.affine_select(out=t_ap, in_=t_ap, pattern=[[pat_step, n]], compare_op=cmp,
                                             fill=val_false, base=base, channel_multiplier=cm),
           reads=[buf], writes=[buf])

    mk_mask(ident[:], ident_b, 1.0, 0.0, 0, 1, -1, 128, ALU.is_equal)
    op("pool", lambda e: e.memset(ones_bf[:], 1.0), writes=[ones_b])
    for h in range(2):
        mk_mask(mask_lt[:, h, :], mask_lt_b, 1.0, 0.0, 0, -1, 1, 128, ALU.is_gt)
    mk_mask(negtri[:], negtri_b, -1.0, 0.0, 0, 1, -1, 128, ALU.is_ge)

    triLE = sb("triLE", [128, 128], F32)
    triLE_b = Buf()
    mk_mask(triLE[:], triLE_b, 1.0, 0.0, 0, -1, 1, 128, ALU.is_ge)
    maskGT = sb("maskGT", [128, 128], F32)
    maskGT_b = Buf()
    mk_mask(maskGT[:], maskGT_b, 1.0, 0.0, 0, 1, -1, 128, ALU.is_gt)
    ones_f = sb("ones_f", [128, 128], F32)
    ones_f_b = Buf()
    op("pool", lambda e: e.memset(ones_f[:], 1.0), writes=[ones_f_b])
    xs_d = nc.dram_tensor("xs_scr", [S, D], BF16, kind="Internal").ap()
    sz_d = nc.dram_tensor("sz_scr", [S, D], BF16, kind="Internal").ap()
    yn_d = nc.dram_tensor("yn_scr", [S, D], BF16, kind="Internal").ap()
    yrw_d = nc.dram_tensor("yrw_scr", [512, S], BF16, kind="Internal").ap()
    yrw_db = [Buf() for _ in range(8)]
    xs_db = [Buf() for _ in range(NT)]
    sz_db = [Buf() for _ in range(NT)]
    yn_db = [Buf() for _ in range(NT)]


    NSTG = 4
    stg = [sb("wstg%d" % q, [128, 512]) for q in range(NSTG)]
    stg_b = [Buf() for _ in range(NSTG)]
    stg_i = [0]

    def load_w(dst_ap, src_ap, buf, fast=False):
        nk, n = dst_ap.shape[1], dst_ap.shape[2]
        g = max(1, 512 // n)
        for k0 in range(0, nk, g):
            k1 = min(nk, k0 + g)
            q = stg_i[0] % NSTG
            stg_i[0] += 1
            sv = stg[q][:, 0:(k1 - k0) * n].rearrange("p (a b) -> p a b", a=k1 - k0)
            dma("sp", sv, src_ap[:, k0:k1, :], writes=[stg_b[q]])
            if not fast:
                op("pool", lambda e: e.tensor_copy(out=dst_ap[:, k0:k1, :], in_=sv), reads=[stg_b[q]], writes=[buf])
            elif q % 2 == 0:
                op("dve", lambda e: e.tensor_copy(out=dst_ap[:, k0:k1, :], in_=sv), reads=[stg_b[q]], writes=[buf])
            else:
                op("act", lambda e: e.activation(out=dst_ap[:, k0:k1, :], in_=sv, func=AF.Copy), reads=[stg_b[q]], writes=[buf])

    def w_in_cols(l, c0, n):
        return w_in_d[l].rearrange("(kc p) n -> p kc n", p=128)[:, :, c0:c0 + n]

    def rms_rstd(src_ap, src_buf, junk_ap, junk_buf, ss_ap, ss_buf, width, eps):
        op("act", lambda e: e.activation(out=junk_ap, in_=src_ap, func=AF.Square, accum_out=ss_ap[:, 0:1]),
           reads=[src_buf], writes=[junk_buf, ss_buf])
        op("act", lambda e: e.activation(out=ss_ap[:, 1:2], in_=ss_ap[:, 0:1], func=AF.Sqrt,
                                         scale=1.0 / width, bias=eps), reads=[ss_buf], writes=[ss_buf])
        op("dve", lambda e: e.reciprocal(out=ss_ap[:, 2:3], in_=ss_ap[:, 1:2]), reads=[ss_buf], writes=[ss_buf])

    for l in range(nlayers):
        dma("sp", rp_sb[:], rp_d[l], writes=[rp_b])
        dma("sp", cp_sb[:], cp_d[l], writes=[cp_b])
        if l == 0:
            for i in range(NT):
                dma("sp", x_sb[:, i, :], x_d[i * 128:(i + 1) * 128, :], writes=[x_b[i]])
        with ExitStack() as les:
            def lsb(name, shape, dt=F32, _les=les):
                return _les.enter_context(nc.sbuf_tensor("%s_l%d" % (name, l), shape, dt))

            hT = lsb("hT", [128, 8, S], BF16)
            hT_b = Buf("hT")
            with ExitStack() as pes:
                junk = pes.enter_context(nc.sbuf_tensor("p0junk_l%d" % l, [128, D], BF16))
                junk_b = Buf()
                xn = [pes.enter_context(nc.sbuf_tensor("p0xn%d_l%d" % (q, l), [128, D], BF16)) for q in range(2)]
                xn_b = [Buf(), Buf()]
                ssq = [pes.enter_context(nc.sbuf_tensor("p0ss%d_l%d" % (q, l), [128, 4], F32)) for q in range(2)]
                ssq_b = [Buf(), Buf()]
                for i in range(NT):
                    q = i % 2
                    rms_rstd(x_sb[:, i, :], x_b[i], junk[:], junk_b, ssq[q], ssq_b[q], D, RMS_EPS)
                    op("dve", lambda e: e.scalar_tensor_tensor(out=xn[q][:], in0=x_sb[:, i, :], scalar=ssq[q][:, 2:3],
                                                               in1=rp_sb[:, RP_NG:RP_NG + D], op0=ALU.mult, op1=ALU.mult),
                       reads=[x_b[i], ssq_b[q], rp_b], writes=[xn_b[q]])
                    pb = ps[i % 2].bitcast(BF16)
                    for kc in range(8):
                        op("pe", lambda e: e.transpose(pb[:, kc * 128:(kc + 1) * 128], xn[q][:, kc * 128:(kc + 1) * 128], ident[:]),
                           reads=[xn_b[q], ident_b], writes=[ps_b[i % 2]])
                    eng = "act" if i % 2 == 0 else "dve"
                    if eng == "act":
                        op("act", lambda e: e.activation(out=hT[:, :, i * 128:(i + 1) * 128],
                                                         in_=pb.rearrange("p (k t) -> p k t", k=8), func=AF.Identity),
                           reads=[ps_b[i % 2]], writes=[hT_b])
                    else:
                        op("dve", lambda e: e.tensor_copy(out=hT[:, :, i * 128:(i + 1) * 128],
                                                          in_=pb.rearrange("p (k t) -> p k t", k=8)),
                           reads=[ps_b[i % 2]], writes=[hT_b])
            kb.barrier()

            def epilogue(bname, nk, w_out_ap, gate_c0, get_lhsT, scale_fn=None):
                with ExitStack() as ees:
                    def esb(name, shape, dt=F32):
                        return ees.enter_context(nc.sbuf_tensor("ep_%s_%s_l%d" % (bname, name, l), shape, dt))
                    wout = esb("wout", [128, nk, D], BF16)
                    wout_b = Buf()
                    wg = esb("wg", [128, 8, D], BF16)
                    wg_b = Buf()
                    for n in range(2):
                        load_w(wout[:, :, n * 512:(n + 1) * 512],
                               w_out_ap.rearrange("(kc p) n -> p kc n", p=128)[:, :, n * 512:(n + 1) * 512], wout_b, fast=True)
                        load_w(wg[:, :, n * 512:(n + 1) * 512], w_in_cols(l, gate_c0 + n * 512, 512), wg_b, fast=True)
                    sig = [esb("sig%d" % q, [128, D]) for q in range(2)]
                    sig_b = [Buf(), Buf()]
                    tt = [esb("tt%d" % q, [128, D]) for q in range(2)]
                    tt_b = [Buf(), Buf()]
                    for i in range(dbg.get("ep_tiles", NT)):
                        q = i % 2
                        lhs_ap, lhs_bufs = get_lhsT(i)
                        pO = [4 + 4 * 0 + 0, 5]
                        gb0 = (6, 2)[i % 2]
                        for n in range(2):
                            for kc in range(8):
                                op("pe", lambda e: e.matmul(ps[gb0 + n][:, :], hT[:, kc, i * 128:(i + 1) * 128],
                                                            wg[:, kc, n * 512:(n + 1) * 512],
                                                            start=(kc == 0), stop=(kc == 7)),
                                   reads=[hT_b, wg_b], writes=[ps_b[gb0 + n]])
                        for n in range(2):
                            for kc in range(nk):
                                op("pe", lambda e: e.matmul(ps[4 + n][:, :], lhs_ap[:, kc, :], wout[:, kc, n * 512:(n + 1) * 512],
                                                            start=(kc == 0), stop=(kc == nk - 1)),
                                   reads=lhs_bufs + [wout_b], writes=[ps_b[4 + n]])
                        for n in range(2):
                            op("act", lambda e: e.activation(out=sig[q][:, n * 512:(n + 1) * 512], in_=ps[gb0 + n][:, :],
                                                             func=AF.Sigmoid), reads=[ps_b[gb0 + n]], writes=[sig_b[q]])
                        for n in range(2):
                            if scale_fn is None:
                                op("dve", lambda e: e.tensor_tensor(out=tt[q][:, n * 512:(n + 1) * 512], in0=ps[4 + n][:, :],
                                                                    in1=sig[q][:, n * 512:(n + 1) * 512], op=ALU.mult),
                                   reads=[ps_b[4 + n], sig_b[q]], writes=[tt_b[q]])
                            else:
                                sc_ap, sc_bufs = scale_fn(i)
                                op("dve", lambda e: e.scalar_tensor_tensor(out=tt[q][:, n * 512:(n + 1) * 512], in0=ps[4 + n][:, :],
                                                                           scalar=sc_ap, in1=sig[q][:, n * 512:(n + 1) * 512],
                                                                           op0=ALU.mult, op1=ALU.mult),
                                   reads=[ps_b[4 + n], sig_b[q]] + sc_bufs, writes=[tt_b[q]])
                        dma("sp", T_d[bname][i * 128:(i + 1) * 128, :], tt[q][:], reads=[tt_b[q]], writes=[T_buf[bname][i]])
                kb.barrier()

            if "sb" in branches:
                with ExitStack() as bes:
                    def bsb(name, shape, dt=F32):
                        return bes.enter_context(nc.sbuf_tensor("sb_%s_l%d" % (name, l), shape, dt))
                    yg_all = bsb("yg_all", [128, NT, 512], BF16)
                    yg_b = [Buf() for _ in range(NT)]
                    if dbg:
                        for i in range(NT):
                            op("pool", lambda e: e.memset(yg_all[:, i, :], 0.0), writes=[yg_b[i]])
                    with ExitStack() as aes:
                        def asb(name, shape, dt=F32):
                            return aes.enter_context(nc.sbuf_tensor("sba_%s_l%d" % (name, l), shape, dt))
                        wp = [asb("wp%d" % q, [128, 8, 512], BF16) for q in range(1)] * 2
                        wp_b = [Buf()] * 2
                        qz2 = asb("qz2", [128, NT, 2, 128], BF16)
                        qT_b = Buf()
                        op("pool", lambda e: e.memset(qz2[64:128, :, 0, :], 0.0), writes=[qT_b])
                        op("pool", lambda e: e.memset(qz2[0:64, :, 1, :], 0.0), writes=[qT_b])
                        kT = asb("kT", [128, S], BF16)
                        kT_b = Buf()
                        v_tm = asb("v_tm", [128, NT, 128], BF16)
                        v_b = Buf()
                        STR = []
                        NS = 4
                        for s_ in range(NS):
                            d_ = dict(sp_all=asb("sp_all%d" % s_, [128, NT, 2, 128], BF16), sp_b=[Buf() for _ in range(NT // 2)],
                                      etmp=[asb("etmp%d_%d" % (s_, q), [128, 512]) for q in range(1)] * 2, etmp_b=[Buf()] * 2,
                                      att=[asb("att%d_%d" % (s_, q), [128, 2, 2, 128], BF16) for q in range(1)] * 2, att_b=[Buf()] * 2,
                                      csA=asb("csA%d" % s_, [128, 32, 2]), csB=asb("csB%d" % s_, [128, 32, 2]), csA_b=Buf(), csB_b=Buf(),
                                      er=asb("er%d" % s_, [128, 16, 2]), er_b=Buf(),
                                      yacc=asb("yacc%d" % s_, [128, 2, 64]), yacc_b=[Buf(), Buf()],
                                      sg=asb("sg%d" % s_, [128, 128]), sg_b=Buf(), base=2 * s_)
                            op("dve", lambda e: e.memset(d_["csA"][:], 0.0), writes=[d_["csA_b"]])
                            op("dve", lambda e: e.memset(d_["csB"][:], 0.0), writes=[d_["csB_b"]])
                            STR.append(d_)

                        def load_pair(hp):
                            q = hp % 2
                            for ci, c0 in enumerate((C_SBQ, C_SBK, C_SBV, C_SBG)):
                                load_w(wp[q][:, :, ci * 128:(ci + 1) * 128], w_in_cols(l, c0 + hp * 128, 128), wp_b[q])

                        def sb_row(i, st, hp, q):
                            sp_all, sp_b, etmp, etmp_b, att, att_b = st["sp_all"], st["sp_b"], st["etmp"], st["etmp_b"], st["att"], st["att_b"]
                            csA, csB, csA_b, csB_b, er, er_b = st["csA"], st["csB"], st["csA_b"], st["csB_b"], st["er"], st["er_b"]
                            yacc, yacc_b, sg, sg_b, base = st["yacc"], st["yacc_b"], st["sg"], st["sg_b"], st["base"]
                            bCS = base + 1
                            bP = base + 1
                            PO = 256
                            ng = i // 2 + 1
                            qi = qz2[:, i, :, :].rearrange("p h t -> p (h t)")
                            for g in range(ng):
                                bk = base
                                eb = g % 2
                                js = [j for j in (2 * g, 2 * g + 1) if j <= i]
                                nj = len(js)
                                for jj, j in enumerate(js):
                                    op("pe", lambda e: e.matmul(ps[bk][:, jj * 256:(jj + 1) * 256], kT[:, j * 128:(j + 1) * 128], qi,
                                                                start=True, stop=True),
                                       reads=[kT_b, qT_b], writes=[ps_b[bk]])
                                yield
                                w = nj * 256
                                op("act", lambda e: e.activation(out=etmp[eb][:, 0:w], in_=ps[bk][:, 0:w], func=AF.Exp),
                                   reads=[ps_b[bk]], writes=[etmp_b[eb]])
                                yield
                                op("act", lambda e: e.activation(
                                    out=sp_all[:, 2 * g:2 * g + nj, :, :].rearrange("p a b c -> p (a b c)"),
                                    in_=etmp[eb][:, 0:w], func=AF.Ln, bias=1.0),
                                   reads=[etmp_b[eb]], writes=[sp_b[g]])
                                yield
                                if i in js:
                                    op("dve", lambda e: e.tensor_tensor(out=sp_all[:, i, :, :], in0=sp_all[:, i, :, :],
                                                                        in1=mask_lt[:], op=ALU.mult),
                                       reads=[sp_b[g], mask_lt_b], writes=[sp_b[g]])
                                    yield
                                for jj, j in enumerate(js):
                                    for h in range(2):
                                        op("pe", lambda e: e.matmul(ps[bCS][:, j * 2 + h:j * 2 + h + 1], sp_all[:, j, h, :],
                                                                    ones_bf[:, 0:1], start=True, stop=True),
                                           reads=[sp_b[g], ones_b], writes=[ps_b[bCS]])
                                    yield
                            nn = (i + 1) * 2
                            op("dve", lambda e: e.tensor_copy(out=csA[:, 0:i + 1, :].rearrange("p a b -> p (a b)"),
                                                              in_=ps[bCS][:, 0:nn]), reads=[ps_b[bCS]], writes=[csA_b])
                            yield
                            src, dst, src_b, dst_b = csA, csB, csA_b, csB_b
                            for d_ in (1, 2, 4, 8):
                                op("dve", lambda e: e.tensor_tensor(out=dst[:, 0:16, :], in0=src[:, 0:16, :],
                                                                    in1=src[:, d_:16 + d_, :], op=ALU.add),
                                   reads=[src_b], writes=[dst_b])
                                src, dst, src_b, dst_b = dst, src, dst_b, src_b
                                yield
                            op("act", lambda e: e.activation(out=er[:, 0:16, :], in_=src[:, 1:17, :], func=AF.Exp, scale=-1.0),
                               reads=[src_b], writes=[er_b])
                            yield
                            first = [True, True]
                            for g in range(ng):
                                bk = base
                                a = g % 2
                                js = [j for j in (2 * g, 2 * g + 1) if j <= i]
                                nj = len(js)
                                for jj, j in enumerate(js):
                                    o = ps[bk][:, jj * 256:(jj + 1) * 256]
                                    op("pe", lambda e: e.matmul(o, kT[:, j * 128:(j + 1) * 128], qi, start=True, stop=False),
                                       reads=[kT_b, qT_b], writes=[ps_b[bk]])
                                    op("pe", lambda e: e.matmul(o, negtri[:], sp_all[:, j, :, :].rearrange("p a b -> p (a b)"),
                                                                start=False, stop=True),
                                       reads=[negtri_b, sp_b[g]], writes=[ps_b[bk]])
                                    yield
                                w = nj * 256
                                op("act", lambda e: e.activation(out=att[a][:, 0:nj, :, :].rearrange("p a b c -> p (a b c)"),
                                                                 in_=ps[bk][:, 0:w], func=AF.Exp),
                                   reads=[ps_b[bk]], writes=[att_b[a]])
                                yield
                                if i in js:
                                    jj_i = js.index(i)
                                    op("dve", lambda e: e.tensor_tensor(out=att[a][:, jj_i, :, :], in0=att[a][:, jj_i, :, :],
                                                                        in1=mask_lt[:], op=ALU.mult),
                                       reads=[att_b[a], mask_lt_b], writes=[att_b[a]])
                                    yield
                                for jj, j in enumerate(js):
                                    for h in range(2):
                                        op("pe", lambda e: e.matmul(ps[bP][:, PO + (jj * 2 + h) * 64:PO + (jj * 2 + h + 1) * 64],
                                                                    att[a][:, jj, h, :], v_tm[:, j, h * 64:(h + 1) * 64],
                                                                    start=True, stop=True),
                                           reads=[att_b[a], v_b], writes=[ps_b[bP]])
                                    yield
                                for jj, j in enumerate(js):
                                    for h in range(2):
                                        pin = ps[bP][:, PO + (jj * 2 + h) * 64:PO + (jj * 2 + h + 1) * 64]
                                        if first[h]:
                                            op("dve", lambda e: e.tensor_scalar(out=yacc[:, h, :], in0=pin, scalar1=er[:, j, h:h + 1],
                                                                                scalar2=None, op0=ALU.mult),
                                               reads=[ps_b[bP], er_b], writes=[yacc_b[h]])
                                            first[h] = False
                                        else:
                                            op("dve", lambda e: e.scalar_tensor_tensor(out=yacc[:, h, :], in0=pin, scalar=er[:, j, h:h + 1],
                                                                                       in1=yacc[:, h, :], op0=ALU.mult, op1=ALU.add),
                                               reads=[ps_b[bP], er_b, yacc_b[h]], writes=[yacc_b[h]])
                                        yield
                            for kc in range(8):
                                op("pe", lambda e: e.matmul(ps[bCS][:, 128:256], hT[:, kc, i * 128:(i + 1) * 128], wp[q][:, kc, 384:512],
                                                            start=(kc == 0), stop=(kc == 7)),
                                   reads=[hT_b, wp_b[q]], writes=[ps_b[bCS]])
                                if kc % 4 == 3:
                                    yield
                            op("act", lambda e: e.activation(out=sg[:], in_=ps[bCS][:, 128:256], func=AF.Silu),
                               reads=[ps_b[bCS]], writes=[sg_b])
                            yield
                            op("dve", lambda e: e.tensor_tensor(out=yg_all[:, i, hp * 128:(hp + 1) * 128],
                                                                in0=yacc[:].rearrange("p a b -> p (a b)"), in1=sg[:], op=ALU.mult),
                               reads=[yacc_b[0], yacc_b[1], sg_b], writes=[yg_b[i]])
                            yield

                        for hp in range(dbg.get("sb_pairs", 4)):
                            q = hp % 2
                            load_pair(hp)
                            for st in STR:
                                op("dve", lambda e: e.memset(st["csA"][:], 0.0), writes=[st["csA_b"]])
                            for ci in range(2):
                                for tg in range(4):
                                    bk = (ci * 4 + tg) % 4
                                    for kc in range(8):
                                        op("pe", lambda e: e.matmul(ps[bk][:, :], wp[q][:, kc, ci * 128:(ci + 1) * 128],
                                                                    hT[:, kc, tg * 512:(tg + 1) * 512],
                                                                    start=(kc == 0), stop=(kc == 7)),
                                           reads=[wp_b[q], hT_b], writes=[ps_b[bk]])
                                    if ci == 0:
                                        for h in range(2):
                                            op("act", lambda e: e.activation(out=qz2[h * 64:(h + 1) * 64, tg * 4:(tg + 1) * 4, h, :],
                                                                             in_=ps[bk][h * 64:(h + 1) * 64, :].rearrange("p (a t) -> p a t", a=4),
                                                                             func=AF.Copy, scale=0.125),
                                               reads=[ps_b[bk]], writes=[qT_b])
                                    else:
                                        op("act", lambda e: e.activation(out=kT[:, tg * 512:(tg + 1) * 512], in_=ps[bk][:, :],
                                                                         func=AF.Copy), reads=[ps_b[bk]], writes=[kT_b])
                            for i4 in range(4):
                                bk = 4 + i4 % 4
                                for ii in range(4):
                                    i = i4 * 4 + ii
                                    for kc in range(8):
                                        op("pe", lambda e: e.matmul(ps[bk][:, ii * 128:(ii + 1) * 128], hT[:, kc, i * 128:(i + 1) * 128],
                                                                    wp[q][:, kc, 256:384], start=(kc == 0), stop=(kc == 7)),
                                           reads=[wp_b[q], hT_b], writes=[ps_b[bk]])
                                op("dve", lambda e: e.tensor_copy(out=v_tm[:, i4 * 4:(i4 + 1) * 4, :],
                                                                  in_=ps[bk][:, :].rearrange("p (a b) -> p a b", a=4)),
                                   reads=[ps_b[bk]], writes=[v_b])
                            nrows = dbg.get("sb_rows", NT)
                            for i0 in range(0, nrows, NS):
                                gens = [sb_row(i0 + s_, STR[s_], hp, q) for s_ in range(NS) if i0 + s_ < nrows]
                                while gens:
                                    for g_ in list(gens):
                                        try:
                                            next(g_)
                                        except StopIteration:
                                            gens.remove(g_)
                    kb.barrier()
                    ygT = [bsb("ygT%d" % q, [128, 4, 128], BF16) for q in range(2)]
                    ygT_b = [Buf(), Buf()]

                    def sb_lhsT(i):
                        q = i % 2
                        pb = ps[q].bitcast(BF16)
                        for c in range(4):
                            op("pe", lambda e: e.transpose(pb[:, c * 128:(c + 1) * 128], yg_all[:, i, c * 128:(c + 1) * 128], ident[:]),
                               reads=[yg_b[i], ident_b], writes=[ps_b[q]])
                        op("dve", lambda e: e.tensor_copy(out=ygT[q][:].rearrange("p a b -> p (a b)"), in_=pb[:, 0:512]),
                           reads=[ps_b[q]], writes=[ygT_b[q]])
                        return ygT[q], [ygT_b[q]]

                    epilogue("sb", 4, w_out_sb_d[l], C_GATE, sb_lhsT)


            if "ssd" in branches:
                with ExitStack() as bes:
                    def bsb(name, shape, dt=F32):
                        return bes.enter_context(nc.sbuf_tensor("ssd_%s_l%d" % (name, l), shape, dt))
                    pes = ExitStack()
                    bsb_outer = bsb

                    def bsb(name, shape, dt=F32):
                        return pes.enter_context(nc.sbuf_tensor("ssd_%s_l%d" % (name, l), shape, dt))
                    B_fm = bsb("B_fm", [128, S], BF16)
                    B_fm_b = Buf()
                    xs_tm = bsb("xs_tm", [128, NT, D], BF16)
                    xs_tm_b = [Buf() for _ in range(NT)]
                    Cz = [bsb("Cz%d" % g, [128, S], BF16) for g in range(2)]
                    Cz_b = Buf()
                    B_tm = bsb("B_tm", [128, NT, 128], BF16)
                    B_tm_b = Buf()
                    dt_tm = bsb("dt_tm", [128, NT, 16])
                    da_tm = bsb("da_tm", [128, NT, 16])
                    dt_b = Buf()
                    ah = bsb("ah", [128, 16])
                    ah_b = Buf()
                    op("act", lambda e: e.activation(out=ah[:], in_=rp_sb[:, RP_ALOG:RP_ALOG + 16], func=AF.Exp), reads=[rp_b], writes=[ah_b])
                    op("dve", lambda e: e.tensor_scalar(out=ah[:], in0=ah[:], scalar1=-1.0, scalar2=None, op0=ALU.mult), reads=[ah_b], writes=[ah_b])
                    op("pool", lambda e: e.memset(Cz[0][64:128, :], 0.0), writes=[Cz_b])
                    op("pool", lambda e: e.memset(Cz[1][0:64, :], 0.0), writes=[Cz_b])
                    with ExitStack() as aes:
                        def asb(name, shape, dt=F32):
                            return aes.enter_context(nc.sbuf_tensor("ssda_%s_l%d" % (name, l), shape, dt))
                        wc = [asb("wc%d" % q, [128, 8, 128], BF16) for q in range(2)]
                        wc_b = [Buf(), Buf()]
                        raw2 = [asb("raw%d" % q, [128, 3 + S]) for q in range(2)]
                        raw2_b = [Buf(), Buf()]
                        acc2 = [asb("acc%d" % q, [128, S]) for q in range(2)]
                        acc2_b = [Buf(), Buf()]
                        fm2 = [asb("fm%d" % q, [128, S], BF16) for q in range(1)] * 2
                        fm2_b = [Buf()] * 2
                        for q in range(2):
                            op("dve", lambda e: e.memset(raw2[q][:, 0:3], 0.0), writes=[raw2_b[q]])
                        load_w(wc[0][:], w_in_cols(l, C_XBC, 128), wc_b[0])

                        def conv_proj(cc):
                            q = cc % 2
                            raw, raw_b = raw2[q], raw2_b[q]
                            if cc + 1 < 10:
                                load_w(wc[1 - q][:], w_in_cols(l, C_XBC + (cc + 1) * 128, 128), wc_b[1 - q])
                            for tg in range(4):
                                bk = tg
                                for kc in range(8):
                                    op("pe", lambda e: e.matmul(ps[bk][:, :], wc[q][:, kc, :], hT[:, kc, tg * 512:(tg + 1) * 512],
                                                                start=(kc == 0), stop=(kc == 7)),
                                       reads=[wc_b[q], hT_b], writes=[ps_b[bk]])
                                op("act", lambda e: e.activation(out=raw[:, 3 + tg * 512:3 + (tg + 1) * 512], in_=ps[bk][:, :], func=AF.Copy),
                                   reads=[ps_b[bk]], writes=[raw_b])

                        def conv_post(cc):
                            q = cc % 2
                            raw, raw_b, acc, acc_b, fm, fm_b = raw2[q], raw2_b[q], acc2[q], acc2_b[q], fm2[q], fm2_b[q]
                            cw = lambda k: cp_sb[:, CP_CW + cc * 4 + k:CP_CW + cc * 4 + k + 1]
                            op("act", lambda e: e.activation(out=acc[:], in_=raw[:, 3:3 + S], func=AF.Identity, scale=cw(3),
                                                             bias=cp_sb[:, CP_CB + cc:CP_CB + cc + 1]),
                               reads=[raw_b, cp_b], writes=[acc_b])
                            for k in (2, 1, 0):
                                op("dve", lambda e: e.scalar_tensor_tensor(out=acc[:], in0=raw[:, k:k + S], scalar=cw(k), in1=acc[:],
                                                                           op0=ALU.mult, op1=ALU.add),
                                   reads=[raw_b, cp_b, acc_b], writes=[acc_b])
                            if cc < 8:
                                op("act", lambda e: e.activation(out=fm[:], in_=acc[:], func=AF.Silu), reads=[acc_b], writes=[fm_b])
                                src_t, src_tb = fm, fm_b
                            elif cc == 8:
                                op("act", lambda e: e.activation(out=B_fm[:], in_=acc[:], func=AF.Silu), reads=[acc_b], writes=[B_fm_b])
                                src_t, src_tb = B_fm, B_fm_b
                            else:
                                for g in range(2):
                                    op("act", lambda e: e.activation(out=Cz[g][g * 64:(g + 1) * 64, :], in_=acc[g * 64:(g + 1) * 64, :],
                                                                     func=AF.Silu), reads=[acc_b], writes=[Cz_b])
                                return
                            for i4 in range(2):
                                bk = 4 + i4
                                pb = ps[bk].bitcast(BF16)
                                for ii in range(8):
                                    i = i4 * 8 + ii
                                    op("pe", lambda e: e.transpose(pb[:, ii * 128:(ii + 1) * 128], src_t[:, i * 128:(i + 1) * 128], ident[:]),
                                       reads=[src_tb, ident_b], writes=[ps_b[bk]])
                                if cc < 8:
                                    op("dve", lambda e: e.tensor_copy(out=xs_tm[:, i4 * 8:(i4 + 1) * 8, cc * 128:(cc + 1) * 128],
                                                                      in_=pb.rearrange("p (a b) -> p a b", a=8)),
                                       reads=[ps_b[bk]], writes=xs_tm_b[i4 * 8:(i4 + 1) * 8])
                                else:
                                    op("dve", lambda e: e.tensor_copy(out=B_tm[:, i4 * 8:(i4 + 1) * 8, :],
                                                                      in_=pb.rearrange("p (a b) -> p a b", a=8)),
                                       reads=[ps_b[bk]], writes=[B_tm_b])

                        for step in range(11):
                            if step < 10:
                                conv_proj(step)
                            if step >= 1:
                                conv_post(step - 1)
                    kb.barrier()
                    with ExitStack() as aes:
                        def asb(name, shape, dt=F32):
                            return aes.enter_context(nc.sbuf_tensor("ssdb_%s_l%d" % (name, l), shape, dt))
                        wdt = asb("wdt", [128, 8, 16], BF16)
                        wdt_b = Buf()
                        wdt_f = asb("wdt_f", [128, 8, 16])
                        wdt_fb = Buf()
                        dma("sp", wdt_f[:], wdt_d[l], writes=[wdt_fb])
                        op("pool", lambda e: e.tensor_copy(out=wdt[:], in_=wdt_f[:]), reads=[wdt_fb], writes=[wdt_b])
                        wz = [asb("wz%d" % q, [128, 8, 512], BF16) for q in range(2)]
                        wz_b = [Buf(), Buf()]
                        for n in range(2):
                            load_w(wz[n][:], w_in_cols(l, C_Z + n * 512, 512), wz_b[n], fast=True)
                        dtmp = asb("dtmp", [128, 16])
                        dtmp_b = Buf()
                        szt = [asb("szt%d" % q, [128, 512], BF16) for q in range(2)]
                        szt_b = [Buf(), Buf()]
                        for i in range(NT):
                            for kc in range(8):
                                op("pe", lambda e: e.matmul(ps[0][:, 0:16], hT[:, kc, i * 128:(i + 1) * 128], wdt[:, kc, :],
                                                            start=(kc == 0), stop=(kc == 7)), reads=[hT_b, wdt_b], writes=[ps_b[0]])
                            op("dve", lambda e: e.tensor_tensor(out=dtmp[:], in0=ps[0][:, 0:16], in1=rp_sb[:, RP_DTB:RP_DTB + 16], op=ALU.add),
                               reads=[ps_b[0], rp_b], writes=[dtmp_b])
                            op("act", lambda e: e.activation(out=dtmp[:], in_=dtmp[:], func=AF.Exp), reads=[dtmp_b], writes=[dtmp_b])
                            op("act", lambda e: e.activation(out=dt_tm[:, i, :], in_=dtmp[:], func=AF.Ln, bias=1.0), reads=[dtmp_b], writes=[dt_b])
                            op("dve", lambda e: e.tensor_tensor(out=da_tm[:, i, :], in0=dt_tm[:, i, :], in1=ah[:], op=ALU.mult),
                               reads=[dt_b, ah_b], writes=[dt_b])
                        for n in range(2):
                            for i in range(NT):
                                q = i % 2
                                bk = 1 + q
                                for kc in range(8):
                                    op("pe", lambda e: e.matmul(ps[bk][:, :], hT[:, kc, i * 128:(i + 1) * 128], wz[n][:, kc, :],
                                                                start=(kc == 0), stop=(kc == 7)), reads=[hT_b, wz_b[n]], writes=[ps_b[bk]])
                                op("act", lambda e: e.activation(out=szt[q][:], in_=ps[bk][:, :], func=AF.Silu), reads=[ps_b[bk]], writes=[szt_b[q]])
                                dma("sp", sz_d[i * 128:(i + 1) * 128, n * 512:(n + 1) * 512], szt[q][:], reads=[szt_b[q]], writes=[sz_db[i]])
                    kb.barrier()
                    with ExitStack() as aes:
                        def asb(name, shape, dt=F32):
                            return aes.enter_context(nc.sbuf_tensor("ssdc_%s_l%d" % (name, l), shape, dt))
                        sz_t = [asb("sz_t%d" % q, [128, D], BF16) for q in range(2)]
                        sz_tb = [Buf(), Buf()]
                        x_dt = asb("x_dt", [128, 16, 64], BF16)
                        x_dt_b = Buf()
                        x_dts = asb("x_dts", [128, 16, 64], BF16)
                        x_dts_b = Buf()
                        eas = asb("eas", [128, 3, 16])
                        eas_b = Buf()
                        rseg = asb("rseg", [128, 16, 128])
                        rseg_b = Buf()
                        eseg = asb("eseg", [128, 16, 128], BF16)
                        eseg_b = Buf()
                        cbm = asb("cbm", [128, 2, 128], BF16)
                        cbm_b = Buf()
                        MT = asb("MT", [128, 16, 128], BF16)
                        MT_b = Buf()
                        ST = asb("ST", [128, 8, 64])
                        ST_b = Buf()
                        ST_bf = asb("ST_bf", [128, 512], BF16)
                        ST_bf_b = Buf()
                        t1 = asb("t1", [128, 16, 64])
                        t1_b = Buf()
                        t2 = asb("t2", [128, D])
                        t2_b = Buf()
                        t3, t3_b = t1, t1_b
                        ssq = asb("ssq", [128, 4])
                        ssq_b = Buf()
                        yn = [asb("yn%d" % q, [128, D], BF16) for q in range(2)]
                        yn_b = [Buf(), Buf()]
                        op("dve", lambda e: e.memset(ST[:], 0.0), writes=[ST_b])
                        op("dve", lambda e: e.memset(ST_bf[:], 0.0), writes=[ST_bf_b])

                        def bc_p(ap2, n0, n1):
                            return ap2.unsqueeze(2).broadcast_to([128, n0, n1])

                        for c in range(dbg.get("ssd_chunks", NT)):
                            q = c % 2
                            tok = slice(c * 128, (c + 1) * 128)
                            xs_c = xs_tm[:, c, :].rearrange("p (a b) -> p a b", a=16)
                            dma("sp", sz_t[q][:], sz_d[tok, :], reads=[sz_db[c]], writes=[sz_tb[q]])
                            da_c = da_tm[:, c, :]
                            for k3, (lh, lh_b) in enumerate(((triLE, triLE_b), (maskGT, maskGT_b), (ones_f, ones_f_b))):
                                op("pe", lambda e: e.matmul(ps[0][:, k3 * 16:(k3 + 1) * 16], lh[:], da_c, start=True, stop=True),
                                   reads=[lh_b, dt_b], writes=[ps_b[0]])
                            op("act", lambda e: e.activation(out=eas[:].rearrange("p a b -> p (a b)"), in_=ps[0][:, 0:48], func=AF.Exp),
                               reads=[ps_b[0]], writes=[eas_b])
                            for g in range(2):
                                op("pe", lambda e: e.matmul(ps[7][:, g * 128:(g + 1) * 128], B_fm[:, tok], Cz[g][:, tok], start=True, stop=True),
                                   reads=[B_fm_b, Cz_b], writes=[ps_b[7]])
                            op("dve", lambda e: e.tensor_tensor(out=cbm[:], in0=ps[7][:, 0:256].rearrange("p (a b) -> p a b", a=2),
                                                                in1=triLE[:].unsqueeze(1).broadcast_to([128, 2, 128]), op=ALU.mult),
                               reads=[ps_b[7], triLE_b], writes=[cbm_b])
                            op("pool", lambda e: e.tensor_tensor(out=rseg[:], in0=triLE[:].unsqueeze(1).broadcast_to([128, 16, 128]),
                                                                 in1=bc_p(da_c, 16, 128), op=ALU.mult),
                               reads=[triLE_b, dt_b], writes=[rseg_b])
                            for q4 in range(4):
                                bk = 1 + q4
                                op("pe", lambda e: e.matmul(ps[bk][:, :], maskGT[:], rseg[:, q4 * 4:(q4 + 1) * 4, :].rearrange("p a b -> p (a b)"),
                                                            start=True, stop=True), reads=[maskGT_b, rseg_b], writes=[ps_b[bk]])
                                op("act", lambda e: e.activation(out=eseg[:, q4 * 4:(q4 + 1) * 4, :].rearrange("p a b -> p (a b)"),
                                                                 in_=ps[bk][:, :], func=AF.Exp), reads=[ps_b[bk]], writes=[eseg_b])
                            for g in range(2):
                                op("dve", lambda e: e.tensor_tensor(out=MT[:, g * 8:(g + 1) * 8, :], in0=eseg[:, g * 8:(g + 1) * 8, :],
                                                                    in1=cbm[:, g, :].unsqueeze(1).broadcast_to([128, 8, 128]), op=ALU.mult),
                                   reads=[eseg_b, cbm_b], writes=[MT_b])
                            op("dve", lambda e: e.tensor_tensor(out=x_dt[:], in0=xs_c, in1=bc_p(dt_tm[:, c, :], 16, 64), op=ALU.mult),
                               reads=[xs_tm_b[c], dt_b], writes=[x_dt_b])
                            op("pool", lambda e: e.tensor_tensor(out=x_dts[:], in0=x_dt[:], in1=bc_p(eas[:, 1, :], 16, 64), op=ALU.mult),
                               reads=[x_dt_b, eas_b], writes=[x_dts_b])
                            for h in range(16):
                                bk = 5 + h // 8
                                op("pe", lambda e: e.matmul(ps[bk][:, (h % 8) * 64:(h % 8 + 1) * 64], MT[:, h, :], x_dt[:, h, :],
                                                            start=True, stop=True), reads=[MT_b, x_dt_b], writes=[ps_b[bk]])
                            for g in range(2):
                                op("pe", lambda e: e.matmul(ps[1 + g][:, :], Cz[g][:, tok], ST_bf[:], start=True, stop=True),
                                   reads=[Cz_b, ST_bf_b], writes=[ps_b[1 + g]])
                            for g in range(2):
                                op("pe", lambda e: e.matmul(ps[3 + g][:, :], B_tm[:, c, :],
                                                            x_dts[:, g * 8:(g + 1) * 8, :].rearrange("p a b -> p (a b)"),
                                                            start=True, stop=True), reads=[B_tm_b, x_dts_b], writes=[ps_b[3 + g]])
                            for g in range(2):
                                op("dve", lambda e: e.tensor_tensor(out=t1[:, g * 8:(g + 1) * 8, :],
                                                                    in0=ps[1 + g][:, :].rearrange("p (a b) -> p a b", a=8),
                                                                    in1=bc_p(eas[:, 0, g * 8:(g + 1) * 8], 8, 64), op=ALU.mult),
                                   reads=[ps_b[1 + g], eas_b], writes=[t1_b])
                            for g in range(2):
                                op("dve", lambda e: e.tensor_tensor(out=t2[:, g * 512:(g + 1) * 512], in0=ps[5 + g][:, :],
                                                                    in1=t1[:, g * 8:(g + 1) * 8, :].rearrange("p a b -> p (a b)"), op=ALU.add),
                                   reads=[ps_b[5 + g], t1_b], writes=[t2_b])
                            op("pool", lambda e: e.tensor_tensor(out=t3[:], in0=xs_c, in1=bc_p(rp_sb[:, RP_DSK:RP_DSK + 16], 16, 64), op=ALU.mult),
                               reads=[xs_tm_b[c], rp_b], writes=[t3_b])
                            op("pool", lambda e: e.tensor_tensor(out=t2[:], in0=t2[:], in1=t3[:].rearrange("p a b -> p (a b)"), op=ALU.add),
                               reads=[t2_b, t3_b], writes=[t2_b])
                            op("pool", lambda e: e.tensor_tensor(out=t2[:], in0=t2[:], in1=sz_t[q][:], op=ALU.mult),
                               reads=[t2_b, sz_tb[q]], writes=[t2_b])
                            for g in range(2):
                                hp_ = slice(g * 64, (g + 1) * 64)
                                op("dve", lambda e: e.tensor_tensor(out=ST[hp_, :, :], in0=ST[hp_, :, :],
                                                                    in1=eas[hp_, 2, g * 8:(g + 1) * 8].unsqueeze(2).broadcast_to([64, 8, 64]),
                                                                    op=ALU.mult), reads=[ST_b, eas_b], writes=[ST_b])
                                op("dve", lambda e: e.tensor_tensor(out=ST[hp_, :, :], in0=ps[3 + g][hp_, :].rearrange("p (a b) -> p a b", a=8),
                                                                    in1=ST[hp_, :, :], op=ALU.add), reads=[ps_b[3 + g], ST_b], writes=[ST_b])
                            op("act", lambda e: e.activation(out=ST_bf[:], in_=ST[:].rearrange("p a b -> p (a b)"), func=AF.Copy),
                               reads=[ST_b], writes=[ST_bf_b])
                            rms_rstd(t2[:], t2_b, x_dts[:].rearrange("p a b -> p (a b)"), x_dts_b, ssq, ssq_b, D, RMS_EPS)
                            op("dve", lambda e: e.scalar_tensor_tensor(out=yn[q][:], in0=t2[:], scalar=ssq[:, 2:3],
                                                                       in1=rp_sb[:, RP_SG:RP_SG + D], op0=ALU.mult, op1=ALU.mult),
                               reads=[t2_b, ssq_b, rp_b], writes=[yn_b[q]])
                            dma("sp", yn_d[tok, :], yn[q][:], reads=[yn_b[q]], writes=[yn_db[c]])
                    kb.barrier()
                    pes.close()
                    bsb = bsb_outer
                    ynl = [bsb("ynl%d" % q, [128, D], BF16) for q in range(2)]
                    ynl_b = [Buf(), Buf()]
                    ynT = [bsb("ynT%d" % q, [128, 8, 128], BF16) for q in range(2)]
                    ynT_b = [Buf(), Buf()]

                    def ssd_lhsT(i):
                        q = i % 2
                        dma("sp", ynl[q][:], yn_d[i * 128:(i + 1) * 128, :], reads=[yn_db[i]], writes=[ynl_b[q]])
                        pb = ps[q].bitcast(BF16)
                        for c in range(8):
                            op("pe", lambda e: e.transpose(pb[:, c * 128:(c + 1) * 128], ynl[q][:, c * 128:(c + 1) * 128], ident[:]),
                               reads=[ynl_b[q], ident_b], writes=[ps_b[q]])
                        op("dve", lambda e: e.tensor_copy(out=ynT[q][:].rearrange("p a b -> p (a b)"), in_=pb[:, :]),
                           reads=[ps_b[q]], writes=[ynT_b[q]])
                        return ynT[q], [ynT_b[q]]

                    epilogue("ssd", 8, w_out_ssd_d[l], C_GATE + D, ssd_lhsT)

            if "rw" in branches:
                with ExitStack() as bes:
                    def bsb(name, shape, dt=F32):
                        return bes.enter_context(nc.sbuf_tensor("rw_%s_l%d" % (name, l), shape, dt))
                    C0 = 0.6065306597126334
                    QT = 256
                    NQ = S // QT
                    NC_ = QT // 64
                    lora_bf = bsb("lora_bf", [128, S], BF16)
                    lora_b = Buf()
                    carry = bsb("carry", [128, 8])
                    carry_b = [Buf() for _ in range(5)]
                    raw = bsb("raw", [128, QT + 1])
                    raw_b = Buf()
                    tmpm = bsb("tmpm", [128, QT])
                    tmpm_b = Buf()
                    pbank = [0]

                    def proj_mix(wt, wt_b, wcol, mu_col, stream, qt, out_ap, out_buf):
                        t0 = qt * QT
                        bk = pbank[0] % 2
                        pbank[0] += 1
                        for kc in range(8):
                            op("pe", lambda e: e.matmul(ps[bk][:, 0:QT], wt[:, kc, wcol:wcol + 128], hT[:, kc, t0:t0 + QT],
                                                        start=(kc == 0), stop=(kc == 7)), reads=[wt_b, hT_b], writes=[ps_b[bk]])
                            if kc % 2 == 1:
                                yield
                        if qt == 0:
                            op("dve", lambda e: e.memset(raw[:, 0:1], 0.0), writes=[raw_b])
                        else:
                            op("dve", lambda e: e.tensor_copy(out=raw[:, 0:1], in_=carry[:, stream:stream + 1]),
                               reads=[carry_b[stream]], writes=[raw_b])
                        yield
                        op("act", lambda e: e.activation(out=raw[:, 1:QT + 1], in_=ps[bk][:, 0:QT], func=AF.Copy), reads=[ps_b[bk]], writes=[raw_b])
                        yield
                        op("dve", lambda e: e.tensor_copy(out=carry[:, stream:stream + 1], in_=raw[:, QT:QT + 1]),
                           reads=[raw_b], writes=[carry_b[stream]])
                        op("dve", lambda e: e.tensor_tensor(out=tmpm[:], in0=raw[:, 0:QT], in1=raw[:, 1:QT + 1], op=ALU.subtract),
                           reads=[raw_b], writes=[tmpm_b])
                        yield
                        op("dve", lambda e: e.scalar_tensor_tensor(out=out_ap, in0=tmpm[:], scalar=cp_sb[:, mu_col:mu_col + 1],
                                                                   in1=raw[:, 1:QT + 1], op0=ALU.mult, op1=ALU.add),
                           reads=[tmpm_b, raw_b, cp_b], writes=[out_buf])
                        yield

                    def drain(g):
                        for _ in g:
                            pass

                    with ExitStack() as aes:
                        wl = aes.enter_context(nc.sbuf_tensor("rw_wl_l%d" % l, [128, 8, 128], BF16))
                        wl_b = Buf()
                        load_w(wl[:], w_in_cols(l, C_RW + 2048, 128), wl_b)
                        lmix = aes.enter_context(nc.sbuf_tensor("rw_lmix_l%d" % l, [128, QT], F32))
                        lmix_b = Buf()
                        for qt in range(NQ):
                            drain(proj_mix(wl, wl_b, 0, CP_MU + 16, 0, qt, lmix[:], lmix_b))
                            op("act", lambda e: e.activation(out=lora_bf[0:64, qt * QT:(qt + 1) * QT], in_=lmix[0:64, :], func=AF.Tanh),
                               reads=[lmix_b], writes=[lora_b])
                            op("act", lambda e: e.activation(out=lora_bf[64:128, qt * QT:(qt + 1) * QT], in_=lmix[64:128, :], func=AF.Copy),
                               reads=[lmix_b], writes=[lora_b])
                    kb.barrier()

                    pes = ExitStack()
                    bsb_outer = bsb

                    def bsb(name, shape, dt=F32):
                        return pes.enter_context(nc.sbuf_tensor("rw_%s_l%d" % (name, l), shape, dt))
                    rmask = bsb("rmask", [128, QT])
                    rmask_b = Buf()
                    op("pool", lambda e: e.memset(rmask[:], 1.0), writes=[rmask_b])
                    op("pool", lambda e: e.memset(rmask[:].rearrange("p (c t) -> p c t", t=64)[:, :, 0:1], 0.0), reads=[rmask_b], writes=[rmask_b])
                    m192 = bsb("m192", [128, 192])
                    m192_b = Buf()
                    mLT = bsb("mLT", [128, 128])
                    mLT_b = Buf()
                    bones = bsb("bones", [128, 128], BF16)
                    bones_b = Buf()
                    op("pool", lambda e: e.memset(m192[:], 0.0), writes=[m192_b])
                    op("pool", lambda e: e.memset(mLT[:], 0.0), writes=[mLT_b])
                    op("pool", lambda e: e.memset(bones[:], 0.0), writes=[bones_b])
                    for h in range(2):
                        hs_ = slice(h * 64, (h + 1) * 64)
                        mk_mask(m192[hs_, h * 64:(h + 1) * 64], m192_b, 1.0, 0.0, 0, -1, 1, 64, ALU.is_gt)
                        mk_mask(m192[hs_, 128:192], m192_b, 1.0, 0.0, 0, -1, 1, 64, ALU.is_ge)
                        mk_mask(mLT[hs_, h * 64:(h + 1) * 64], mLT_b, 1.0, 0.0, 0, 1, -1, 64, ALU.is_gt)
                        op("pool", lambda e: e.memset(bones[hs_, h * 64:(h + 1) * 64], 1.0), reads=[bones_b], writes=[bones_b])
                    S_st = bsb("S_st", [128, 128])
                    S_st_b = Buf()
                    S_bf = bsb("S_bf", [128, 128], BF16)
                    S_bf_b = Buf()
                    omka = bsb("omka", [128, 4])
                    omka_b = Buf()
                    op("dve", lambda e: e.tensor_scalar(out=omka[:], in0=cp_sb[:, CP_KA:CP_KA + 4], scalar1=-1.0, scalar2=1.0,
                                                        op0=ALU.mult, op1=ALU.add), reads=[cp_b], writes=[omka_b])
                    wr = bsb("wr", [128, 8, 512], BF16)
                    wr_b = Buf()
                    wst = bsb("wst", [128, 128])
                    wst_b = Buf()
                    WU = bsb("WU", [128, 128], BF16)
                    AU = bsb("AU", [128, 128], BF16)
                    WU_b = Buf()
                    op("pool", lambda e: e.memset(WU[64:128, :], 0.0), writes=[WU_b])
                    op("pool", lambda e: e.memset(AU[0:64, :], 0.0), writes=[WU_b])
                    NPB = 3
                    PQ = []
                    for k_ in range(NPB):
                        d_ = dict(AR=bsb("AR%d" % k_, [128, NC_, 192], BF16), bbd=bsb("bbd%d" % k_, [128, NC_, 128], BF16),
                                  kbd=bsb("kbd%d" % k_, [128, NC_, 128], BF16), vbd=bsb("vbd%d" % k_, [128, NC_, 128], BF16),
                                  v=bsb("v_bf%d" % k_, [128, QT], BF16), g=bsb("g_bf%d" % k_, [128, QT], BF16),
                                  rk=bsb("rk_bf%d" % k_, [128, QT], BF16), gC=bsb("gC%d" % k_, [128, NC_]), b=Buf())
                        for nm in ("AR", "bbd", "kbd", "vbd"):
                            op("pool", lambda e: e.memset(d_[nm][:], 0.0), writes=[d_["b"]])
                        PQ.append(d_)
                    Fn, Fb = {}, {}
                    for nm in ("r", "k", "sig", "a", "kk", "L", "t1", "t2"):
                        Fn[nm] = bsb("F_" + nm, [128, QT])
                        Fb[nm] = Buf()
                    sq_bf = bsb("sq_bf", [128, QT], BF16)
                    sq_b = Buf()
                    TQ = []
                    for k_ in range(2):
                        TQ.append(dict(AB=bsb("AB%d" % k_, [128, NC_, 192], BF16), AK=bsb("AK%d" % k_, [128, NC_, 192], BF16),
                                       TR=bsb("TR%d" % k_, [128, NC_, 3, 128], BF16), Tinv=bsb("Tinv%d" % k_, [128, NC_, 128], BF16),
                                       cb=[Buf() for _ in range(NC_)], tb=Buf()))
                    ABT = bsb("ABT", [128, NC_, 128], BF16)
                    ABT_b = Buf()
                    An = [bsb("An%d" % q_, [128, 4, 2, 128], BF16) for q_ in range(2)]
                    An_b = [Buf(), Buf()]
                    Pn = [bsb("Pn%d" % q_, [128, 4, 128], BF16) for q_ in range(2)]
                    Pn_b = [Buf(), Buf()]
                    WT = bsb("WT", [128, 128], BF16)
                    WT_b = Buf()
                    UT = bsb("UT", [128, 128], BF16)
                    UT_b = Buf()
                    stmp = bsb("stmp", [128, 128])
                    stmp_b = Buf()
                    y_fm = bsb("y_fm", [128, QT])
                    y_fm_b = Buf()
                    yc = bsb("yc", [128, QT])
                    yc_b = Buf()
                    yb = bsb("yb", [128, QT], BF16)
                    yb_b = Buf()
                    f1 = bsb("f1", [128, QT])
                    f1_b = Buf()
                    yo = [bsb("yo%d" % q_, [128, QT], BF16) for q_ in range(2)]
                    yo_b = [Buf(), Buf()]

                    def cview(ap2):
                        return ap2.rearrange("p (c t) -> p c t", t=64)

                    units = [(hp, qt) for hp in range(dbg.get("rw_pairs", 4)) for qt in range(dbg.get("rw_quarters", NQ))]

                    def stage_P(u):
                        hp, qt = units[u]
                        pq = PQ[u % NPB]
                        pb_ = pq["b"]
                        t0 = qt * QT
                        cpc = lambda base: cp_sb[:, base + hp:base + hp + 1]
                        if qt == 0:
                            for ci in range(4):
                                load_w(wr[:, :, ci * 128:(ci + 1) * 128], w_in_cols(l, C_RW + ci * 512 + hp * 128, 128), wr_b)
                            dma("sp", wst[0:64, :], w_up_d[l][:, hp * 128:(hp + 1) * 128], writes=[wst_b])
                            dma("sp", wst[64:128, :], a_up_d[l][:, hp * 128:(hp + 1) * 128], writes=[wst_b])
                            op("pool", lambda e: e.tensor_copy(out=WU[0:64, :], in_=wst[0:64, :]), reads=[wst_b], writes=[WU_b])
                            op("pool", lambda e: e.tensor_copy(out=AU[64:128, :], in_=wst[64:128, :]), reads=[wst_b], writes=[WU_b])
                            yield
                        yield from proj_mix(wr, wr_b, 0, CP_MU + 0 + hp, 1, qt, Fn["r"][:], Fb["r"])
                        yield from proj_mix(wr, wr_b, 128, CP_MU + 4 + hp, 2, qt, Fn["k"][:], Fb["k"])
                        yield from proj_mix(wr, wr_b, 256, CP_MU + 8 + hp, 3, qt, pq["v"][:], pb_)
                        yield from proj_mix(wr, wr_b, 384, CP_MU + 12 + hp, 4, qt, Fn["t1"][:], Fb["t1"])
                        op("act", lambda e: e.activation(out=pq["g"][:], in_=Fn["t1"][:], func=AF.Silu), reads=[Fb["t1"]], writes=[pb_])
                        yield
                        op("pe", lambda e: e.matmul(ps[2][:, 0:QT], WU[:], lora_bf[:, t0:t0 + QT], start=True, stop=True),
                           reads=[WU_b, lora_b], writes=[ps_b[2]])
                        op("pe", lambda e: e.matmul(ps[2][:, QT:2 * QT], AU[:], lora_bf[:, t0:t0 + QT], start=True, stop=True),
                           reads=[WU_b, lora_b], writes=[ps_b[2]])
                        yield
                        op("act", lambda e: e.activation(out=Fn["sig"][:], in_=ps[2][:, 0:QT], func=AF.Sigmoid, bias=cpc(CP_W0)),
                           reads=[ps_b[2], cp_b], writes=[Fb["sig"]])
                        op("act", lambda e: e.activation(out=Fn["a"][:], in_=ps[2][:, QT:2 * QT], func=AF.Sigmoid, bias=cpc(CP_A0)),
                           reads=[ps_b[2], cp_b], writes=[Fb["a"]])
                        yield
                        op("dve", lambda e: e.tensor_scalar(out=Fn["kk"][:], in0=Fn["k"][:], scalar1=cpc(CP_KK), scalar2=None, op0=ALU.mult),
                           reads=[Fb["k"], cp_b], writes=[Fb["kk"]])
                        yield
                        op("act", lambda e: e.activation(out=sq_bf[:], in_=Fn["kk"][:], func=AF.Square), reads=[Fb["kk"]], writes=[sq_b])
                        yield
                        op("pe", lambda e: e.matmul(ps[2][:, 0:QT], bones[:], sq_bf[:], start=True, stop=True),
                           reads=[bones_b, sq_b], writes=[ps_b[2]])
                        yield
                        op("act", lambda e: e.activation(out=Fn["t1"][:], in_=ps[2][:, 0:QT], func=AF.Sqrt), reads=[ps_b[2]], writes=[Fb["t1"]])
                        yield
                        op("dve", lambda e: e.tensor_scalar(out=Fn["t1"][:], in0=Fn["t1"][:], scalar1=1e-12, scalar2=None, op0=ALU.max),
                           reads=[Fb["t1"]], writes=[Fb["t1"]])
                        yield
                        op("dve", lambda e: e.reciprocal(out=Fn["t1"][:], in_=Fn["t1"][:]), reads=[Fb["t1"]], writes=[Fb["t1"]])
                        yield
                        op("dve", lambda e: e.tensor_tensor(out=Fn["kk"][:], in0=Fn["kk"][:], in1=Fn["t1"][:], op=ALU.mult),
                           reads=[Fb["kk"], Fb["t1"]], writes=[Fb["kk"]])
                        yield
                        op("pool", lambda e: e.tensor_scalar(out=Fn["t1"][:], in0=Fn["a"][:], scalar1=cpc(CP_KA), scalar2=omka[:, hp:hp + 1],
                                                             op0=ALU.mult, op1=ALU.add), reads=[Fb["a"], cp_b, omka_b], writes=[Fb["t1"]])
                        yield
                        op("pool", lambda e: e.tensor_tensor(out=Fn["k"][:], in0=Fn["k"][:], in1=Fn["t1"][:], op=ALU.mult),
                           reads=[Fb["k"], Fb["t1"]], writes=[Fb["k"]])
                        yield
                        op("dve", lambda e: e.scalar_tensor_tensor(out=pq["rk"][:], in0=Fn["r"][:], scalar=cpc(CP_RK), in1=Fn["k"][:],
                                                                   op0=ALU.mult, op1=ALU.mult),
                           reads=[Fb["r"], Fb["k"], cp_b], writes=[pb_])
                        yield
                        op("dve", lambda e: e.tensor_tensor_scan(out=Fn["L"][:], data0=rmask[:], data1=Fn["sig"][:], initial=0.0,
                                                                 op0=ALU.mult, op1=ALU.add),
                           reads=[rmask_b, Fb["sig"]], writes=[Fb["L"]])
                        yield
                        op("act", lambda e: e.activation(out=Fn["t1"][:], in_=Fn["L"][:], func=AF.Exp, scale=-C0), reads=[Fb["L"]], writes=[Fb["t1"]])
                        yield
                        op("pool", lambda e: e.tensor_copy(out=pq["gC"][:], in_=cview(Fn["t1"][:])[:, :, 63]), reads=[Fb["t1"]], writes=[pb_])
                        op("dve", lambda e: e.tensor_tensor(out=pq["AR"][:, :, 128:192], in0=cview(Fn["r"][:]), in1=cview(Fn["t1"][:]), op=ALU.mult),
                           reads=[Fb["r"], Fb["t1"]], writes=[pb_])
                        yield
                        op("pool", lambda e: e.tensor_tensor(out=Fn["t2"][:], in0=Fn["L"][:], in1=Fn["sig"][:], op=ALU.subtract),
                           reads=[Fb["L"], Fb["sig"]], writes=[Fb["t2"]])
                        yield
                        op("act", lambda e: e.activation(out=Fn["t2"][:], in_=Fn["t2"][:], func=AF.Exp, scale=-C0), reads=[Fb["t2"]], writes=[Fb["t2"]])
                        yield
                        for h in range(2):
                            hs_ = slice(h * 64, (h + 1) * 64)
                            op("dve", lambda e: e.scalar_tensor_tensor(out=pq["AR"][hs_, :, h * 64:(h + 1) * 64], in0=cview(Fn["kk"][:])[hs_],
                                                                       scalar=-1.0, in1=cview(Fn["t2"][:])[hs_], op0=ALU.mult, op1=ALU.mult),
                               reads=[Fb["kk"], Fb["t2"]], writes=[pb_])
                            yield
                        op("act", lambda e: e.activation(out=Fn["t1"][:], in_=Fn["L"][:], func=AF.Exp, scale=C0), reads=[Fb["L"]], writes=[Fb["t1"]])
                        yield
                        op("pool", lambda e: e.tensor_tensor(out=Fn["t2"][:], in0=Fn["kk"][:], in1=Fn["a"][:], op=ALU.mult),
                           reads=[Fb["kk"], Fb["a"]], writes=[Fb["t2"]])
                        yield
                        for h in range(2):
                            hs_ = slice(h * 64, (h + 1) * 64)
                            op("dve", lambda e: e.tensor_tensor(out=pq["bbd"][hs_, :, h * 64:(h + 1) * 64], in0=cview(Fn["t2"][:])[hs_],
                                                                in1=cview(Fn["t1"][:])[hs_], op=ALU.mult),
                               reads=[Fb["t2"], Fb["t1"]], writes=[pb_])
                            yield
                            op("dve", lambda e: e.tensor_tensor(out=pq["kbd"][hs_, :, h * 64:(h + 1) * 64], in0=cview(Fn["k"][:])[hs_],
                                                                in1=cview(Fn["t1"][:])[hs_], op=ALU.mult),
                               reads=[Fb["k"], Fb["t1"]], writes=[pb_])
                            yield
                            op("pool", lambda e: e.tensor_copy(out=pq["vbd"][hs_, :, h * 64:(h + 1) * 64], in_=cview(pq["v"][:])[hs_]),
                               reads=[pb_], writes=[pb_])
                            yield

                    def stage_T(u):
                        pq = PQ[u % NPB]
                        pb_ = pq["b"]
                        tq = TQ[u % 2]
                        AB, AK, TR, Tinv = tq["AB"], tq["AK"], tq["TR"], tq["Tinv"]
                        cd_b = tq["cb"]
                        AR, bbd, kbd, vbd = pq["AR"], pq["bbd"], pq["kbd"], pq["vbd"]
                        for c in range(NC_):
                            bk = 3 + c % 2
                            op("pe", lambda e: e.matmul(ps[bk][:, 0:192], bbd[:, c, :], AR[:, c, :], start=True, stop=True),
                               reads=[pb_], writes=[ps_b[bk]])
                            op("pe", lambda e: e.matmul(ps[bk][:, 192:384], kbd[:, c, :], AR[:, c, :], start=True, stop=True),
                               reads=[pb_], writes=[ps_b[bk]])
                            op("pe", lambda e: e.matmul(ps[bk][:, 384:512], AR[:, c, 0:128], bbd[:, c, :], start=True, stop=True),
                               reads=[pb_], writes=[ps_b[bk]])
                            yield
                            op("dve", lambda e: e.tensor_tensor(out=AB[:, c, :], in0=ps[bk][:, 0:192], in1=m192[:], op=ALU.mult),
                               reads=[ps_b[bk], m192_b], writes=[cd_b[c]])
                            yield
                            op("dve", lambda e: e.tensor_tensor(out=AK[:, c, :], in0=ps[bk][:, 192:384], in1=m192[:], op=ALU.mult),
                               reads=[ps_b[bk], m192_b], writes=[cd_b[c]])
                            yield
                            op("dve", lambda e: e.tensor_tensor(out=ABT[:, c, :], in0=ps[bk][:, 384:512], in1=mLT[:], op=ALU.mult),
                               reads=[ps_b[bk], mLT_b], writes=[ABT_b])
                            yield
                            pb = ps[5].bitcast(BF16)
                            for k3, src_t in enumerate((vbd, bbd, kbd)):
                                op("pe", lambda e: e.transpose(pb[:, k3 * 128:(k3 + 1) * 128], src_t[:, c, :], ident[:]),
                                   reads=[pb_, ident_b], writes=[ps_b[5]])
                            yield
                            op("act", lambda e: e.activation(out=TR[:, c, :, :].rearrange("p a b -> p (a b)"), in_=pb[:, 0:384], func=AF.Copy),
                               reads=[ps_b[5]], writes=[cd_b[c]])
                            yield
                        cs_ = list(range(NC_))
                        cur_p = 0
                        for i_, c in enumerate(cs_):
                            op("dve", lambda e: e.tensor_tensor(out=Pn[0][:, i_, :], in0=AB[:, c, 0:128], in1=ident[:], op=ALU.add),
                               reads=[cd_b[c], ident_b], writes=[Pn_b[0]])
                            yield
                        getA = lambda i_, c: AB[:, c, 0:128]
                        getAT = lambda i_, c: ABT[:, c, :]
                        a_bufs = list(cd_b) + [ABT_b]
                        for lvl in range(1, 6):
                            an = An[lvl % 2]
                            an_b = An_b[lvl % 2]
                            for i_, c in enumerate(cs_):
                                bk = 3 + (i_ // 2)
                                off = (i_ % 2) * 256
                                if lvl < 5:
                                    op("pe", lambda e: e.matmul(ps[bk][:, off:off + 128], getAT(i_, c), getA(i_, c), start=True, stop=True),
                                       reads=a_bufs, writes=[ps_b[bk]])
                                op("pe", lambda e: e.matmul(ps[bk][:, off + 128:off + 256], getA(i_, c), getAT(i_, c), start=True, stop=True),
                                   reads=a_bufs, writes=[ps_b[bk]])
                                yield
                            for half in range(2):
                                bk = 3 + half
                                if lvl < 5:
                                    op("act", lambda e: e.activation(out=an[:, half * 2:half * 2 + 2, :, :].rearrange("p a b c -> p (a b c)"),
                                                                     in_=ps[bk][:, :], func=AF.Copy), reads=[ps_b[bk]], writes=[an_b])
                                else:
                                    op("act", lambda e: e.activation(out=an[:, half * 2:half * 2 + 2, 1, :],
                                                                     in_=ps[bk][:, :].rearrange("p (a b c) -> p a b c", a=2, b=2)[:, :, 1, :],
                                                                     func=AF.Copy), reads=[ps_b[bk]], writes=[an_b])
                                yield
                            getA = lambda i_, c, an=an: an[:, i_, 0, :]
                            getAT = lambda i_, c, an=an: an[:, i_, 1, :]
                            a_bufs = [an_b]
                            pcur, pnew = Pn[cur_p], Pn[1 - cur_p]
                            pcur_b, pnew_b = Pn_b[cur_p], Pn_b[1 - cur_p]
                            for i_, c in enumerate(cs_):
                                op("pe", lambda e: e.matmul(ps[5][:, i_ * 128:(i_ + 1) * 128], getAT(i_, c), pcur[:, i_, :], start=True, stop=True),
                                   reads=[an_b, pcur_b], writes=[ps_b[5]])
                                if i_ % 2 == 1:
                                    yield
                            if lvl < 5:
                                op("dve", lambda e: e.tensor_tensor(out=pnew[:].rearrange("p a b -> p (a b)"), in0=ps[5][:, :],
                                                                    in1=pcur[:].rearrange("p a b -> p (a b)"), op=ALU.add),
                                   reads=[ps_b[5], pcur_b], writes=[pnew_b])
                            else:
                                op("dve", lambda e: e.tensor_tensor(out=Tinv[:].rearrange("p a b -> p (a b)"),
                                                                    in0=ps[5][:, :], in1=pcur[:].rearrange("p a b -> p (a b)"), op=ALU.add),
                                   reads=[ps_b[5], pcur_b], writes=[tq["tb"]])
                            yield
                            cur_p = 1 - cur_p

                    def stage_Q(u):
                        hp, qt = units[u]
                        pq = PQ[u % NPB]
                        pb_ = pq["b"]
                        tq = TQ[u % 2]
                        AB, AK, TR, Tinv = tq["AB"], tq["AK"], tq["TR"], tq["Tinv"]
                        cd_b = tq["cb"]
                        AR, gC = pq["AR"], pq["gC"]
                        t0 = qt * QT
                        cpc = lambda base: cp_sb[:, base + hp:base + hp + 1]
                        if qt == 0:
                            op("dve", lambda e: e.memset(S_st[:], 0.0), writes=[S_st_b])
                            op("dve", lambda e: e.memset(S_bf[:], 0.0), writes=[S_bf_b])
                            yield
                        for c in range(NC_):
                            op("pe", lambda e: e.matmul(ps[6][:, 0:128], AR[:, c, 0:128], S_bf[:], start=True, stop=False),
                               reads=[pb_, S_bf_b], writes=[ps_b[6]])
                            op("pe", lambda e: e.matmul(ps[6][:, 0:128], AK[:, c, 0:128], TR[:, c, 0, :], start=False, stop=True),
                               reads=[cd_b[c]], writes=[ps_b[6]])
                            yield
                            op("act", lambda e: e.activation(out=WT[:], in_=ps[6][:, 0:128], func=AF.Copy), reads=[ps_b[6]], writes=[WT_b])
                            yield
                            op("pe", lambda e: e.matmul(ps[6][:, 128:256], Tinv[:, c, :], WT[:], start=True, stop=True),
                               reads=[tq["tb"], WT_b], writes=[ps_b[6]])
                            yield
                            op("act", lambda e: e.activation(out=UT[:], in_=ps[6][:, 128:256], func=AF.Copy), reads=[ps_b[6]], writes=[UT_b])
                            yield
                            op("pe", lambda e: e.matmul(ps[6][:, 256:384], TR[:, c, 1, :], UT[:], start=True, stop=False),
                               reads=[cd_b[c], UT_b], writes=[ps_b[6]])
                            op("pe", lambda e: e.matmul(ps[6][:, 256:384], TR[:, c, 2, :], TR[:, c, 0, :], start=False, stop=True),
                               reads=[cd_b[c]], writes=[ps_b[6]])
                            yield
                            yo_ = ps[7][:, c * 64:(c + 1) * 64]
                            op("pe", lambda e: e.matmul(yo_, S_bf[:], AR[:, c, 128:192], start=True, stop=False),
                               reads=[S_bf_b, pb_], writes=[ps_b[7]])
                            op("pe", lambda e: e.matmul(yo_, UT[:], AB[:, c, 128:192], start=False, stop=False),
                               reads=[UT_b, cd_b[c]], writes=[ps_b[7]])
                            op("pe", lambda e: e.matmul(yo_, TR[:, c, 0, :], AK[:, c, 128:192], start=False, stop=True),
                               reads=[cd_b[c]], writes=[ps_b[7]])
                            yield
                            op("dve", lambda e: e.tensor_tensor(out=stmp[:], in0=ps[6][:, 256:384], in1=S_st[:], op=ALU.add),
                               reads=[ps_b[6], S_st_b], writes=[stmp_b])
                            yield
                            op("act", lambda e: e.activation(out=S_bf[:], in_=stmp[:], func=AF.Copy, scale=gC[:, c:c + 1]),
                               reads=[stmp_b, pb_], writes=[S_bf_b])
                            op("dve", lambda e: e.tensor_scalar(out=S_st[:], in0=stmp[:], scalar1=gC[:, c:c + 1], scalar2=None, op0=ALU.mult),
                               reads=[stmp_b, pb_], writes=[S_st_b])
                            yield
                        Y = ps[7][:, 0:QT]
                        M = ps[7][:, QT:2 * QT]
                        op("act", lambda e: e.activation(out=y_fm[:], in_=Y, func=AF.Copy), reads=[ps_b[7]], writes=[y_fm_b])
                        yield
                        op("act", lambda e: e.activation(out=yb[:], in_=y_fm[:], func=AF.Copy), reads=[y_fm_b], writes=[yb_b])
                        yield
                        op("pe", lambda e: e.matmul(M, bones[:], yb[:], start=True, stop=True), reads=[bones_b, yb_b], writes=[ps_b[7]])
                        yield
                        op("dve", lambda e: e.scalar_tensor_tensor(out=yc[:], in0=M, scalar=-1.0 / 64, in1=y_fm[:],
                                                                   op0=ALU.mult, op1=ALU.add), reads=[ps_b[7], y_fm_b], writes=[yc_b])
                        yield
                        op("act", lambda e: e.activation(out=yb[:], in_=yc[:], func=AF.Square), reads=[yc_b], writes=[yb_b])
                        yield
                        op("pe", lambda e: e.matmul(M, bones[:], yb[:], start=True, stop=True), reads=[bones_b, yb_b], writes=[ps_b[7]])
                        yield
                        op("act", lambda e: e.activation(out=f1[:], in_=M, func=AF.Sqrt, scale=1.0 / 64, bias=GN_EPS),
                           reads=[ps_b[7]], writes=[f1_b])
                        yield
                        op("pe", lambda e: e.matmul(M, bones[:], pq["rk"][:], start=True, stop=True), reads=[bones_b, pb_], writes=[ps_b[7]])
                        op("dve", lambda e: e.reciprocal(out=f1[:], in_=f1[:]), reads=[f1_b], writes=[f1_b])
                        yield
                        op("dve", lambda e: e.tensor_tensor(out=yc[:], in0=yc[:], in1=f1[:], op=ALU.mult), reads=[yc_b, f1_b], writes=[yc_b])
                        yield
                        op("dve", lambda e: e.tensor_scalar(out=yc[:], in0=yc[:], scalar1=cpc(CP_LG), scalar2=cpc(CP_LB), op0=ALU.mult, op1=ALU.add),
                           reads=[yc_b, cp_b], writes=[yc_b])
                        yield
                        op("dve", lambda e: e.tensor_tensor(out=f1[:], in0=M, in1=pq["v"][:], op=ALU.mult), reads=[ps_b[7], pb_], writes=[f1_b])
                        yield
                        op("pool", lambda e: e.tensor_tensor(out=yc[:], in0=yc[:], in1=f1[:], op=ALU.add), reads=[yc_b, f1_b], writes=[yc_b])
                        yield
                        q_ = u % 2
                        op("pool", lambda e: e.tensor_tensor(out=yo[q_][:], in0=yc[:], in1=pq["g"][:], op=ALU.mult), reads=[yc_b, pb_], writes=[yo_b[q_]])
                        yield
                        dma("sp", yrw_d[hp * 128:(hp + 1) * 128, t0:t0 + QT], yo[q_][:], reads=[yo_b[q_]], writes=[yrw_db[qt]])
                        yield

                    nu = len(units)
                    for tick in range(nu + 2):
                        gens = []
                        if tick - 2 >= 0:
                            gens.append(stage_Q(tick - 2))
                        if 0 <= tick - 1 < nu:
                            gens.append(stage_T(tick - 1))
                        if tick < nu:
                            gens.append(stage_P(tick))
                        while gens:
                            for g in list(gens):
                                try:
                                    next(g)
                                except StopIteration:
                                    gens.remove(g)
                    kb.barrier()
                    pes.close()
                    bsb = bsb_outer
                    rwl = [bsb("rwl%d" % q_, [128, 4, 128], BF16) for q_ in range(2)]
                    rwl_b = [Buf(), Buf()]

                    def rw_lhsT(i):
                        q_ = i % 2
                        dma("sp", rwl[q_][:], yrw_d.rearrange("(hp p) s -> p hp s", p=128)[:, :, i * 128:(i + 1) * 128],
                            reads=[yrw_db[(i * 128) // QT]], writes=[rwl_b[q_]])
                        return rwl[q_], [rwl_b[q_]]

                    epilogue("rw", 4, w_out_rw_d[l], C_GATE + 2 * D, rw_lhsT)
        kb.barrier()
        with ExitStack() as fes:
            def fsb(name, shape, dt=F32):
                return fes.enter_context(nc.sbuf_tensor("fin_%s_l%d" % (name, l), shape, dt))
            wo = fsb("wo", [128, 8, D], BF16)
            wo_b = Buf()
            for n in range(2):
                load_w(wo[:, :, n * 512:(n + 1) * 512], w_o_d[l].rearrange("(kc p) n -> p kc n", p=128)[:, :, n * 512:(n + 1) * 512], wo_b, fast=True)
            bl = [b for b in ("sb", "ssd", "rw") if b in branches]
            tin = {(b, q): fsb("tin_%s%d" % (b, q), [128, D]) for b in bl for q in range(2)}
            tin_b = {(b, q): Buf() for b in bl for q in range(2)}
            msum = [fsb("msum%d" % q, [128, D]) for q in range(2)]
            msum_b = [Buf(), Buf()]
            mbf = [fsb("mbf%d" % q, [128, D], BF16) for q in range(2)]
            mbf_b = [Buf(), Buf()]
            mT = [fsb("mT%d" % q, [128, 8, 128], BF16) for q in range(2)]
            mT_b = [Buf(), Buf()]
            for i in range(dbg.get("ep_tiles", NT)):
                q = i % 2
                for b in bl:
                    dma("sp", tin[(b, q)][:], T_d[b][i * 128:(i + 1) * 128, :], reads=[T_buf[b][i]], writes=[tin_b[(b, q)]])
                if len(bl) == 1:
                    op("pool", lambda e: e.tensor_copy(out=mbf[q][:], in_=tin[(bl[0], q)][:]), reads=[tin_b[(bl[0], q)]], writes=[mbf_b[q]])
                else:
                    op("pool", lambda e: e.tensor_tensor(out=msum[q][:], in0=tin[(bl[0], q)][:], in1=tin[(bl[1], q)][:], op=ALU.add),
                       reads=[tin_b[(bl[0], q)], tin_b[(bl[1], q)]], writes=[msum_b[q]])
                    if len(bl) == 3:
                        op("pool", lambda e: e.tensor_tensor(out=mbf[q][:], in0=msum[q][:], in1=tin[(bl[2], q)][:], op=ALU.add),
                           reads=[msum_b[q], tin_b[(bl[2], q)]], writes=[mbf_b[q]])
                    else:
                        op("pool", lambda e: e.tensor_copy(out=mbf[q][:], in_=msum[q][:]), reads=[msum_b[q]], writes=[mbf_b[q]])
                pb = ps[q].bitcast(BF16)
                for kc in range(8):
                    op("pe", lambda e: e.transpose(pb[:, kc * 128:(kc + 1) * 128], mbf[q][:, kc * 128:(kc + 1) * 128], ident[:]),
                       reads=[mbf_b[q], ident_b], writes=[ps_b[q]])
                op("act", lambda e: e.activation(out=mT[q][:].rearrange("p a b -> p (a b)"), in_=pb[:, :], func=AF.Identity),
                   reads=[ps_b[q]], writes=[mT_b[q]])
                for n in range(2):
                    bk = 2 + 2 * q + n
                    for kc in range(8):
                        op("pe", lambda e: e.matmul(ps[bk][:, :], mT[q][:, kc, :], wo[:, kc, n * 512:(n + 1) * 512],
                                                    start=(kc == 0), stop=(kc == 7)),
                           reads=[mT_b[q], wo_b], writes=[ps_b[bk]])
                    op("dve", lambda e: e.tensor_tensor(out=x_sb[:, i, n * 512:(n + 1) * 512], in0=ps[bk][:, :],
                                                        in1=x_sb[:, i, n * 512:(n + 1) * 512], op=ALU.add),
                       reads=[ps_b[bk], x_b[i]], writes=[x_b[i]])
        kb.barrier()

    with ExitStack() as oes:
        fg = oes.enter_context(nc.sbuf_tensor("fg_sb", [128, D], F32))
        fg_b = Buf()
        dma("sp", fg[:], fg_d[:, :], writes=[fg_b])
        junk = oes.enter_context(nc.sbuf_tensor("ojunk", [128, D], BF16))
        junk_b = Buf()
        yo = [oes.enter_context(nc.sbuf_tensor("yo%d" % q, [128, D], F32)) for q in range(2)]
        yo_b = [Buf(), Buf()]
        ssq = [oes.enter_context(nc.sbuf_tensor("oss%d" % q, [128, 4], F32)) for q in range(2)]
        ssq_b = [Buf(), Buf()]
        out_toks = []
        for i in range(NT):
            q = i % 2
            rms_rstd(x_sb[:, i, :], x_b[i], junk[:], junk_b, ssq[q], ssq_b[q], D, RMS_EPS)
            op("dve", lambda e: e.scalar_tensor_tensor(out=yo[q][:], in0=x_sb[:, i, :], scalar=ssq[q][:, 2:3], in1=fg[:],
                                                       op0=ALU.mult, op1=ALU.mult),
               reads=[x_b[i], ssq_b[q], fg_b], writes=[yo_b[q]])
            ob = Buf()
            dma("sp", y_d[i * 128:(i + 1) * 128, :], yo[q][:], reads=[yo_b[q]], writes=[ob])
        kb.barrier(engines=("sp",))
    kb.barrier()
    es.close()
    return nc, kb


def _prep_inputs(inp):
    f = lambda a: np.ascontiguousarray(np.asarray(a, dtype=np.float32))
    cp = np.zeros((L, 128, NCP), np.float32)
    rp = np.zeros((L, 128, NRP), np.float32)
    for l in range(L):
        cw = f(inp["conv_w"])[l]
        cp[l, :, CP_CW:CP_CW + 40] = cw.reshape(4, 10, 128).transpose(2, 1, 0).reshape(128, 40)
        cp[l, :, CP_CB:CP_CB + 10] = f(inp["conv_b"])[l].reshape(10, 128).T
        cp[l, :, CP_MU:CP_MU + 17] = f(inp["rw_mu"])[l].reshape(17, 128).T
        for off, nm in ((CP_W0, "rw_w0"), (CP_A0, "rw_a0"), (CP_KK, "rw_k_k"), (CP_KA, "rw_k_a"),
                        (CP_RK, "rw_r_k"), (CP_LG, "rw_ln_g"), (CP_LB, "rw_ln_b")):
            cp[l, :, off:off + 4] = f(inp[nm])[l].reshape(4, 128).T
        rp[l, :, RP_NG:RP_NG + D] = f(inp["norm_g"])[l][None, :]
        rp[l, :, RP_SG:RP_SG + D] = f(inp["ssd_norm_g"])[l][None, :]
        rp[l, :, RP_DTB:RP_DTB + 16] = f(inp["dt_bias"])[l][None, :]
        rp[l, :, RP_ALOG:RP_ALOG + 16] = f(inp["a_log"])[l][None, :]
        rp[l, :, RP_DSK:RP_DSK + 16] = f(inp["d_skip"])[l][None, :]
    fg = np.ascontiguousarray(np.broadcast_to(f(inp["final_g"])[None, :], (128, D)))
    shared = {
        "w_in": f(inp["w_in"]), "w_out_sb": f(inp["w_out_sb"]), "w_out_ssd": f(inp["w_out_ssd"]),
        "w_out_rw": f(inp["w_out_rw"]), "w_o": f(inp["w_o"]), "rw_w_up": f(inp["rw_w_up"]),
        "rw_a_up": f(inp["rw_a_up"]), "cp": cp, "rp": rp, "fg": fg,
        "w_dt": np.ascontiguousarray(f(inp["w_in"])[:, :, C_DT:C_DT + 16].reshape(L, 8, 128, 16).transpose(0, 2, 1, 3)),
    }
    x = f(inp["x"])
    return [dict(shared, x=x[b]) for b in range(x.shape[0])]


def kernel(**inputs):
    in_maps = _prep_inputs(inputs)
    nc, _ = build()
    res = run_bass_kernel_spmd(nc, in_maps, core_ids=list(range(8)))
    return np.stack([r["y"] for r in res.results], axis=0).astype(np.float32)
```

```python
from contextlib import ExitStack
import numpy as np
import concourse.bass as bass
import concourse.mybir as mybir
from concourse.bass_utils import run_bass_kernel_spmd

F32 = mybir.dt.float32
BF16 = mybir.dt.bfloat16
AF = mybir.ActivationFunctionType
ALU = mybir.AluOpType
AX = mybir.AxisListType

S = 2048
D = 1024
L = 2
NT = 16
N_IN = 9616
C_SBQ, C_SBK, C_SBV, C_SBG = 0, 512, 1024, 1536
C_Z, C_XBC, C_DT = 2048, 3072, 4352
C_RW = 4368
C_GATE = 6544
RMS_EPS = 1e-6
GN_EPS = 64e-5
CP_CW, CP_CB, CP_MU, CP_W0, CP_A0, CP_KK, CP_KA, CP_RK, CP_LG, CP_LB = 0, 40, 50, 67, 71, 75, 79, 83, 87, 91
NCP = 95
RP_NG, RP_SG, RP_DTB, RP_ALOG, RP_DSK = 0, 1024, 2048, 2064, 2080
NRP = 2096

SAME_ENGINE_SYNC = True
SAME_ENGINE_WAW = True
NDS = 24


class Buf:
    __slots__ = ("name", "w", "r", "excl")

    def __init__(self, name="", excl=False):
        self.name = name
        self.excl = excl
        self.w = None
        self.r = {}


class Eng:
    def __init__(self, name, e, sem, key):
        self.name, self.e, self.sem, self.key = name, e, sem, key
        self.cnt = 0
        self.clock = {}


class KB:
    def __init__(self, nc, es):
        self.nc = nc
        self.engs = {}
        self.sems = []
        for name, e in (("pe", nc.tensor), ("act", nc.scalar), ("dve", nc.vector),
                        ("pool", nc.gpsimd), ("sp", nc.sync)):
            sem = es.enter_context(nc.semaphore("s_" + name))
            self.engs[name] = Eng(name, e, sem, len(self.sems))
            self.sems.append(sem)
        self.dbase = len(self.sems)
        for i in range(NDS):
            self.sems.append(es.enter_context(nc.semaphore("d%d" % i)))
        self.dcnt = [0] * NDS
        self.dnext = 0
        self.ninst = 0

    def _wait_deps(self, E, reads, writes):
        need = {}

        def add(tok):
            if tok is None:
                return
            k = tok[0]
            if k == E.key and (E.name == "pe" or not SAME_ENGINE_SYNC):
                return
            cur = need.get(k)
            if cur is None or cur[1] < tok[1]:
                need[k] = tok

        for b in reads:
            add(b.w)
        for b in writes:
            if b.w is not None and (b.w[0] != E.key or SAME_ENGINE_WAW):
                add(b.w)
            for t in b.r.values():
                if t[0] != E.key or SAME_ENGINE_WAW:
                    add(t)
        new = None
        for k, tok in need.items():
            if E.clock.get(k, 0) < tok[1]:
                E.e.wait_ge(self.sems[k], tok[1])
                if new is None:
                    new = dict(E.clock)
                for kk, vv in tok[2].items():
                    if new.get(kk, 0) < vv:
                        new[kk] = vv
                if new.get(k, 0) < tok[1]:
                    new[k] = tok[1]
        if new is not None:
            E.clock = new

    def _mark(self, tok, reads, writes):
        for b in reads:
            b.r[tok[0]] = tok
        for b in writes:
            b.w = tok
            b.r = {}

    def op(self, en, fn, reads=(), writes=()):
        E = self.engs[en]
        if any(b.excl for b in reads):
            writes = list(writes) + [b for b in reads if b.excl and b not in writes]
            reads = [b for b in reads if not b.excl]
        self._wait_deps(E, reads, writes)
        inst = fn(E.e)
        E.cnt += 1
        inst.then_inc(E.sem, 1)
        self.ninst += 1
        tok = (E.key, E.cnt, E.clock)
        self._mark(tok, reads, writes)
        return tok

    def dma(self, q, out, in_, reads=(), writes=(), **kw):
        E = self.engs[q]
        self._wait_deps(E, reads, writes)
        j = self.dnext
        self.dnext = (j + 1) % NDS
        k = self.dbase + j
        prev = 16 * self.dcnt[j]
        if prev and E.clock.get(k, 0) < prev:
            E.e.wait_ge(self.sems[k], prev)
            new = dict(E.clock)
            new[k] = prev
            E.clock = new
        inst = E.e.dma_start(out=out, in_=in_, **kw)
        self.dcnt[j] += 1
        inst.then_inc(self.sems[k], 16)
        self.ninst += 1
        tok = (k, 16 * self.dcnt[j], E.clock)
        self._mark(tok, reads, writes)
        return tok

    def barrier(self, engines=("pe", "act", "dve", "pool", "sp")):
        for en in engines:
            E = self.engs[en]
            new = dict(E.clock)
            for F in self.engs.values():
                if F is E or F.cnt == 0:
                    continue
                if new.get(F.key, 0) < F.cnt:
                    E.e.wait_ge(F.sem, F.cnt)
                    new[F.key] = F.cnt
            for j in range(NDS):
                v = 16 * self.dcnt[j]
                k = self.dbase + j
                if v and new.get(k, 0) < v:
                    E.e.wait_ge(self.sems[k], v)
                    new[k] = v
            E.clock = new


def build(nlayers=L, branches=("sb", "ssd", "rw"), debug=False, dbg=None):
    dbg = dbg or {}
    nc = bass.Bass("TRN2", target_bir_lowering=False)
    es = ExitStack()
    kb = KB(nc, es)
    op, dma = kb.op, kb.dma

    def dram_in(name, shape):
        return nc.dram_tensor(name, shape, F32, kind="ExternalInput").ap()

    x_d = dram_in("x", [S, D])
    w_in_d = dram_in("w_in", [L, D, N_IN])
    w_out_sb_d = dram_in("w_out_sb", [L, 512, D])
    w_out_ssd_d = dram_in("w_out_ssd", [L, 1024, D])
    w_out_rw_d = dram_in("w_out_rw", [L, 512, D])
    w_o_d = dram_in("w_o", [L, D, D])
    w_up_d = dram_in("rw_w_up", [L, 64, 512])
    a_up_d = dram_in("rw_a_up", [L, 64, 512])
    cp_d = dram_in("cp", [L, 128, NCP])
    rp_d = dram_in("rp", [L, 128, NRP])
    fg_d = dram_in("fg", [128, D])
    wdt_d = dram_in("w_dt", [L, 128, 8, 16])
    y_d = nc.dram_tensor("y", [S, D], F32, kind="ExternalOutput").ap()
    tkind = "ExternalOutput" if debug else "Internal"
    T_d = {b: nc.dram_tensor("T_" + b, [S, D], F32, kind=tkind).ap() for b in ("sb", "ssd", "rw")}
    T_buf = {b: [Buf() for _ in range(NT)] for b in ("sb", "ssd", "rw")}

    def sb(name, shape, dt=F32):
        return es.enter_context(nc.sbuf_tensor(name, shape, dt))

    x_sb = sb("x_sb", [128, NT, D])
    x_b = [Buf("x%d" % i) for i in range(NT)]
    ident = sb("ident", [128, 128], BF16)
    ident_b = Buf("ident")
    ones_bf = sb("ones_bf", [128, 128], BF16)
    ones_b = Buf("ones")
    mask_lt = sb("mask_lt", [128, 2, 128], F32)
    mask_lt_b = Buf()
    negtri = sb("negtri", [128, 128], BF16)
    negtri_b = Buf()
    cp_sb = sb("cp_sb", [128, NCP])
    cp_b = Buf()
    rp_sb = sb("rp_sb", [128, NRP])
    rp_b = Buf()
    ps = [es.enter_context(nc.psum_tensor("ps%d" % i, [128, 512], F32)) for i in range(8)]
    ps_b = [Buf("ps%d" % i, excl=True) for i in range(8)]

    def mk_mask(t_ap, buf, val_true, val_false, base, cm, pat_step, n, cmp):
        op("pool", lambda e: e.memset(t_ap, val_true), writes=[buf])
        op("pool", lambda e: e.affine_select(out=t_ap, in_=t_ap, pattern=[[pat_step, n]], compare_op=cmp,
                                             fill=val_false, base=base, channel_multiplier=cm),
           reads=[buf], writes=[buf])

    mk_mask(ident[:], ident_b, 1.0, 0.0, 0, 1, -1, 128, ALU.is_equal)
    op("pool", lambda e: e.memset(ones_bf[:], 1.0), writes=[ones_b])
    for h in range(2):
        mk_mask(mask_lt[:, h, :], mask_lt_b, 1.0, 0.0, 0, -1, 1, 128, ALU.is_gt)
    mk_mask(negtri[:], negtri_b, -1.0, 0.0, 0, 1, -1, 128, ALU.is_ge)

    triLE = sb("triLE", [128, 128], F32)
    triLE_b = Buf()
    mk_mask(triLE[:], triLE_b, 1.0, 0.0, 0, -1, 1, 128, ALU.is_ge)
    maskGT = sb("maskGT", [128, 128], F32)
    maskGT_b = Buf()
    mk_mask(maskGT[:], maskGT_b, 1.0, 0.0, 0, 1, -1, 128, ALU.is_gt)
    ones_f = sb("ones_f", [128, 128], F32)
    ones_f_b = Buf()
    op("pool", lambda e: e.memset(ones_f[:], 1.0), writes=[ones_f_b])
    xs_d = nc.dram_tensor("xs_scr", [S, D], BF16, kind="Internal").ap()
    sz_d = nc.dram_tensor("sz_scr", [S, D], BF16, kind="Internal").ap()
    yn_d = nc.dram_tensor("yn_scr", [S, D], BF16, kind="Internal").ap()
    yrw_d = nc.dram_tensor("yrw_scr", [512, S], BF16, kind="Internal").ap()
    yrw_db = [Buf() for _ in range(8)]
    xs_db = [Buf() for _ in range(NT)]
    sz_db = [Buf() for _ in range(NT)]
    yn_db = [Buf() for _ in range(NT)]


    NSTG = 4
    stg = [sb("wstg%d" % q, [128, 512]) for q in range(NSTG)]
    stg_b = [Buf() for _ in range(NSTG)]
    stg_i = [0]

    def load_w(dst_ap, src_ap, buf, fast=False):
        nk, n = dst_ap.shape[1], dst_ap.shape[2]
        g = max(1, 512 // n)
        for k0 in range(0, nk, g):
            k1 = min(nk, k0 + g)
            q = stg_i[0] % NSTG
            stg_i[0] += 1
            sv = stg[q][:, 0:(k1 - k0) * n].rearrange("p (a b) -> p a b", a=k1 - k0)
            dma("sp", sv, src_ap[:, k0:k1, :], writes=[stg_b[q]])
            if not fast:
                op("pool", lambda e: e.tensor_copy(out=dst_ap[:, k0:k1, :], in_=sv), reads=[stg_b[q]], writes=[buf])
            elif q % 2 == 0:
                op("dve", lambda e: e.tensor_copy(out=dst_ap[:, k0:k1, :], in_=sv), reads=[stg_b[q]], writes=[buf])
            else:
                op("act", lambda e: e.activation(out=dst_ap[:, k0:k1, :], in_=sv, func=AF.Copy), reads=[stg_b[q]], writes=[buf])

    def w_in_cols(l, c0, n):
        return w_in_d[l].rearrange("(kc p) n -> p kc n", p=128)[:, :, c0:c0 + n]

    def rms_rstd(src_ap, src_buf, junk_ap, junk_buf, ss_ap, ss_buf, width, eps):
        op("act", lambda e: e.activation(out=junk_ap, in_=src_ap, func=AF.Square, accum_out=ss_ap[:, 0:1]),
           reads=[src_buf], writes=[junk_buf, ss_buf])
        op("act", lambda e: e.activation(out=ss_ap[:, 1:2], in_=ss_ap[:, 0:1], func=AF.Sqrt,
                                         scale=1.0 / width, bias=eps), reads=[ss_buf], writes=[ss_buf])
        op("dve", lambda e: e.reciprocal(out=ss_ap[:, 2:3], in_=ss_ap[:, 1:2]), reads=[ss_buf], writes=[ss_buf])

    for l in range(nlayers):
        dma("sp", rp_sb[:], rp_d[l], writes=[rp_b])
        dma("sp", cp_sb[:], cp_d[l], writes=[cp_b])
        if l == 0:
            for i in range(NT):
                dma("sp", x_sb[:, i, :], x_d[i * 128:(i + 1) * 128, :], writes=[x_b[i]])
        with ExitStack() as les:
            def lsb(name, shape, dt=F32, _les=les):
                return _les.enter_context(nc.sbuf_tensor("%s_l%d" % (name, l), shape, dt))

            hT = lsb("hT", [128, 8, S], BF16)
            hT_b = Buf("hT")
            with ExitStack() as pes:
                junk = pes.enter_context(nc.sbuf_tensor("p0junk_l%d" % l, [128, D], BF16))
                junk_b = Buf()
                xn = [pes.enter_context(nc.sbuf_tensor("p0xn%d_l%d" % (q, l), [128, D], BF16)) for q in range(2)]
                xn_b = [Buf(), Buf()]
                ssq = [pes.enter_context(nc.sbuf_tensor("p0ss%d_l%d" % (q, l), [128, 4], F32)) for q in range(2)]
                ssq_b = [Buf(), Buf()]
                for i in range(NT):
                    q = i % 2
                    rms_rstd(x_sb[:, i, :], x_b[i], junk[:], junk_b, ssq[q], ssq_b[q], D, RMS_EPS)
                    op("dve", lambda e: e.scalar_tensor_tensor(out=xn[q][:], in0=x_sb[:, i, :], scalar=ssq[q][:, 2:3],
                                                               in1=rp_sb[:, RP_NG:RP_NG + D], op0=ALU.mult, op1=ALU.mult),
                       reads=[x_b[i], ssq_b[q], rp_b], writes=[xn_b[q]])
                    pb = ps[i % 2].bitcast(BF16)
                    for kc in range(8):
                        op("pe", lambda e: e.transpose(pb[:, kc * 128:(kc + 1) * 128], xn[q][:, kc * 128:(kc + 1) * 128], ident[:]),
                           reads=[xn_b[q], ident_b], writes=[ps_b[i % 2]])
                    eng = "act" if i % 2 == 0 else "dve"
                    if eng == "act":
                        op("act", lambda e: e.activation(out=hT[:, :, i * 128:(i + 1) * 128],
                                                         in_=pb.rearrange("p (k t) -> p k t", k=8), func=AF.Identity),
                           reads=[ps_b[i % 2]], writes=[hT_b])
                    else:
                        op("dve", lambda e: e.tensor_copy(out=hT[:, :, i * 128:(i + 1) * 128],
                                                          in_=pb.rearrange("p (k t) -> p k t", k=8)),
                           reads=[ps_b[i % 2]], writes=[hT_b])
            kb.barrier()

            def epilogue(bname, nk, w_out_ap, gate_c0, get_lhsT, scale_fn=None):
                with ExitStack() as ees:
                    def esb(name, shape, dt=F32):
                        return ees.enter_context(nc.sbuf_tensor("ep_%s_%s_l%d" % (bname, name, l), shape, dt))
                    wout = esb("wout", [128, nk, D], BF16)
                    wout_bs = [Buf(), Buf()]
                    wg = esb("wg", [128, 8, D], BF16)
                    wg_bs = [Buf(), Buf()]
                    for n in range(2):
                        load_w(wg[:, :, n * 512:(n + 1) * 512], w_in_cols(l, gate_c0 + n * 512, 512), wg_bs[n], fast=True)
                    for n in range(2):
                        load_w(wout[:, :, n * 512:(n + 1) * 512],
                               w_out_ap.rearrange("(kc p) n -> p kc n", p=128)[:, :, n * 512:(n + 1) * 512], wout_bs[n], fast=True)
                    sig = [esb("sig%d" % q, [128, D]) for q in range(2)]
                    sig_b = [Buf(), Buf()]
                    tt = [esb("tt%d" % q, [128, D]) for q in range(2)]
                    tt_b = [Buf(), Buf()]
                    for i in range(dbg.get("ep_tiles", NT)):
                        q = i % 2
                        lhs_ap, lhs_bufs = get_lhsT(i)
                        pO = [4 + 4 * 0 + 0, 5]
                        gb0 = (6, 2)[i % 2]
                        for n in range(2):
                            for kc in range(8):
                                op("pe", lambda e: e.matmul(ps[gb0 + n][:, :], hT[:, kc, i * 128:(i + 1) * 128],
                                                            wg[:, kc, n * 512:(n + 1) * 512],
                                                            start=(kc == 0), stop=(kc == 7)),
                                   reads=[hT_b, wg_bs[n]], writes=[ps_b[gb0 + n]])
                        for n in range(2):
                            for kc in range(nk):
                                op("pe", lambda e: e.matmul(ps[4 + n][:, :], lhs_ap[:, kc, :], wout[:, kc, n * 512:(n + 1) * 512],
                                                            start=(kc == 0), stop=(kc == nk - 1)),
                                   reads=lhs_bufs + [wout_bs[n]], writes=[ps_b[4 + n]])
                        for n in range(2):
                            op("act", lambda e: e.activation(out=sig[q][:, n * 512:(n + 1) * 512], in_=ps[gb0 + n][:, :],
                                                             func=AF.Sigmoid), reads=[ps_b[gb0 + n]], writes=[sig_b[q]])
                        for n in range(2):
                            if scale_fn is None:
                                op("dve", lambda e: e.tensor_tensor(out=tt[q][:, n * 512:(n + 1) * 512], in0=ps[4 + n][:, :],
                                                                    in1=sig[q][:, n * 512:(n + 1) * 512], op=ALU.mult),
                                   reads=[ps_b[4 + n], sig_b[q]], writes=[tt_b[q]])
                            else:
                                sc_ap, sc_bufs = scale_fn(i)
                                op("dve", lambda e: e.scalar_tensor_tensor(out=tt[q][:, n * 512:(n + 1) * 512], in0=ps[4 + n][:, :],
                                                                           scalar=sc_ap, in1=sig[q][:, n * 512:(n + 1) * 512],
                                                                           op0=ALU.mult, op1=ALU.mult),
                                   reads=[ps_b[4 + n], sig_b[q]] + sc_bufs, writes=[tt_b[q]])
                        dma("sp", T_d[bname][i * 128:(i + 1) * 128, :], tt[q][:], reads=[tt_b[q]], writes=[T_buf[bname][i]])
                kb.barrier()

            if "sb" in branches:
                with ExitStack() as bes:
                    def bsb(name, shape, dt=F32):
                        return bes.enter_context(nc.sbuf_tensor("sb_%s_l%d" % (name, l), shape, dt))
                    yg_all = bsb("yg_all", [128, NT, 512], BF16)
                    yg_b = [Buf() for _ in range(NT)]
                    if dbg:
                        for i in range(NT):
                            op("pool", lambda e: e.memset(yg_all[:, i, :], 0.0), writes=[yg_b[i]])
                    with ExitStack() as aes:
                        def asb(name, shape, dt=F32):
                            return aes.enter_context(nc.sbuf_tensor("sba_%s_l%d" % (name, l), shape, dt))
                        wp = [asb("wp%d" % q, [128, 8, 512], BF16) for q in range(1)] * 2
                        wp_b = [Buf()] * 2
                        qz2 = asb("qz2", [128, NT, 2, 128], BF16)
                        qT_b = Buf()
                        op("pool", lambda e: e.memset(qz2[64:128, :, 0, :], 0.0), writes=[qT_b])
                        op("pool", lambda e: e.memset(qz2[0:64, :, 1, :], 0.0), writes=[qT_b])
                        kT = asb("kT", [128, S], BF16)
                        kT_b = Buf()
                        v_tm = asb("v_tm", [128, NT, 128], BF16)
                        v_b = Buf()
                        STR = []
                        NS = 4
                        for s_ in range(NS):
                            d_ = dict(sp_all=asb("sp_all%d" % s_, [128, NT, 2, 128], BF16), sp_b=[Buf() for _ in range(NT // 2)],
                                      etmp=[asb("etmp%d_%d" % (s_, q), [128, 512]) for q in range(1)] * 2, etmp_b=[Buf()] * 2,
                                      att=[asb("att%d_%d" % (s_, q), [128, 2, 2, 128], BF16) for q in range(1)] * 2, att_b=[Buf()] * 2,
                                      csA=asb("csA%d" % s_, [128, 32, 2]), csB=asb("csB%d" % s_, [128, 32, 2]), csA_b=Buf(), csB_b=Buf(),
                                      er=asb("er%d" % s_, [128, 16, 2]), er_b=Buf(),
                                      yacc=asb("yacc%d" % s_, [128, 2, 64]), yacc_b=[Buf(), Buf()],
                                      sg=asb("sg%d" % s_, [128, 128]), sg_b=Buf(), base=2 * s_)
                            op("dve", lambda e: e.memset(d_["csA"][:], 0.0), writes=[d_["csA_b"]])
                            op("dve", lambda e: e.memset(d_["csB"][:], 0.0), writes=[d_["csB_b"]])
                            STR.append(d_)

                        def load_pair(hp):
                            q = hp % 2
                            for ci, c0 in enumerate((C_SBQ, C_SBK, C_SBV, C_SBG)):
                                load_w(wp[q][:, :, ci * 128:(ci + 1) * 128], w_in_cols(l, c0 + hp * 128, 128), wp_b[q])

                        def sb_row(i, st, hp, q):
                            sp_all, sp_b, etmp, etmp_b, att, att_b = st["sp_all"], st["sp_b"], st["etmp"], st["etmp_b"], st["att"], st["att_b"]
                            csA, csB, csA_b, csB_b, er, er_b = st["csA"], st["csB"], st["csA_b"], st["csB_b"], st["er"], st["er_b"]
                            yacc, yacc_b, sg, sg_b, base = st["yacc"], st["yacc_b"], st["sg"], st["sg_b"], st["base"]
                            bCS = base + 1
                            bP = base + 1
                            PO = 256
                            ng = i // 2 + 1
                            qi = qz2[:, i, :, :].rearrange("p h t -> p (h t)")
                            for g in range(ng):
                                bk = base
                                eb = g % 2
                                js = [j for j in (2 * g, 2 * g + 1) if j <= i]
                                nj = len(js)
                                for jj, j in enumerate(js):
                                    op("pe", lambda e: e.matmul(ps[bk][:, jj * 256:(jj + 1) * 256], kT[:, j * 128:(j + 1) * 128], qi,
                                                                start=True, stop=True),
                                       reads=[kT_b, qT_b], writes=[ps_b[bk]])
                                yield
                                w = nj * 256
                                op("act", lambda e: e.activation(out=etmp[eb][:, 0:w], in_=ps[bk][:, 0:w], func=AF.Exp),
                                   reads=[ps_b[bk]], writes=[etmp_b[eb]])
                                yield
                                op("act", lambda e: e.activation(
                                    out=sp_all[:, 2 * g:2 * g + nj, :, :].rearrange("p a b c -> p (a b c)"),
                                    in_=etmp[eb][:, 0:w], func=AF.Ln, bias=1.0),
                                   reads=[etmp_b[eb]], writes=[sp_b[g]])
                                yield
                                if i in js:
                                    op("dve", lambda e: e.tensor_tensor(out=sp_all[:, i, :, :], in0=sp_all[:, i, :, :],
                                                                        in1=mask_lt[:], op=ALU.mult),
                                       reads=[sp_b[g], mask_lt_b], writes=[sp_b[g]])
                                    yield
                                for jj, j in enumerate(js):
                                    for h in range(2):
                                        op("pe", lambda e: e.matmul(ps[bCS][:, j * 2 + h:j * 2 + h + 1], sp_all[:, j, h, :],
                                                                    ones_bf[:, 0:1], start=True, stop=True),
                                           reads=[sp_b[g], ones_b], writes=[ps_b[bCS]])
                                    yield
                            nn = (i + 1) * 2
                            op("dve", lambda e: e.tensor_copy(out=csA[:, 0:i + 1, :].rearrange("p a b -> p (a b)"),
                                                              in_=ps[bCS][:, 0:nn]), reads=[ps_b[bCS]], writes=[csA_b])
                            yield
                            src, dst, src_b, dst_b = csA, csB, csA_b, csB_b
                            for d_ in (1, 2, 4, 8):
                                op("dve", lambda e: e.tensor_tensor(out=dst[:, 0:16, :], in0=src[:, 0:16, :],
                                                                    in1=src[:, d_:16 + d_, :], op=ALU.add),
                                   reads=[src_b], writes=[dst_b])
                                src, dst, src_b, dst_b = dst, src, dst_b, src_b
                                yield
                            op("act", lambda e: e.activation(out=er[:, 0:16, :], in_=src[:, 1:17, :], func=AF.Exp, scale=-1.0),
                               reads=[src_b], writes=[er_b])
                            yield
                            first = [True, True]
                            for g in range(ng):
                                bk = base
                                a = g % 2
                                js = [j for j in (2 * g, 2 * g + 1) if j <= i]
                                nj = len(js)
                                for jj, j in enumerate(js):
                                    o = ps[bk][:, jj * 256:(jj + 1) * 256]
                                    op("pe", lambda e: e.matmul(o, kT[:, j * 128:(j + 1) * 128], qi, start=True, stop=False),
                                       reads=[kT_b, qT_b], writes=[ps_b[bk]])
                                    op("pe", lambda e: e.matmul(o, negtri[:], sp_all[:, j, :, :].rearrange("p a b -> p (a b)"),
                                                                start=False, stop=True),
                                       reads=[negtri_b, sp_b[g]], writes=[ps_b[bk]])
                                    yield
                                w = nj * 256
                                op("act", lambda e: e.activation(out=att[a][:, 0:nj, :, :].rearrange("p a b c -> p (a b c)"),
                                                                 in_=ps[bk][:, 0:w], func=AF.Exp),
                                   reads=[ps_b[bk]], writes=[att_b[a]])
                                yield
                                if i in js:
                                    jj_i = js.index(i)
                                    op("dve", lambda e: e.tensor_tensor(out=att[a][:, jj_i, :, :], in0=att[a][:, jj_i, :, :],
                                                                        in1=mask_lt[:], op=ALU.mult),
                                       reads=[att_b[a], mask_lt_b], writes=[att_b[a]])
                                    yield
                                for jj, j in enumerate(js):
                                    for h in range(2):
                                        op("pe", lambda e: e.matmul(ps[bP][:, PO + (jj * 2 + h) * 64:PO + (jj * 2 + h + 1) * 64],
                                                                    att[a][:, jj, h, :], v_tm[:, j, h * 64:(h + 1) * 64],
                                                                    start=True, stop=True),
                                           reads=[att_b[a], v_b], writes=[ps_b[bP]])
                                    yield
                                for jj, j in enumerate(js):
                                    for h in range(2):
                                        pin = ps[bP][:, PO + (jj * 2 + h) * 64:PO + (jj * 2 + h + 1) * 64]
                                        if first[h]:
                                            op("dve", lambda e: e.tensor_scalar(out=yacc[:, h, :], in0=pin, scalar1=er[:, j, h:h + 1],
                                                                                scalar2=None, op0=ALU.mult),
                                               reads=[ps_b[bP], er_b], writes=[yacc_b[h]])
                                            first[h] = False
                                        else:
                                            op("dve", lambda e: e.scalar_tensor_tensor(out=yacc[:, h, :], in0=pin, scalar=er[:, j, h:h + 1],
                                                                                       in1=yacc[:, h, :], op0=ALU.mult, op1=ALU.add),
                                               reads=[ps_b[bP], er_b, yacc_b[h]], writes=[yacc_b[h]])
                                        yield
                            for kc in range(8):
                                op("pe", lambda e: e.matmul(ps[bCS][:, 128:256], hT[:, kc, i * 128:(i + 1) * 128], wp[q][:, kc, 384:512],
                                                            start=(kc == 0), stop=(kc == 7)),
                                   reads=[hT_b, wp_b[q]], writes=[ps_b[bCS]])
                                if kc % 4 == 3:
                                    yield
                            op("act", lambda e: e.activation(out=sg[:], in_=ps[bCS][:, 128:256], func=AF.Silu),
                               reads=[ps_b[bCS]], writes=[sg_b])
                            yield
                            op("dve", lambda e: e.tensor_tensor(out=yg_all[:, i, hp * 128:(hp + 1) * 128],
                                                                in0=yacc[:].rearrange("p a b -> p (a b)"), in1=sg[:], op=ALU.mult),
                               reads=[yacc_b[0], yacc_b[1], sg_b], writes=[yg_b[i]])
                            yield

                        for hp in range(dbg.get("sb_pairs", 4)):
                            q = hp % 2
                            load_pair(hp)
                            for st in STR:
                                op("dve", lambda e: e.memset(st["csA"][:], 0.0), writes=[st["csA_b"]])
                            for ci in range(2):
                                for tg in range(4):
                                    bk = (ci * 4 + tg) % 4
                                    for kc in range(8):
                                        op("pe", lambda e: e.matmul(ps[bk][:, :], wp[q][:, kc, ci * 128:(ci + 1) * 128],
                                                                    hT[:, kc, tg * 512:(tg + 1) * 512],
                                                                    start=(kc == 0), stop=(kc == 7)),
                                           reads=[wp_b[q], hT_b], writes=[ps_b[bk]])
                                    if ci == 0:
                                        for h in range(2):
                                            op("act", lambda e: e.activation(out=qz2[h * 64:(h + 1) * 64, tg * 4:(tg + 1) * 4, h, :],
                                                                             in_=ps[bk][h * 64:(h + 1) * 64, :].rearrange("p (a t) -> p a t", a=4),
                                                                             func=AF.Copy, scale=0.125),
                                               reads=[ps_b[bk]], writes=[qT_b])
                                    else:
                                        op("act", lambda e: e.activation(out=kT[:, tg * 512:(tg + 1) * 512], in_=ps[bk][:, :],
                                                                         func=AF.Copy), reads=[ps_b[bk]], writes=[kT_b])
                            for i4 in range(4):
                                bk = 4 + i4 % 4
                                for ii in range(4):
                                    i = i4 * 4 + ii
                                    for kc in range(8):
                                        op("pe", lambda e: e.matmul(ps[bk][:, ii * 128:(ii + 1) * 128], hT[:, kc, i * 128:(i + 1) * 128],
                                                                    wp[q][:, kc, 256:384], start=(kc == 0), stop=(kc == 7)),
                                           reads=[wp_b[q], hT_b], writes=[ps_b[bk]])
                                op("dve", lambda e: e.tensor_copy(out=v_tm[:, i4 * 4:(i4 + 1) * 4, :],
                                                                  in_=ps[bk][:, :].rearrange("p (a b) -> p a b", a=4)),
                                   reads=[ps_b[bk]], writes=[v_b])
                            nrows = dbg.get("sb_rows", NT)
                            for i0 in range(0, nrows, NS):
                                gens = [sb_row(i0 + s_, STR[s_], hp, q) for s_ in range(NS) if i0 + s_ < nrows]
                                while gens:
                                    for g_ in list(gens):
                                        try:
                                            next(g_)
                                        except StopIteration:
                                            gens.remove(g_)
                    kb.barrier()
                    ygT = [bsb("ygT%d" % q, [128, 4, 128], BF16) for q in range(2)]
                    ygT_b = [Buf(), Buf()]

                    def sb_lhsT(i):
                        q = i % 2
                        pb = ps[q].bitcast(BF16)
                        for c in range(4):
                            op("pe", lambda e: e.transpose(pb[:, c * 128:(c + 1) * 128], yg_all[:, i, c * 128:(c + 1) * 128], ident[:]),
                               reads=[yg_b[i], ident_b], writes=[ps_b[q]])
                        op("dve", lambda e: e.tensor_copy(out=ygT[q][:].rearrange("p a b -> p (a b)"), in_=pb[:, 0:512]),
                           reads=[ps_b[q]], writes=[ygT_b[q]])
                        return ygT[q], [ygT_b[q]]

                    epilogue("sb", 4, w_out_sb_d[l], C_GATE, sb_lhsT)


            if "ssd" in branches:
                with ExitStack() as bes:
                    def bsb(name, shape, dt=F32):
                        return bes.enter_context(nc.sbuf_tensor("ssd_%s_l%d" % (name, l), shape, dt))
                    pes = ExitStack()
                    bsb_outer = bsb

                    def bsb(name, shape, dt=F32):
                        return pes.enter_context(nc.sbuf_tensor("ssd_%s_l%d" % (name, l), shape, dt))
                    B_fm = bsb("B_fm", [128, S], BF16)
                    B_fm_b = Buf()
                    xs_tm = bsb("xs_tm", [128, NT, D], BF16)
                    xs_tm_b = [Buf() for _ in range(NT)]
                    Cz = [bsb("Cz%d" % g, [128, S], BF16) for g in range(2)]
                    Cz_b = Buf()
                    B_tm = bsb("B_tm", [128, NT, 128], BF16)
                    B_tm_b = Buf()
                    dt_tm = bsb("dt_tm", [128, NT, 16])
                    da_tm = bsb("da_tm", [128, NT, 16])
                    dt_b = Buf()
                    ah = bsb("ah", [128, 16])
                    ah_b = Buf()
                    op("act", lambda e: e.activation(out=ah[:], in_=rp_sb[:, RP_ALOG:RP_ALOG + 16], func=AF.Exp), reads=[rp_b], writes=[ah_b])
                    op("dve", lambda e: e.tensor_scalar(out=ah[:], in0=ah[:], scalar1=-1.0, scalar2=None, op0=ALU.mult), reads=[ah_b], writes=[ah_b])
                    op("pool", lambda e: e.memset(Cz[0][64:128, :], 0.0), writes=[Cz_b])
                    op("pool", lambda e: e.memset(Cz[1][0:64, :], 0.0), writes=[Cz_b])
                    with ExitStack() as aes:
                        def asb(name, shape, dt=F32):
                            return aes.enter_context(nc.sbuf_tensor("ssda_%s_l%d" % (name, l), shape, dt))
                        wc = [asb("wc%d" % q, [128, 8, 128], BF16) for q in range(2)]
                        wc_b = [Buf(), Buf()]
                        raw2 = [asb("raw%d" % q, [128, 3 + S]) for q in range(2)]
                        raw2_b = [Buf(), Buf()]
                        acc2 = [asb("acc%d" % q, [128, S]) for q in range(2)]
                        acc2_b = [Buf(), Buf()]
                        fm2 = [asb("fm%d" % q, [128, S], BF16) for q in range(1)] * 2
                        fm2_b = [Buf()] * 2
                        for q in range(2):
                            op("dve", lambda e: e.memset(raw2[q][:, 0:3], 0.0), writes=[raw2_b[q]])
                        load_w(wc[0][:], w_in_cols(l, C_XBC, 128), wc_b[0])

                        def conv_proj(cc):
                            q = cc % 2
                            raw, raw_b = raw2[q], raw2_b[q]
                            if cc + 1 < 10:
                                load_w(wc[1 - q][:], w_in_cols(l, C_XBC + (cc + 1) * 128, 128), wc_b[1 - q])
                            for tg in range(4):
                                bk = tg
                                for kc in range(8):
                                    op("pe", lambda e: e.matmul(ps[bk][:, :], wc[q][:, kc, :], hT[:, kc, tg * 512:(tg + 1) * 512],
                                                                start=(kc == 0), stop=(kc == 7)),
                                       reads=[wc_b[q], hT_b], writes=[ps_b[bk]])
                                op("act", lambda e: e.activation(out=raw[:, 3 + tg * 512:3 + (tg + 1) * 512], in_=ps[bk][:, :], func=AF.Copy),
                                   reads=[ps_b[bk]], writes=[raw_b])

                        def conv_post(cc):
                            q = cc % 2
                            raw, raw_b, acc, acc_b, fm, fm_b = raw2[q], raw2_b[q], acc2[q], acc2_b[q], fm2[q], fm2_b[q]
                            cw = lambda k: cp_sb[:, CP_CW + cc * 4 + k:CP_CW + cc * 4 + k + 1]
                            op("act", lambda e: e.activation(out=acc[:], in_=raw[:, 3:3 + S], func=AF.Identity, scale=cw(3),
                                                             bias=cp_sb[:, CP_CB + cc:CP_CB + cc + 1]),
                               reads=[raw_b, cp_b], writes=[acc_b])
                            for k in (2, 1, 0):
                                op("dve", lambda e: e.scalar_tensor_tensor(out=acc[:], in0=raw[:, k:k + S], scalar=cw(k), in1=acc[:],
                                                                           op0=ALU.mult, op1=ALU.add),
                                   reads=[raw_b, cp_b, acc_b], writes=[acc_b])
                            if cc < 8:
                                op("act", lambda e: e.activation(out=fm[:], in_=acc[:], func=AF.Silu), reads=[acc_b], writes=[fm_b])
                                src_t, src_tb = fm, fm_b
                            elif cc == 8:
                                op("act", lambda e: e.activation(out=B_fm[:], in_=acc[:], func=AF.Silu), reads=[acc_b], writes=[B_fm_b])
                                src_t, src_tb = B_fm, B_fm_b
                            else:
                                for g in range(2):
                                    op("act", lambda e: e.activation(out=Cz[g][g * 64:(g + 1) * 64, :], in_=acc[g * 64:(g + 1) * 64, :],
                                                                     func=AF.Silu), reads=[acc_b], writes=[Cz_b])
                                return
                            for i4 in range(2):
                                bk = 4 + i4
                                pb = ps[bk].bitcast(BF16)
                                for ii in range(8):
                                    i = i4 * 8 + ii
                                    op("pe", lambda e: e.transpose(pb[:, ii * 128:(ii + 1) * 128], src_t[:, i * 128:(i + 1) * 128], ident[:]),
                                       reads=[src_tb, ident_b], writes=[ps_b[bk]])
                                if cc < 8:
                                    op("dve", lambda e: e.tensor_copy(out=xs_tm[:, i4 * 8:(i4 + 1) * 8, cc * 128:(cc + 1) * 128],
                                                                      in_=pb.rearrange("p (a b) -> p a b", a=8)),
                                       reads=[ps_b[bk]], writes=xs_tm_b[i4 * 8:(i4 + 1) * 8])
                                else:
                                    op("dve", lambda e: e.tensor_copy(out=B_tm[:, i4 * 8:(i4 + 1) * 8, :],
                                                                      in_=pb.rearrange("p (a b) -> p a b", a=8)),
                                       reads=[ps_b[bk]], writes=[B_tm_b])

                        for step in range(11):
                            if step < 10:
                                conv_proj(step)
                            if step >= 1:
                                conv_post(step - 1)
                    kb.barrier()
                    with ExitStack() as aes:
                        def asb(name, shape, dt=F32):
                            return aes.enter_context(nc.sbuf_tensor("ssdb_%s_l%d" % (name, l), shape, dt))
                        wdt = asb("wdt", [128, 8, 16], BF16)
                        wdt_b = Buf()
                        wdt_f = asb("wdt_f", [128, 8, 16])
                        wdt_fb = Buf()
                        dma("sp", wdt_f[:], wdt_d[l], writes=[wdt_fb])
                        op("pool", lambda e: e.tensor_copy(out=wdt[:], in_=wdt_f[:]), reads=[wdt_fb], writes=[wdt_b])
                        wz = [asb("wz%d" % q, [128, 8, 512], BF16) for q in range(2)]
                        wz_b = [Buf(), Buf()]
                        for n in range(2):
                            load_w(wz[n][:], w_in_cols(l, C_Z + n * 512, 512), wz_b[n], fast=True)
                        dtmp = asb("dtmp", [128, 16])
                        dtmp_b = Buf()
                        szt = [asb("szt%d" % q, [128, 512], BF16) for q in range(2)]
                        szt_b = [Buf(), Buf()]
                        for i in range(NT):
                            for kc in range(8):
                                op("pe", lambda e: e.matmul(ps[0][:, 0:16], hT[:, kc, i * 128:(i + 1) * 128], wdt[:, kc, :],
                                                            start=(kc == 0), stop=(kc == 7)), reads=[hT_b, wdt_b], writes=[ps_b[0]])
                            op("dve", lambda e: e.tensor_tensor(out=dtmp[:], in0=ps[0][:, 0:16], in1=rp_sb[:, RP_DTB:RP_DTB + 16], op=ALU.add),
                               reads=[ps_b[0], rp_b], writes=[dtmp_b])
                            op("act", lambda e: e.activation(out=dtmp[:], in_=dtmp[:], func=AF.Exp), reads=[dtmp_b], writes=[dtmp_b])
                            op("act", lambda e: e.activation(out=dt_tm[:, i, :], in_=dtmp[:], func=AF.Ln, bias=1.0), reads=[dtmp_b], writes=[dt_b])
                            op("dve", lambda e: e.tensor_tensor(out=da_tm[:, i, :], in0=dt_tm[:, i, :], in1=ah[:], op=ALU.mult),
                               reads=[dt_b, ah_b], writes=[dt_b])
                        for n in range(2):
                            for i in range(NT):
                                q = i % 2
                                bk = 1 + q
                                for kc in range(8):
                                    op("pe", lambda e: e.matmul(ps[bk][:, :], hT[:, kc, i * 128:(i + 1) * 128], wz[n][:, kc, :],
                                                                start=(kc == 0), stop=(kc == 7)), reads=[hT_b, wz_b[n]], writes=[ps_b[bk]])
                                op("act", lambda e: e.activation(out=szt[q][:], in_=ps[bk][:, :], func=AF.Silu), reads=[ps_b[bk]], writes=[szt_b[q]])
                                dma("sp", sz_d[i * 128:(i + 1) * 128, n * 512:(n + 1) * 512], szt[q][:], reads=[szt_b[q]], writes=[sz_db[i]])
                    kb.barrier()
                    with ExitStack() as aes:
                        def asb(name, shape, dt=F32):
                            return aes.enter_context(nc.sbuf_tensor("ssdc_%s_l%d" % (name, l), shape, dt))
                        sz_t = [asb("sz_t%d" % q, [128, D], BF16) for q in range(2)]
                        sz_tb = [Buf(), Buf()]
                        x_dt = asb("x_dt", [128, 16, 64], BF16)
                        x_dt_b = Buf()
                        x_dts = asb("x_dts", [128, 16, 64], BF16)
                        x_dts_b = Buf()
                        eas = asb("eas", [128, 3, 16])
                        eas_b = Buf()
                        rseg = asb("rseg", [128, 16, 128])
                        rseg_b = Buf()
                        eseg = asb("eseg", [128, 16, 128], BF16)
                        eseg_b = Buf()
                        cbm = asb("cbm", [128, 2, 128], BF16)
                        cbm_b = Buf()
                        MT = asb("MT", [128, 16, 128], BF16)
                        MT_b = Buf()
                        ST = asb("ST", [128, 8, 64])
                        ST_b = Buf()
                        ST_bf = asb("ST_bf", [128, 512], BF16)
                        ST_bf_b = Buf()
                        t1 = asb("t1", [128, 16, 64])
                        t1_b = Buf()
                        t2 = asb("t2", [128, D])
                        t2_b = Buf()
                        t3, t3_b = t1, t1_b
                        ssq = asb("ssq", [128, 4])
                        ssq_b = Buf()
                        yn = [asb("yn%d" % q, [128, D], BF16) for q in range(2)]
                        yn_b = [Buf(), Buf()]
                        op("dve", lambda e: e.memset(ST[:], 0.0), writes=[ST_b])
                        op("dve", lambda e: e.memset(ST_bf[:], 0.0), writes=[ST_bf_b])

                        def bc_p(ap2, n0, n1):
                            return ap2.unsqueeze(2).broadcast_to([128, n0, n1])

                        for c in range(dbg.get("ssd_chunks", NT)):
                            q = c % 2
                            tok = slice(c * 128, (c + 1) * 128)
                            xs_c = xs_tm[:, c, :].rearrange("p (a b) -> p a b", a=16)
                            dma("sp", sz_t[q][:], sz_d[tok, :], reads=[sz_db[c]], writes=[sz_tb[q]])
                            da_c = da_tm[:, c, :]
                            for k3, (lh, lh_b) in enumerate(((triLE, triLE_b), (maskGT, maskGT_b), (ones_f, ones_f_b))):
                                op("pe", lambda e: e.matmul(ps[0][:, k3 * 16:(k3 + 1) * 16], lh[:], da_c, start=True, stop=True),
                                   reads=[lh_b, dt_b], writes=[ps_b[0]])
                            op("act", lambda e: e.activation(out=eas[:].rearrange("p a b -> p (a b)"), in_=ps[0][:, 0:48], func=AF.Exp),
                               reads=[ps_b[0]], writes=[eas_b])
                            for g in range(2):
                                op("pe", lambda e: e.matmul(ps[7][:, g * 128:(g + 1) * 128], B_fm[:, tok], Cz[g][:, tok], start=True, stop=True),
                                   reads=[B_fm_b, Cz_b], writes=[ps_b[7]])
                            op("dve", lambda e: e.tensor_tensor(out=cbm[:], in0=ps[7][:, 0:256].rearrange("p (a b) -> p a b", a=2),
                                                                in1=triLE[:].unsqueeze(1).broadcast_to([128, 2, 128]), op=ALU.mult),
                               reads=[ps_b[7], triLE_b], writes=[cbm_b])
                            op("pool", lambda e: e.tensor_tensor(out=rseg[:], in0=triLE[:].unsqueeze(1).broadcast_to([128, 16, 128]),
                                                                 in1=bc_p(da_c, 16, 128), op=ALU.mult),
                               reads=[triLE_b, dt_b], writes=[rseg_b])
                            for q4 in range(4):
                                bk = 1 + q4
                                op("pe", lambda e: e.matmul(ps[bk][:, :], maskGT[:], rseg[:, q4 * 4:(q4 + 1) * 4, :].rearrange("p a b -> p (a b)"),
                                                            start=True, stop=True), reads=[maskGT_b, rseg_b], writes=[ps_b[bk]])
                                op("act", lambda e: e.activation(out=eseg[:, q4 * 4:(q4 + 1) * 4, :].rearrange("p a b -> p (a b)"),
                                                                 in_=ps[bk][:, :], func=AF.Exp), reads=[ps_b[bk]], writes=[eseg_b])
                            for g in range(2):
                                op("dve", lambda e: e.tensor_tensor(out=MT[:, g * 8:(g + 1) * 8, :], in0=eseg[:, g * 8:(g + 1) * 8, :],
                                                                    in1=cbm[:, g, :].unsqueeze(1).broadcast_to([128, 8, 128]), op=ALU.mult),
                                   reads=[eseg_b, cbm_b], writes=[MT_b])
                            op("dve", lambda e: e.tensor_tensor(out=x_dt[:], in0=xs_c, in1=bc_p(dt_tm[:, c, :], 16, 64), op=ALU.mult),
                               reads=[xs_tm_b[c], dt_b], writes=[x_dt_b])
                            op("pool", lambda e: e.tensor_tensor(out=x_dts[:], in0=x_dt[:], in1=bc_p(eas[:, 1, :], 16, 64), op=ALU.mult),
                               reads=[x_dt_b, eas_b], writes=[x_dts_b])
                            for h in range(16):
                                bk = 5 + h // 8
                                op("pe", lambda e: e.matmul(ps[bk][:, (h % 8) * 64:(h % 8 + 1) * 64], MT[:, h, :], x_dt[:, h, :],
                                                            start=True, stop=True), reads=[MT_b, x_dt_b], writes=[ps_b[bk]])
                            for g in range(2):
                                op("pe", lambda e: e.matmul(ps[1 + g][:, :], Cz[g][:, tok], ST_bf[:], start=True, stop=True),
                                   reads=[Cz_b, ST_bf_b], writes=[ps_b[1 + g]])
                            for g in range(2):
                                op("pe", lambda e: e.matmul(ps[3 + g][:, :], B_tm[:, c, :],
                                                            x_dts[:, g * 8:(g + 1) * 8, :].rearrange("p a b -> p (a b)"),
                                                            start=True, stop=True), reads=[B_tm_b, x_dts_b], writes=[ps_b[3 + g]])
                            for g in range(2):
                                op("dve", lambda e: e.tensor_tensor(out=t1[:, g * 8:(g + 1) * 8, :],
                                                                    in0=ps[1 + g][:, :].rearrange("p (a b) -> p a b", a=8),
                                                                    in1=bc_p(eas[:, 0, g * 8:(g + 1) * 8], 8, 64), op=ALU.mult),
                                   reads=[ps_b[1 + g], eas_b], writes=[t1_b])
                            for g in range(2):
                                op("dve", lambda e: e.tensor_tensor(out=t2[:, g * 512:(g + 1) * 512], in0=ps[5 + g][:, :],
                                                                    in1=t1[:, g * 8:(g + 1) * 8, :].rearrange("p a b -> p (a b)"), op=ALU.add),
                                   reads=[ps_b[5 + g], t1_b], writes=[t2_b])
                            op("pool", lambda e: e.tensor_tensor(out=t3[:], in0=xs_c, in1=bc_p(rp_sb[:, RP_DSK:RP_DSK + 16], 16, 64), op=ALU.mult),
                               reads=[xs_tm_b[c], rp_b], writes=[t3_b])
                            op("pool", lambda e: e.tensor_tensor(out=t2[:], in0=t2[:], in1=t3[:].rearrange("p a b -> p (a b)"), op=ALU.add),
                               reads=[t2_b, t3_b], writes=[t2_b])
                            op("pool", lambda e: e.tensor_tensor(out=t2[:], in0=t2[:], in1=sz_t[q][:], op=ALU.mult),
                               reads=[t2_b, sz_tb[q]], writes=[t2_b])
                            for g in range(2):
                                hp_ = slice(g * 64, (g + 1) * 64)
                                op("dve", lambda e: e.tensor_tensor(out=ST[hp_, :, :], in0=ST[hp_, :, :],
                                                                    in1=eas[hp_, 2, g * 8:(g + 1) * 8].unsqueeze(2).broadcast_to([64, 8, 64]),
                                                                    op=ALU.mult), reads=[ST_b, eas_b], writes=[ST_b])
                                op("dve", lambda e: e.tensor_tensor(out=ST[hp_, :, :], in0=ps[3 + g][hp_, :].rearrange("p (a b) -> p a b", a=8),
                                                                    in1=ST[hp_, :, :], op=ALU.add), reads=[ps_b[3 + g], ST_b], writes=[ST_b])
                            op("act", lambda e: e.activation(out=ST_bf[:], in_=ST[:].rearrange("p a b -> p (a b)"), func=AF.Copy),
                               reads=[ST_b], writes=[ST_bf_b])
                            rms_rstd(t2[:], t2_b, x_dts[:].rearrange("p a b -> p (a b)"), x_dts_b, ssq, ssq_b, D, RMS_EPS)
                            op("dve", lambda e: e.scalar_tensor_tensor(out=yn[q][:], in0=t2[:], scalar=ssq[:, 2:3],
                                                                       in1=rp_sb[:, RP_SG:RP_SG + D], op0=ALU.mult, op1=ALU.mult),
                               reads=[t2_b, ssq_b, rp_b], writes=[yn_b[q]])
                            dma("sp", yn_d[tok, :], yn[q][:], reads=[yn_b[q]], writes=[yn_db[c]])
                    kb.barrier()
                    pes.close()
                    bsb = bsb_outer
                    ynl = [bsb("ynl%d" % q, [128, D], BF16) for q in range(2)]
                    ynl_b = [Buf(), Buf()]
                    ynT = [bsb("ynT%d" % q, [128, 8, 128], BF16) for q in range(2)]
                    ynT_b = [Buf(), Buf()]

                    def ssd_lhsT(i):
                        q = i % 2
                        dma("sp", ynl[q][:], yn_d[i * 128:(i + 1) * 128, :], reads=[yn_db[i]], writes=[ynl_b[q]])
                        pb = ps[q].bitcast(BF16)
                        for c in range(8):
                            op("pe", lambda e: e.transpose(pb[:, c * 128:(c + 1) * 128], ynl[q][:, c * 128:(c + 1) * 128], ident[:]),
                               reads=[ynl_b[q], ident_b], writes=[ps_b[q]])
                        op("dve", lambda e: e.tensor_copy(out=ynT[q][:].rearrange("p a b -> p (a b)"), in_=pb[:, :]),
                           reads=[ps_b[q]], writes=[ynT_b[q]])
                        return ynT[q], [ynT_b[q]]

                    epilogue("ssd", 8, w_out_ssd_d[l], C_GATE + D, ssd_lhsT)

            if "rw" in branches:
                with ExitStack() as bes:
                    def bsb(name, shape, dt=F32):
                        return bes.enter_context(nc.sbuf_tensor("rw_%s_l%d" % (name, l), shape, dt))
                    C0 = 0.6065306597126334
                    QT = 256
                    NQ = S // QT
                    NC_ = QT // 64
                    lora_bf = bsb("lora_bf", [128, S], BF16)
                    lora_b = Buf()
                    carry = bsb("carry", [128, 8])
                    carry_b = [Buf() for _ in range(5)]
                    raw = bsb("raw", [128, QT + 1])
                    raw_b = Buf()
                    tmpm = bsb("tmpm", [128, QT])
                    tmpm_b = Buf()
                    pbank = [0]

                    def proj_mix(wt, wt_b, wcol, mu_col, stream, qt, out_ap, out_buf):
                        t0 = qt * QT
                        bk = pbank[0] % 2
                        pbank[0] += 1
                        for kc in range(8):
                            op("pe", lambda e: e.matmul(ps[bk][:, 0:QT], wt[:, kc, wcol:wcol + 128], hT[:, kc, t0:t0 + QT],
                                                        start=(kc == 0), stop=(kc == 7)), reads=[wt_b, hT_b], writes=[ps_b[bk]])
                            if kc % 2 == 1:
                                yield
                        if qt == 0:
                            op("dve", lambda e: e.memset(raw[:, 0:1], 0.0), writes=[raw_b])
                        else:
                            op("dve", lambda e: e.tensor_copy(out=raw[:, 0:1], in_=carry[:, stream:stream + 1]),
                               reads=[carry_b[stream]], writes=[raw_b])
                        yield
                        op("act", lambda e: e.activation(out=raw[:, 1:QT + 1], in_=ps[bk][:, 0:QT], func=AF.Copy), reads=[ps_b[bk]], writes=[raw_b])
                        yield
                        op("dve", lambda e: e.tensor_copy(out=carry[:, stream:stream + 1], in_=raw[:, QT:QT + 1]),
                           reads=[raw_b], writes=[carry_b[stream]])
                        op("dve", lambda e: e.tensor_tensor(out=tmpm[:], in0=raw[:, 0:QT], in1=raw[:, 1:QT + 1], op=ALU.subtract),
                           reads=[raw_b], writes=[tmpm_b])
                        yield
                        op("dve", lambda e: e.scalar_tensor_tensor(out=out_ap, in0=tmpm[:], scalar=cp_sb[:, mu_col:mu_col + 1],
                                                                   in1=raw[:, 1:QT + 1], op0=ALU.mult, op1=ALU.add),
                           reads=[tmpm_b, raw_b, cp_b], writes=[out_buf])
                        yield

                    def drain(g):
                        for _ in g:
                            pass

                    with ExitStack() as aes:
                        wl = aes.enter_context(nc.sbuf_tensor("rw_wl_l%d" % l, [128, 8, 128], BF16))
                        wl_b = Buf()
                        load_w(wl[:], w_in_cols(l, C_RW + 2048, 128), wl_b)
                        lmix = aes.enter_context(nc.sbuf_tensor("rw_lmix_l%d" % l, [128, QT], F32))
                        lmix_b = Buf()
                        for qt in range(NQ):
                            drain(proj_mix(wl, wl_b, 0, CP_MU + 16, 0, qt, lmix[:], lmix_b))
                            op("act", lambda e: e.activation(out=lora_bf[0:64, qt * QT:(qt + 1) * QT], in_=lmix[0:64, :], func=AF.Tanh),
                               reads=[lmix_b], writes=[lora_b])
                            op("act", lambda e: e.activation(out=lora_bf[64:128, qt * QT:(qt + 1) * QT], in_=lmix[64:128, :], func=AF.Copy),
                               reads=[lmix_b], writes=[lora_b])
                    kb.barrier()

                    pes = ExitStack()
                    bsb_outer = bsb

                    def bsb(name, shape, dt=F32):
                        return pes.enter_context(nc.sbuf_tensor("rw_%s_l%d" % (name, l), shape, dt))
                    rmask = bsb("rmask", [128, QT])
                    rmask_b = Buf()
                    op("pool", lambda e: e.memset(rmask[:], 1.0), writes=[rmask_b])
                    op("pool", lambda e: e.memset(rmask[:].rearrange("p (c t) -> p c t", t=64)[:, :, 0:1], 0.0), reads=[rmask_b], writes=[rmask_b])
                    m192 = bsb("m192", [128, 192])
                    m192_b = Buf()
                    mLT = bsb("mLT", [128, 128])
                    mLT_b = Buf()
                    bones = bsb("bones", [128, 128], BF16)
                    bones_b = Buf()
                    op("pool", lambda e: e.memset(m192[:], 0.0), writes=[m192_b])
                    op("pool", lambda e: e.memset(mLT[:], 0.0), writes=[mLT_b])
                    op("pool", lambda e: e.memset(bones[:], 0.0), writes=[bones_b])
                    for h in range(2):
                        hs_ = slice(h * 64, (h + 1) * 64)
                        mk_mask(m192[hs_, h * 64:(h + 1) * 64], m192_b, 1.0, 0.0, 0, -1, 1, 64, ALU.is_gt)
                        mk_mask(m192[hs_, 128:192], m192_b, 1.0, 0.0, 0, -1, 1, 64, ALU.is_ge)
                        mk_mask(mLT[hs_, h * 64:(h + 1) * 64], mLT_b, 1.0, 0.0, 0, 1, -1, 64, ALU.is_gt)
                        op("pool", lambda e: e.memset(bones[hs_, h * 64:(h + 1) * 64], 1.0), reads=[bones_b], writes=[bones_b])
                    S_st = bsb("S_st", [128, 128])
                    S_st_b = Buf()
                    S_bf = bsb("S_bf", [128, 128], BF16)
                    S_bf_b = Buf()
                    omka = bsb("omka", [128, 4])
                    omka_b = Buf()
                    op("dve", lambda e: e.tensor_scalar(out=omka[:], in0=cp_sb[:, CP_KA:CP_KA + 4], scalar1=-1.0, scalar2=1.0,
                                                        op0=ALU.mult, op1=ALU.add), reads=[cp_b], writes=[omka_b])
                    wr = bsb("wr", [128, 8, 512], BF16)
                    wr_b = Buf()
                    wst = bsb("wst", [128, 128])
                    wst_b = Buf()
                    WU = bsb("WU", [128, 128], BF16)
                    AU = bsb("AU", [128, 128], BF16)
                    WU_b = Buf()
                    op("pool", lambda e: e.memset(WU[64:128, :], 0.0), writes=[WU_b])
                    op("pool", lambda e: e.memset(AU[0:64, :], 0.0), writes=[WU_b])
                    NPB = 3
                    PQ = []
                    for k_ in range(NPB):
                        d_ = dict(AR=bsb("AR%d" % k_, [128, NC_, 192], BF16), bbd=bsb("bbd%d" % k_, [128, NC_, 128], BF16),
                                  kbd=bsb("kbd%d" % k_, [128, NC_, 128], BF16), vbd=bsb("vbd%d" % k_, [128, NC_, 128], BF16),
                                  v=bsb("v_bf%d" % k_, [128, QT], BF16), g=bsb("g_bf%d" % k_, [128, QT], BF16),
                                  rk=bsb("rk_bf%d" % k_, [128, QT], BF16), gC=bsb("gC%d" % k_, [128, NC_]), b=Buf())
                        for nm in ("AR", "bbd", "kbd", "vbd"):
                            op("pool", lambda e: e.memset(d_[nm][:], 0.0), writes=[d_["b"]])
                        PQ.append(d_)
                    Fn, Fb = {}, {}
                    for nm in ("r", "k", "sig", "a", "kk", "L", "t1", "t2"):
                        Fn[nm] = bsb("F_" + nm, [128, QT])
                        Fb[nm] = Buf()
                    sq_bf = bsb("sq_bf", [128, QT], BF16)
                    sq_b = Buf()
                    TQ = []
                    for k_ in range(2):
                        TQ.append(dict(AB=bsb("AB%d" % k_, [128, NC_, 192], BF16), AK=bsb("AK%d" % k_, [128, NC_, 192], BF16),
                                       TR=bsb("TR%d" % k_, [128, NC_, 3, 128], BF16), Tinv=bsb("Tinv%d" % k_, [128, NC_, 128], BF16),
                                       cb=[Buf() for _ in range(NC_)], tb=Buf()))
                    ABT = bsb("ABT", [128, NC_, 128], BF16)
                    ABT_b = Buf()
                    An = [bsb("An%d" % q_, [128, 4, 2, 128], BF16) for q_ in range(2)]
                    An_b = [Buf(), Buf()]
                    Pn = [bsb("Pn%d" % q_, [128, 4, 128], BF16) for q_ in range(2)]
                    Pn_b = [Buf(), Buf()]
                    WT = bsb("WT", [128, 128], BF16)
                    WT_b = Buf()
                    UT = bsb("UT", [128, 128], BF16)
                    UT_b = Buf()
                    stmp = bsb("stmp", [128, 128])
                    stmp_b = Buf()
                    y_fm = bsb("y_fm", [128, QT])
                    y_fm_b = Buf()
                    yc = bsb("yc", [128, QT])
                    yc_b = Buf()
                    yb = bsb("yb", [128, QT], BF16)
                    yb_b = Buf()
                    f1 = bsb("f1", [128, QT])
                    f1_b = Buf()
                    yo = [bsb("yo%d" % q_, [128, QT], BF16) for q_ in range(2)]
                    yo_b = [Buf(), Buf()]

                    def cview(ap2):
                        return ap2.rearrange("p (c t) -> p c t", t=64)

                    units = [(hp, qt) for hp in range(dbg.get("rw_pairs", 4)) for qt in range(dbg.get("rw_quarters", NQ))]

                    def stage_P(u):
                        hp, qt = units[u]
                        pq = PQ[u % NPB]
                        pb_ = pq["b"]
                        t0 = qt * QT
                        cpc = lambda base: cp_sb[:, base + hp:base + hp + 1]
                        if qt == 0:
                            for ci in range(4):
                                load_w(wr[:, :, ci * 128:(ci + 1) * 128], w_in_cols(l, C_RW + ci * 512 + hp * 128, 128), wr_b)
                            dma("sp", wst[0:64, :], w_up_d[l][:, hp * 128:(hp + 1) * 128], writes=[wst_b])
                            dma("sp", wst[64:128, :], a_up_d[l][:, hp * 128:(hp + 1) * 128], writes=[wst_b])
                            op("pool", lambda e: e.tensor_copy(out=WU[0:64, :], in_=wst[0:64, :]), reads=[wst_b], writes=[WU_b])
                            op("pool", lambda e: e.tensor_copy(out=AU[64:128, :], in_=wst[64:128, :]), reads=[wst_b], writes=[WU_b])
                            yield
                        yield from proj_mix(wr, wr_b, 0, CP_MU + 0 + hp, 1, qt, Fn["r"][:], Fb["r"])
                        yield from proj_mix(wr, wr_b, 128, CP_MU + 4 + hp, 2, qt, Fn["k"][:], Fb["k"])
                        yield from proj_mix(wr, wr_b, 256, CP_MU + 8 + hp, 3, qt, pq["v"][:], pb_)
                        yield from proj_mix(wr, wr_b, 384, CP_MU + 12 + hp, 4, qt, Fn["t1"][:], Fb["t1"])
                        op("act", lambda e: e.activation(out=pq["g"][:], in_=Fn["t1"][:], func=AF.Silu), reads=[Fb["t1"]], writes=[pb_])
                        yield
                        op("pe", lambda e: e.matmul(ps[2][:, 0:QT], WU[:], lora_bf[:, t0:t0 + QT], start=True, stop=True),
                           reads=[WU_b, lora_b], writes=[ps_b[2]])
                        op("pe", lambda e: e.matmul(ps[2][:, QT:2 * QT], AU[:], lora_bf[:, t0:t0 + QT], start=True, stop=True),
                           reads=[WU_b, lora_b], writes=[ps_b[2]])
                        yield
                        op("act", lambda e: e.activation(out=Fn["sig"][:], in_=ps[2][:, 0:QT], func=AF.Sigmoid, bias=cpc(CP_W0)),
                           reads=[ps_b[2], cp_b], writes=[Fb["sig"]])
                        op("act", lambda e: e.activation(out=Fn["a"][:], in_=ps[2][:, QT:2 * QT], func=AF.Sigmoid, bias=cpc(CP_A0)),
                           reads=[ps_b[2], cp_b], writes=[Fb["a"]])
                        yield
                        op("dve", lambda e: e.tensor_scalar(out=Fn["kk"][:], in0=Fn["k"][:], scalar1=cpc(CP_KK), scalar2=None, op0=ALU.mult),
                           reads=[Fb["k"], cp_b], writes=[Fb["kk"]])
                        yield
                        op("act", lambda e: e.activation(out=sq_bf[:], in_=Fn["kk"][:], func=AF.Square), reads=[Fb["kk"]], writes=[sq_b])
                        yield
                        op("pe", lambda e: e.matmul(ps[2][:, 0:QT], bones[:], sq_bf[:], start=True, stop=True),
                           reads=[bones_b, sq_b], writes=[ps_b[2]])
                        yield
                        op("act", lambda e: e.activation(out=Fn["t1"][:], in_=ps[2][:, 0:QT], func=AF.Sqrt), reads=[ps_b[2]], writes=[Fb["t1"]])
                        yield
                        op("dve", lambda e: e.tensor_scalar(out=Fn["t1"][:], in0=Fn["t1"][:], scalar1=1e-12, scalar2=None, op0=ALU.max),
                           reads=[Fb["t1"]], writes=[Fb["t1"]])
                        yield
                        op("dve", lambda e: e.reciprocal(out=Fn["t1"][:], in_=Fn["t1"][:]), reads=[Fb["t1"]], writes=[Fb["t1"]])
                        yield
                        op("dve", lambda e: e.tensor_tensor(out=Fn["kk"][:], in0=Fn["kk"][:], in1=Fn["t1"][:], op=ALU.mult),
                           reads=[Fb["kk"], Fb["t1"]], writes=[Fb["kk"]])
                        yield
                        op("pool", lambda e: e.tensor_scalar(out=Fn["t1"][:], in0=Fn["a"][:], scalar1=cpc(CP_KA), scalar2=omka[:, hp:hp + 1],
                                                             op0=ALU.mult, op1=ALU.add), reads=[Fb["a"], cp_b, omka_b], writes=[Fb["t1"]])
                        yield
                        op("pool", lambda e: e.tensor_tensor(out=Fn["k"][:], in0=Fn["k"][:], in1=Fn["t1"][:], op=ALU.mult),
                           reads=[Fb["k"], Fb["t1"]], writes=[Fb["k"]])
                        yield
                        op("dve", lambda e: e.scalar_tensor_tensor(out=pq["rk"][:], in0=Fn["r"][:], scalar=cpc(CP_RK), in1=Fn["k"][:],
                                                                   op0=ALU.mult, op1=ALU.mult),
                           reads=[Fb["r"], Fb["k"], cp_b], writes=[pb_])
                        yield
                        op("dve", lambda e: e.tensor_tensor_scan(out=Fn["L"][:], data0=rmask[:], data1=Fn["sig"][:], initial=0.0,
                                                                 op0=ALU.mult, op1=ALU.add),
                           reads=[rmask_b, Fb["sig"]], writes=[Fb["L"]])
                        yield
                        op("act", lambda e: e.activation(out=Fn["t1"][:], in_=Fn["L"][:], func=AF.Exp, scale=-C0), reads=[Fb["L"]], writes=[Fb["t1"]])
                        yield
                        op("pool", lambda e: e.tensor_copy(out=pq["gC"][:], in_=cview(Fn["t1"][:])[:, :, 63]), reads=[Fb["t1"]], writes=[pb_])
                        op("dve", lambda e: e.tensor_tensor(out=pq["AR"][:, :, 128:192], in0=cview(Fn["r"][:]), in1=cview(Fn["t1"][:]), op=ALU.mult),
                           reads=[Fb["r"], Fb["t1"]], writes=[pb_])
                        yield
                        op("pool", lambda e: e.tensor_tensor(out=Fn["t2"][:], in0=Fn["L"][:], in1=Fn["sig"][:], op=ALU.subtract),
                           reads=[Fb["L"], Fb["sig"]], writes=[Fb["t2"]])
                        yield
                        op("act", lambda e: e.activation(out=Fn["t2"][:], in_=Fn["t2"][:], func=AF.Exp, scale=-C0), reads=[Fb["t2"]], writes=[Fb["t2"]])
                        yield
                        for h in range(2):
                            hs_ = slice(h * 64, (h + 1) * 64)
                            op("dve", lambda e: e.scalar_tensor_tensor(out=pq["AR"][hs_, :, h * 64:(h + 1) * 64], in0=cview(Fn["kk"][:])[hs_],
                                                                       scalar=-1.0, in1=cview(Fn["t2"][:])[hs_], op0=ALU.mult, op1=ALU.mult),
                               reads=[Fb["kk"], Fb["t2"]], writes=[pb_])
                            yield
                        op("act", lambda e: e.activation(out=Fn["t1"][:], in_=Fn["L"][:], func=AF.Exp, scale=C0), reads=[Fb["L"]], writes=[Fb["t1"]])
                        yield
                        op("pool", lambda e: e.tensor_tensor(out=Fn["t2"][:], in0=Fn["kk"][:], in1=Fn["a"][:], op=ALU.mult),
                           reads=[Fb["kk"], Fb["a"]], writes=[Fb["t2"]])
                        yield
                        for h in range(2):
                            hs_ = slice(h * 64, (h + 1) * 64)
                            op("dve", lambda e: e.tensor_tensor(out=pq["bbd"][hs_, :, h * 64:(h + 1) * 64], in0=cview(Fn["t2"][:])[hs_],
                                                                in1=cview(Fn["t1"][:])[hs_], op=ALU.mult),
                               reads=[Fb["t2"], Fb["t1"]], writes=[pb_])
                            yield
                            op("dve", lambda e: e.tensor_tensor(out=pq["kbd"][hs_, :, h * 64:(h + 1) * 64], in0=cview(Fn["k"][:])[hs_],
                                                                in1=cview(Fn["t1"][:])[hs_], op=ALU.mult),
                               reads=[Fb["k"], Fb["t1"]], writes=[pb_])
                            yield
                            op("pool", lambda e: e.tensor_copy(out=pq["vbd"][hs_, :, h * 64:(h + 1) * 64], in_=cview(pq["v"][:])[hs_]),
                               reads=[pb_], writes=[pb_])
                            yield

                    def stage_T(u):
                        pq = PQ[u % NPB]
                        pb_ = pq["b"]
                        tq = TQ[u % 2]
                        AB, AK, TR, Tinv = tq["AB"], tq["AK"], tq["TR"], tq["Tinv"]
                        cd_b = tq["cb"]
                        AR, bbd, kbd, vbd = pq["AR"], pq["bbd"], pq["kbd"], pq["vbd"]
                        for c in range(NC_):
                            bk = 3 + c % 2
                            op("pe", lambda e: e.matmul(ps[bk][:, 0:192], bbd[:, c, :], AR[:, c, :], start=True, stop=True),
                               reads=[pb_], writes=[ps_b[bk]])
                            op("pe", lambda e: e.matmul(ps[bk][:, 192:384], kbd[:, c, :], AR[:, c, :], start=True, stop=True),
                               reads=[pb_], writes=[ps_b[bk]])
                            op("pe", lambda e: e.matmul(ps[bk][:, 384:512], AR[:, c, 0:128], bbd[:, c, :], start=True, stop=True),
                               reads=[pb_], writes=[ps_b[bk]])
                            yield
                            op("dve", lambda e: e.tensor_tensor(out=AB[:, c, :], in0=ps[bk][:, 0:192], in1=m192[:], op=ALU.mult),
                               reads=[ps_b[bk], m192_b], writes=[cd_b[c]])
                            yield
                            op("dve", lambda e: e.tensor_tensor(out=AK[:, c, :], in0=ps[bk][:, 192:384], in1=m192[:], op=ALU.mult),
                               reads=[ps_b[bk], m192_b], writes=[cd_b[c]])
                            yield
                            op("dve", lambda e: e.tensor_tensor(out=ABT[:, c, :], in0=ps[bk][:, 384:512], in1=mLT[:], op=ALU.mult),
                               reads=[ps_b[bk], mLT_b], writes=[ABT_b])
                            yield
                            pb = ps[5].bitcast(BF16)
                            for k3, src_t in enumerate((vbd, bbd, kbd)):
                                op("pe", lambda e: e.transpose(pb[:, k3 * 128:(k3 + 1) * 128], src_t[:, c, :], ident[:]),
                                   reads=[pb_, ident_b], writes=[ps_b[5]])
                            yield
                            op("act", lambda e: e.activation(out=TR[:, c, :, :].rearrange("p a b -> p (a b)"), in_=pb[:, 0:384], func=AF.Copy),
                               reads=[ps_b[5]], writes=[cd_b[c]])
                            yield
                        cs_ = list(range(NC_))
                        cur_p = 0
                        for i_, c in enumerate(cs_):
                            op("dve", lambda e: e.tensor_tensor(out=Pn[0][:, i_, :], in0=AB[:, c, 0:128], in1=ident[:], op=ALU.add),
                               reads=[cd_b[c], ident_b], writes=[Pn_b[0]])
                            yield
                        getA = lambda i_, c: AB[:, c, 0:128]
                        getAT = lambda i_, c: ABT[:, c, :]
                        a_bufs = list(cd_b) + [ABT_b]
                        for lvl in range(1, 6):
                            an = An[lvl % 2]
                            an_b = An_b[lvl % 2]
                            for i_, c in enumerate(cs_):
                                bk = 3 + (i_ // 2)
                                off = (i_ % 2) * 256
                                if lvl < 5:
                                    op("pe", lambda e: e.matmul(ps[bk][:, off:off + 128], getAT(i_, c), getA(i_, c), start=True, stop=True),
                                       reads=a_bufs, writes=[ps_b[bk]])
                                op("pe", lambda e: e.matmul(ps[bk][:, off + 128:off + 256], getA(i_, c), getAT(i_, c), start=True, stop=True),
                                   reads=a_bufs, writes=[ps_b[bk]])
                                yield
                            for half in range(2):
                                bk = 3 + half
                                if lvl < 5:
                                    op("act", lambda e: e.activation(out=an[:, half * 2:half * 2 + 2, :, :].rearrange("p a b c -> p (a b c)"),
                                                                     in_=ps[bk][:, :], func=AF.Copy), reads=[ps_b[bk]], writes=[an_b])
                                else:
                                    op("act", lambda e: e.activation(out=an[:, half * 2:half * 2 + 2, 1, :],
                                                                     in_=ps[bk][:, :].rearrange("p (a b c) -> p a b c", a=2, b=2)[:, :, 1, :],
                                                                     func=AF.Copy), reads=[ps_b[bk]], writes=[an_b])
                                yield
                            getA = lambda i_, c, an=an: an[:, i_, 0, :]
                            getAT = lambda i_, c, an=an: an[:, i_, 1, :]
                            a_bufs = [an_b]
                            pcur, pnew = Pn[cur_p], Pn[1 - cur_p]
                            pcur_b, pnew_b = Pn_b[cur_p], Pn_b[1 - cur_p]
                            for i_, c in enumerate(cs_):
                                op("pe", lambda e: e.matmul(ps[5][:, i_ * 128:(i_ + 1) * 128], getAT(i_, c), pcur[:, i_, :], start=True, stop=True),
                                   reads=[an_b, pcur_b], writes=[ps_b[5]])
                                if i_ % 2 == 1:
                                    yield
                            if lvl < 5:
                                op("dve", lambda e: e.tensor_tensor(out=pnew[:].rearrange("p a b -> p (a b)"), in0=ps[5][:, :],
                                                                    in1=pcur[:].rearrange("p a b -> p (a b)"), op=ALU.add),
                                   reads=[ps_b[5], pcur_b], writes=[pnew_b])
                            else:
                                op("dve", lambda e: e.tensor_tensor(out=Tinv[:].rearrange("p a b -> p (a b)"),
                                                                    in0=ps[5][:, :], in1=pcur[:].rearrange("p a b -> p (a b)"), op=ALU.add),
                                   reads=[ps_b[5], pcur_b], writes=[tq["tb"]])
                            yield
                            cur_p = 1 - cur_p

                    def stage_Q(u):
                        hp, qt = units[u]
                        pq = PQ[u % NPB]
                        pb_ = pq["b"]
                        tq = TQ[u % 2]
                        AB, AK, TR, Tinv = tq["AB"], tq["AK"], tq["TR"], tq["Tinv"]
                        cd_b = tq["cb"]
                        AR, gC = pq["AR"], pq["gC"]
                        t0 = qt * QT
                        cpc = lambda base: cp_sb[:, base + hp:base + hp + 1]
                        if qt == 0:
                            op("dve", lambda e: e.memset(S_st[:], 0.0), writes=[S_st_b])
                            op("dve", lambda e: e.memset(S_bf[:], 0.0), writes=[S_bf_b])
                            yield
                        for c in range(NC_):
                            op("pe", lambda e: e.matmul(ps[6][:, 0:128], AR[:, c, 0:128], S_bf[:], start=True, stop=False),
                               reads=[pb_, S_bf_b], writes=[ps_b[6]])
                            op("pe", lambda e: e.matmul(ps[6][:, 0:128], AK[:, c, 0:128], TR[:, c, 0, :], start=False, stop=True),
                               reads=[cd_b[c]], writes=[ps_b[6]])
                            yield
                            op("act", lambda e: e.activation(out=WT[:], in_=ps[6][:, 0:128], func=AF.Copy), reads=[ps_b[6]], writes=[WT_b])
                            yield
                            op("pe", lambda e: e.matmul(ps[6][:, 128:256], Tinv[:, c, :], WT[:], start=True, stop=True),
                               reads=[tq["tb"], WT_b], writes=[ps_b[6]])
                            yield
                            op("act", lambda e: e.activation(out=UT[:], in_=ps[6][:, 128:256], func=AF.Copy), reads=[ps_b[6]], writes=[UT_b])
                            yield
                            op("pe", lambda e: e.matmul(ps[6][:, 256:384], TR[:, c, 1, :], UT[:], start=True, stop=False),
                               reads=[cd_b[c], UT_b], writes=[ps_b[6]])
                            op("pe", lambda e: e.matmul(ps[6][:, 256:384], TR[:, c, 2, :], TR[:, c, 0, :], start=False, stop=True),
                               reads=[cd_b[c]], writes=[ps_b[6]])
                            yield
                            yo_ = ps[7][:, c * 64:(c + 1) * 64]
                            op("pe", lambda e: e.matmul(yo_, S_bf[:], AR[:, c, 128:192], start=True, stop=False),
                               reads=[S_bf_b, pb_], writes=[ps_b[7]])
                            op("pe", lambda e: e.matmul(yo_, UT[:], AB[:, c, 128:192], start=False, stop=False),
                               reads=[UT_b, cd_b[c]], writes=[ps_b[7]])
                            op("pe", lambda e: e.matmul(yo_, TR[:, c, 0, :], AK[:, c, 128:192], start=False, stop=True),
                               reads=[cd_b[c]], writes=[ps_b[7]])
                            yield
                            op("dve", lambda e: e.tensor_tensor(out=stmp[:], in0=ps[6][:, 256:384], in1=S_st[:], op=ALU.add),
                               reads=[ps_b[6], S_st_b], writes=[stmp_b])
                            yield
                            op("act", lambda e: e.activation(out=S_bf[:], in_=stmp[:], func=AF.Copy, scale=gC[:, c:c + 1]),
                               reads=[stmp_b, pb_], writes=[S_bf_b])
                            op("dve", lambda e: e.tensor_scalar(out=S_st[:], in0=stmp[:], scalar1=gC[:, c:c + 1], scalar2=None, op0=ALU.mult),
                               reads=[stmp_b, pb_], writes=[S_st_b])
                            yield
                        Y = ps[7][:, 0:QT]
                        M = ps[7][:, QT:2 * QT]
                        op("act", lambda e: e.activation(out=y_fm[:], in_=Y, func=AF.Copy), reads=[ps_b[7]], writes=[y_fm_b])
                        yield
                        op("act", lambda e: e.activation(out=yb[:], in_=y_fm[:], func=AF.Copy), reads=[y_fm_b], writes=[yb_b])
                        yield
                        op("pe", lambda e: e.matmul(M, bones[:], yb[:], start=True, stop=True), reads=[bones_b, yb_b], writes=[ps_b[7]])
                        yield
                        op("dve", lambda e: e.scalar_tensor_tensor(out=yc[:], in0=M, scalar=-1.0 / 64, in1=y_fm[:],
                                                                   op0=ALU.mult, op1=ALU.add), reads=[ps_b[7], y_fm_b], writes=[yc_b])
                        yield
                        op("act", lambda e: e.activation(out=yb[:], in_=yc[:], func=AF.Square), reads=[yc_b], writes=[yb_b])
                        yield
                        op("pe", lambda e: e.matmul(M, bones[:], yb[:], start=True, stop=True), reads=[bones_b, yb_b], writes=[ps_b[7]])
                        yield
                        op("act", lambda e: e.activation(out=f1[:], in_=M, func=AF.Sqrt, scale=1.0 / 64, bias=GN_EPS),
                           reads=[ps_b[7]], writes=[f1_b])
                        yield
                        op("pe", lambda e: e.matmul(M, bones[:], pq["rk"][:], start=True, stop=True), reads=[bones_b, pb_], writes=[ps_b[7]])
                        op("dve", lambda e: e.reciprocal(out=f1[:], in_=f1[:]), reads=[f1_b], writes=[f1_b])
                        yield
                        op("dve", lambda e: e.tensor_tensor(out=yc[:], in0=yc[:], in1=f1[:], op=ALU.mult), reads=[yc_b, f1_b], writes=[yc_b])
                        yield
                        op("dve", lambda e: e.tensor_scalar(out=yc[:], in0=yc[:], scalar1=cpc(CP_LG), scalar2=cpc(CP_LB), op0=ALU.mult, op1=ALU.add),
                           reads=[yc_b, cp_b], writes=[yc_b])
                        yield
                        op("dve", lambda e: e.tensor_tensor(out=f1[:], in0=M, in1=pq["v"][:], op=ALU.mult), reads=[ps_b[7], pb_], writes=[f1_b])
                        yield
                        op("pool", lambda e: e.tensor_tensor(out=yc[:], in0=yc[:], in1=f1[:], op=ALU.add), reads=[yc_b, f1_b], writes=[yc_b])
                        yield
                        q_ = u % 2
                        op("pool", lambda e: e.tensor_tensor(out=yo[q_][:], in0=yc[:], in1=pq["g"][:], op=ALU.mult), reads=[yc_b, pb_], writes=[yo_b[q_]])
                        yield
                        dma("sp", yrw_d[hp * 128:(hp + 1) * 128, t0:t0 + QT], yo[q_][:], reads=[yo_b[q_]], writes=[yrw_db[qt]])
                        yield

                    nu = len(units)
                    for tick in range(nu + 2):
                        gens = []
                        if tick - 2 >= 0:
                            gens.append(stage_Q(tick - 2))
                        if 0 <= tick - 1 < nu:
                            gens.append(stage_T(tick - 1))
                        if tick < nu:
                            gens.append(stage_P(tick))
                        while gens:
                            for g in list(gens):
                                try:
                                    next(g)
                                except StopIteration:
                                    gens.remove(g)
                    kb.barrier()
                    pes.close()
                    bsb = bsb_outer
                    rwl = [bsb("rwl%d" % q_, [128, 4, 128], BF16) for q_ in range(2)]
                    rwl_b = [Buf(), Buf()]

                    def rw_lhsT(i):
                        q_ = i % 2
                        dma("sp", rwl[q_][:], yrw_d.rearrange("(hp p) s -> p hp s", p=128)[:, :, i * 128:(i + 1) * 128],
                            reads=[yrw_db[(i * 128) // QT]], writes=[rwl_b[q_]])
                        return rwl[q_], [rwl_b[q_]]

                    epilogue("rw", 4, w_out_rw_d[l], C_GATE + 2 * D, rw_lhsT)
        kb.barrier()
        with ExitStack() as fes:
            def fsb(name, shape, dt=F32):
                return fes.enter_context(nc.sbuf_tensor("fin_%s_l%d" % (name, l), shape, dt))
            wo = fsb("wo", [128, 8, D], BF16)
            wo_b = Buf()
            for n in range(2):
                load_w(wo[:, :, n * 512:(n + 1) * 512], w_o_d[l].rearrange("(kc p) n -> p kc n", p=128)[:, :, n * 512:(n + 1) * 512], wo_b, fast=True)
            bl = [b for b in ("sb", "ssd", "rw") if b in branches]
            tin = {(b, q): fsb("tin_%s%d" % (b, q), [128, D]) for b in bl for q in range(2)}
            tin_b = {(b, q): Buf() for b in bl for q in range(2)}
            msum = [fsb("msum%d" % q, [128, D]) for q in range(2)]
            msum_b = [Buf(), Buf()]
            mbf = [fsb("mbf%d" % q, [128, D], BF16) for q in range(2)]
            mbf_b = [Buf(), Buf()]
            mT = [fsb("mT%d" % q, [128, 8, 128], BF16) for q in range(2)]
            mT_b = [Buf(), Buf()]
            for i in range(dbg.get("ep_tiles", NT)):
                q = i % 2
                for b in bl:
                    dma("sp", tin[(b, q)][:], T_d[b][i * 128:(i + 1) * 128, :], reads=[T_buf[b][i]], writes=[tin_b[(b, q)]])
                if len(bl) == 1:
                    op("pool", lambda e: e.tensor_copy(out=mbf[q][:], in_=tin[(bl[0], q)][:]), reads=[tin_b[(bl[0], q)]], writes=[mbf_b[q]])
                else:
                    op("pool", lambda e: e.tensor_tensor(out=msum[q][:], in0=tin[(bl[0], q)][:], in1=tin[(bl[1], q)][:], op=ALU.add),
                       reads=[tin_b[(bl[0], q)], tin_b[(bl[1], q)]], writes=[msum_b[q]])
                    if len(bl) == 3:
                        op("pool", lambda e: e.tensor_tensor(out=mbf[q][:], in0=msum[q][:], in1=tin[(bl[2], q)][:], op=ALU.add),
                           reads=[msum_b[q], tin_b[(bl[2], q)]], writes=[mbf_b[q]])
                    else:
                        op("pool", lambda e: e.tensor_copy(out=mbf[q][:], in_=msum[q][:]), reads=[msum_b[q]], writes=[mbf_b[q]])
                pb = ps[q].bitcast(BF16)
                for kc in range(8):
                    op("pe", lambda e: e.transpose(pb[:, kc * 128:(kc + 1) * 128], mbf[q][:, kc * 128:(kc + 1) * 128], ident[:]),
                       reads=[mbf_b[q], ident_b], writes=[ps_b[q]])
                op("act", lambda e: e.activation(out=mT[q][:].rearrange("p a b -> p (a b)"), in_=pb[:, :], func=AF.Identity),
                   reads=[ps_b[q]], writes=[mT_b[q]])
                for n in range(2):
                    bk = 2 + 2 * q + n
                    for kc in range(8):
                        op("pe", lambda e: e.matmul(ps[bk][:, :], mT[q][:, kc, :], wo[:, kc, n * 512:(n + 1) * 512],
                                                    start=(kc == 0), stop=(kc == 7)),
                           reads=[mT_b[q], wo_b], writes=[ps_b[bk]])
                    op("dve", lambda e: e.tensor_tensor(out=x_sb[:, i, n * 512:(n + 1) * 512], in0=ps[bk][:, :],
                                                        in1=x_sb[:, i, n * 512:(n + 1) * 512], op=ALU.add),
                       reads=[ps_b[bk], x_b[i]], writes=[x_b[i]])
        kb.barrier()

    with ExitStack() as oes:
        fg = oes.enter_context(nc.sbuf_tensor("fg_sb", [128, D], F32))
        fg_b = Buf()
        dma("sp", fg[:], fg_d[:, :], writes=[fg_b])
        junk = oes.enter_context(nc.sbuf_tensor("ojunk", [128, D], BF16))
        junk_b = Buf()
        yo = [oes.enter_context(nc.sbuf_tensor("yo%d" % q, [128, D], F32)) for q in range(2)]
        yo_b = [Buf(), Buf()]
        ssq = [oes.enter_context(nc.sbuf_tensor("oss%d" % q, [128, 4], F32)) for q in range(2)]
        ssq_b = [Buf(), Buf()]
        out_toks = []
        for i in range(NT):
            q = i % 2
            rms_rstd(x_sb[:, i, :], x_b[i], junk[:], junk_b, ssq[q], ssq_b[q], D, RMS_EPS)
            op("dve", lambda e: e.scalar_tensor_tensor(out=yo[q][:], in0=x_sb[:, i, :], scalar=ssq[q][:, 2:3], in1=fg[:],
                                                       op0=ALU.mult, op1=ALU.mult),
               reads=[x_b[i], ssq_b[q], fg_b], writes=[yo_b[q]])
            ob = Buf()
            dma("sp", y_d[i * 128:(i + 1) * 128, :], yo[q][:], reads=[yo_b[q]], writes=[ob])
        kb.barrier(engines=("sp",))
    kb.barrier()
    es.close()
    return nc, kb


def _prep_inputs(inp):
    f = lambda a: np.ascontiguousarray(np.asarray(a, dtype=np.float32))
    cp = np.zeros((L, 128, NCP), np.float32)
    rp = np.zeros((L, 128, NRP), np.float32)
    for l in range(L):
        cw = f(inp["conv_w"])[l]
        cp[l, :, CP_CW:CP_CW + 40] = cw.reshape(4, 10, 128).transpose(2, 1, 0).reshape(128, 40)
        cp[l, :, CP_CB:CP_CB + 10] = f(inp["conv_b"])[l].reshape(10, 128).T
        cp[l, :, CP_MU:CP_MU + 17] = f(inp["rw_mu"])[l].reshape(17, 128).T
        for off, nm in ((CP_W0, "rw_w0"), (CP_A0, "rw_a0"), (CP_KK, "rw_k_k"), (CP_KA, "rw_k_a"),
                        (CP_RK, "rw_r_k"), (CP_LG, "rw_ln_g"), (CP_LB, "rw_ln_b")):
            cp[l, :, off:off + 4] = f(inp[nm])[l].reshape(4, 128).T
        rp[l, :, RP_NG:RP_NG + D] = f(inp["norm_g"])[l][None, :]
        rp[l, :, RP_SG:RP_SG + D] = f(inp["ssd_norm_g"])[l][None, :]
        rp[l, :, RP_DTB:RP_DTB + 16] = f(inp["dt_bias"])[l][None, :]
        rp[l, :, RP_ALOG:RP_ALOG + 16] = f(inp["a_log"])[l][None, :]
        rp[l, :, RP_DSK:RP_DSK + 16] = f(inp["d_skip"])[l][None, :]
    fg = np.ascontiguousarray(np.broadcast_to(f(inp["final_g"])[None, :], (128, D)))
    shared = {
        "w_in": f(inp["w_in"]), "w_out_sb": f(inp["w_out_sb"]), "w_out_ssd": f(inp["w_out_ssd"]),
        "w_out_rw": f(inp["w_out_rw"]), "w_o": f(inp["w_o"]), "rw_w_up": f(inp["rw_w_up"]),
        "rw_a_up": f(inp["rw_a_up"]), "cp": cp, "rp": rp, "fg": fg,
        "w_dt": np.ascontiguousarray(f(inp["w_in"])[:, :, C_DT:C_DT + 16].reshape(L, 8, 128, 16).transpose(0, 2, 1, 3)),
    }
    x = f(inp["x"])
    return [dict(shared, x=x[b]) for b in range(x.shape[0])]


def kernel(**inputs):
    in_maps = _prep_inputs(inputs)
    nc, _ = build()
    res = run_bass_kernel_spmd(nc, in_maps, core_ids=list(range(8)))
    return np.stack([r["y"] for r in res.results], axis=0).astype(np.float32)
```
